# Optimizing a Trainium2 kernel written in Bass

```python
import jax
import jax.numpy as jnp
from jax import lax
import numpy as np

D_MODEL = 1024
BATCH = 8
SEQ = 4096
DEPTH = 2

CTX_LEN = 256
GRID_W = 64
HEAD_DIM = D_MODEL // 16
N_FOURIER_GROUPS = 4
N_NA_HEADS = 4
N_GLA_HEADS = 4
N_GQA_HEADS = 4
N_GQA_KV_HEADS = 2
GQA_GROUP = N_GQA_HEADS // N_GQA_KV_HEADS
GLA_DK = HEAD_DIM // 2
GLA_DV = HEAD_DIM
GLA_GATE_RANK = 16
GLA_TAU = 16.0
GLA_CHUNK = 64
NA_ROWS = 8
NA_COLS = 16
Q_BLOCK = 128
D_FF = 256 * ((8 * D_MODEL // 3 + 255) // 256)
ROPE_THETA = 10000.0
EPS = 1e-6
N_MOD = 9

W_FOURIER = N_FOURIER_GROUPS * HEAD_DIM
W_NA = N_NA_HEADS * HEAD_DIM
W_GLA = N_GLA_HEADS * GLA_DV
W_GQA = N_GQA_HEADS * HEAD_DIM
D_MIX = W_FOURIER + W_NA + W_GLA + W_GQA

IN_LAYOUT = (
    ('fourier', W_FOURIER),
    ('na_q', W_NA), ('na_k', W_NA), ('na_v', W_NA),
    ('gla_q', N_GLA_HEADS * GLA_DK), ('gla_k', N_GLA_HEADS * GLA_DK),
    ('gla_v', W_GLA), ('gla_r', W_GLA),
    ('gla_gf', GLA_GATE_RANK), ('gla_gb', GLA_GATE_RANK),
    ('gqa_q', W_GQA), ('gqa_k', N_GQA_KV_HEADS * HEAD_DIM), ('gqa_v', N_GQA_KV_HEADS * HEAD_DIM),
)
D_IN = sum(w for _, w in IN_LAYOUT)

kernel_name = 'hybrid_parallel_group_diffusion_block'


def rms_norm(x, gain=None):
    xf = x.astype(jnp.float32)
    y = xf * lax.rsqrt(jnp.mean(xf * xf, axis=-1, keepdims=True) + EPS)
    if gain is not None:
        y = y * gain.astype(jnp.float32)
    return y.astype(x.dtype)


def modulate(h, shift, scale):
    return h * (1.0 + scale) + shift


def swiglu(h, w1, w3, w2):
    return (jax.nn.silu(h @ w1) * (h @ w3)) @ w2


def heads(t, n, d):
    return t.reshape(t.shape[:-1] + (n, d))


def split_in(h):
    offsets = np.cumsum([w for _, w in IN_LAYOUT])[:-1].tolist()
    parts = jnp.split(h, offsets, axis=-1)
    return {name: part for (name, _), part in zip(IN_LAYOUT, parts)}


def axial_rope(n_tokens):
    t = jnp.arange(n_tokens, dtype=jnp.int32)
    row = (t // GRID_W).astype(jnp.float32)
    col = (t % GRID_W).astype(jnp.float32)
    n_freq = HEAD_DIM // 4
    inv_freq = ROPE_THETA ** (-jnp.arange(n_freq, dtype=jnp.float32) / n_freq)
    ang = jnp.concatenate([row[:, None] * inv_freq, col[:, None] * inv_freq], axis=-1)
    return jnp.cos(ang)[:, None, :], jnp.sin(ang)[:, None, :]


def apply_rope(x, cos, sin):
    xf = x.astype(jnp.float32)
    x1, x2 = jnp.split(xf, 2, axis=-1)
    return jnp.concatenate([x1 * cos - x2 * sin, x1 * sin + x2 * cos], axis=-1).astype(x.dtype)


def fourier_mix(u):
    b_, l_, _ = u.shape
    uf = u.astype(jnp.float32).reshape(b_, l_, N_FOURIER_GROUPS, HEAD_DIM)
    y = jnp.fft.fft2(uf, axes=(1, 3), norm='ortho').real
    return y.reshape(b_, l_, W_FOURIER).astype(u.dtype)


def block_attention(q, k, v):
    b_, lq, hk, g, dh = q.shape
    nb = lq // Q_BLOCK
    qb = (q * dh ** -0.5).reshape(b_, nb, Q_BLOCK, hk, g, dh).swapaxes(0, 1)

    def one_block(q_blk):
        s = jnp.einsum('bqhgd,bkhd->bhgqk', q_blk, k).astype(jnp.float32)
        p = jax.nn.softmax(s, axis=-1).astype(v.dtype)
        return jnp.einsum('bhgqk,bkhd->bqhgd', p, v)

    o = lax.map(one_block, qb)
    return o.swapaxes(0, 1).reshape(b_, lq, hk * g * dh)


def neighborhood_attention(q, k, v, k_ctx, v_ctx, rpb):
    b_, s_, h_, dh = q.shape
    rows = s_ // GRID_W
    kr = min(NA_ROWS, rows)
    kg = k.reshape(b_, rows, GRID_W, h_, dh)
    vg = v.reshape(b_, rows, GRID_W, h_, dh)
    qg = q.reshape(b_, rows, GRID_W, h_, dh).transpose(1, 0, 2, 3, 4)
    r_pos = jnp.arange(rows)
    c_pos = jnp.arange(GRID_W)
    row_start = jnp.clip(r_pos - kr // 2, 0, rows - kr)
    rel_row = row_start[:, None] + jnp.arange(kr)[None, :] - r_pos[:, None]
    col_start = jnp.clip(c_pos - NA_COLS // 2, 0, GRID_W - NA_COLS)
    col_idx = col_start[:, None] + jnp.arange(NA_COLS)[None, :]
    rel_col = col_idx - c_pos[:, None]
    bias_col = rpb.astype(jnp.float32)[:, :, rel_col + NA_COLS - 1]
    scale = dh ** -0.5
    n_win = kr * NA_COLS

    def row_block(args):
        q_r, rs, rr = args
        k_blk = lax.dynamic_slice_in_dim(kg, rs, kr, axis=1)[:, :, col_idx]
        v_blk = lax.dynamic_slice_in_dim(vg, rs, kr, axis=1)[:, :, col_idx]
        bias = bias_col[:, rr + NA_ROWS - 1].transpose(0, 2, 1, 3)
        s_win = jnp.einsum('bchd,brckhd->bhcrk', q_r, k_blk).astype(jnp.float32) * scale + bias[None]
        s_ctx = jnp.einsum('bchd,bjhd->bhcj', q_r, k_ctx).astype(jnp.float32) * scale
        s = jnp.concatenate([s_win.reshape(b_, h_, GRID_W, n_win), s_ctx], axis=-1)
        p = jax.nn.softmax(s, axis=-1).astype(v.dtype)
        p_win = p[..., :n_win].reshape(b_, h_, GRID_W, kr, NA_COLS)
        p_ctx = p[..., n_win:]
        return (jnp.einsum('bhcrk,brckhd->bchd', p_win, v_blk)
                + jnp.einsum('bhcj,bjhd->bchd', p_ctx, v_ctx))

    out = lax.map(row_block, (qg, row_start, rel_row))
    return out.transpose(1, 0, 2, 3, 4).reshape(b_, s_, h_ * dh)


def gla_scan(q, k, v, log_a, s0):
    b_, h_, l_, _ = q.shape
    dv = v.shape[-1]
    n_chunks = l_ // GLA_CHUNK

    def chunks(t):
        return t.reshape(b_, h_, n_chunks, GLA_CHUNK, t.shape[-1]).transpose(2, 0, 1, 3, 4)

    lower_tri = jnp.tril(jnp.ones((GLA_CHUNK, GLA_CHUNK), dtype=bool))[None, None, :, :, None]

    def step(s, inp):
        qc, kc, vc, ac = inp
        qf, kf, vf = (t.astype(jnp.float32) for t in (qc, kc, vc))
        b = jnp.cumsum(ac.astype(jnp.float32), axis=2)
        rel = b[:, :, :, None, :] - b[:, :, None, :, :]
        decay = jnp.where(lower_tri, jnp.exp(jnp.minimum(rel, 0.0)), 0.0)
        scores = jnp.einsum('bhid,bhjd,bhijd->bhij', qf, kf, decay)
        o = (jnp.einsum('bhcd,bhde->bhce', qf * jnp.exp(b), s)
             + jnp.einsum('bhij,bhje->bhie', scores, vf))
        b_last = b[:, :, -1:, :]
        s_new = (jnp.exp(b_last[:, :, 0, :])[..., None] * s
                 + jnp.einsum('bhcd,bhce->bhde', kf * jnp.exp(b_last - b), vf))
        return s_new, o.astype(v.dtype)

    s_final, o = lax.scan(step, s0, (chunks(q), chunks(k), chunks(v), chunks(log_a)))
    return o.transpose(1, 2, 0, 3, 4).reshape(b_, h_, l_, dv), s_final


def bidir_gla(q, k, v, a_f, a_b, s0_f, s0_b):
    flip = lambda t: jnp.flip(t, axis=2)
    o_f, s_f = gla_scan(q, k, v, a_f, s0_f)
    o_b, s_b = gla_scan(flip(q), flip(k), flip(v), flip(a_b), s0_b)
    return o_f + flip(o_b), s_f, s_b


def gla_prepare(p, w_gf, b_gf, w_gb, b_gb):
    to_bhld = lambda t, d: heads(t, N_GLA_HEADS, d).transpose(0, 2, 1, 3)
    q = to_bhld(p['gla_q'] * GLA_DK ** -0.5, GLA_DK)
    k = to_bhld(p['gla_k'], GLA_DK)
    v = to_bhld(p['gla_v'], GLA_DV)
    a_f = to_bhld(jax.nn.log_sigmoid((p['gla_gf'] @ w_gf + b_gf).astype(jnp.float32)) / GLA_TAU, GLA_DK)
    a_b = to_bhld(jax.nn.log_sigmoid((p['gla_gb'] @ w_gb + b_gb).astype(jnp.float32)) / GLA_TAU, GLA_DK)
    return q, k, v, a_f, a_b


def gla_output(o, r, gain):
    b_, _, l_, _ = o.shape
    o = rms_norm(o.transpose(0, 2, 1, 3), gain).reshape(b_, l_, W_GLA)
    return o * jax.nn.silu(r)


def token_mixing(hx, hz, cos, sin, na_q_norm, na_k_norm, na_rpb, gla_w_gate_f, gla_b_gate_f,
                 gla_w_gate_b, gla_b_gate_b, gla_norm, gqa_q_norm, gqa_k_norm, with_ctx_out):
    px, pz = split_in(hx), split_in(hz)
    b_, s_ = hx.shape[0], hx.shape[1]
    lc = hz.shape[1]
    a_x = fourier_mix(px['fourier'])
    na_q = lambda p: rms_norm(heads(p['na_q'], N_NA_HEADS, HEAD_DIM), na_q_norm)
    na_k = lambda p: rms_norm(heads(p['na_k'], N_NA_HEADS, HEAD_DIM), na_k_norm)
    na_v = lambda p: heads(p['na_v'], N_NA_HEADS, HEAD_DIM)
    kz_na, vz_na = na_k(pz), na_v(pz)
    b_x = neighborhood_attention(na_q(px), na_k(px), na_v(px), kz_na, vz_na, na_rpb)
    gates = (gla_w_gate_f, gla_b_gate_f, gla_w_gate_b, gla_b_gate_b)
    s0 = jnp.zeros((b_, N_GLA_HEADS, GLA_DK, GLA_DV), jnp.float32)
    oz_gla, s_f, s_b = bidir_gla(*gla_prepare(pz, *gates), s0, s0)
    ox_gla, _, _ = bidir_gla(*gla_prepare(px, *gates), s_f, s_b)
    c_x = gla_output(ox_gla, px['gla_r'], gla_norm)
    gqa_q = lambda p: rms_norm(heads(p['gqa_q'], N_GQA_HEADS, HEAD_DIM), gqa_q_norm)
    gqa_k = lambda p: rms_norm(heads(p['gqa_k'], N_GQA_KV_HEADS, HEAD_DIM), gqa_k_norm)
    gqa_v = lambda p: heads(p['gqa_v'], N_GQA_KV_HEADS, HEAD_DIM)
    kz_gqa, vz_gqa = gqa_k(pz), gqa_v(pz)
    qx_gqa = apply_rope(gqa_q(px), cos, sin).reshape(b_, s_, N_GQA_KV_HEADS, GQA_GROUP, HEAD_DIM)
    kx_gqa = apply_rope(gqa_k(px), cos, sin)
    d_x = block_attention(qx_gqa, jnp.concatenate([kx_gqa, kz_gqa], axis=1),
                          jnp.concatenate([gqa_v(px), vz_gqa], axis=1))
    out_x = jnp.concatenate([a_x, b_x, c_x, d_x], axis=-1)
    if not with_ctx_out:
        return out_x, None
    a_z = fourier_mix(pz['fourier'])
    b_z = block_attention(na_q(pz)[:, :, :, None, :], kz_na, vz_na)
    c_z = gla_output(oz_gla, pz['gla_r'], gla_norm)
    d_z = block_attention(gqa_q(pz).reshape(b_, lc, N_GQA_KV_HEADS, GQA_GROUP, HEAD_DIM), kz_gqa, vz_gqa)
    return out_x, jnp.concatenate([a_z, b_z, c_z, d_z], axis=-1)


def setup_inputs(seed: int = 0) -> dict:
    key = jax.random.key(seed)
    ks = iter(jax.random.split(key, 32))
    f32 = jnp.float32

    def nrm(shape, std):
        return jax.random.normal(next(ks), shape, f32) * std

    L, D = DEPTH, D_MODEL
    return {
        'x': nrm((BATCH, SEQ, D), 1.0),
        'c': nrm((BATCH, D), 1.0),
        'ctx': nrm((BATCH, CTX_LEN, D), 1.0),
        'c_ctx': nrm((D,), 1.0),
        'w_mod': nrm((L, D, N_MOD * D), 0.5 * D ** -0.5),
        'b_mod': nrm((L, N_MOD * D), 0.02),
        'ffn1_w1': nrm((L, D, D_FF), D ** -0.5),
        'ffn1_w3': nrm((L, D, D_FF), D ** -0.5),
        'ffn1_w2': nrm((L, D_FF, D), D_FF ** -0.5),
        'w_in': nrm((L, D, D_IN), D ** -0.5),
        'na_q_norm': 1.0 + nrm((L, HEAD_DIM), 0.1),
        'na_k_norm': 1.0 + nrm((L, HEAD_DIM), 0.1),
        'na_rpb': nrm((L, N_NA_HEADS, 2 * NA_ROWS - 1, 2 * NA_COLS - 1), 0.1),
        'gla_w_gate_f': nrm((L, GLA_GATE_RANK, N_GLA_HEADS * GLA_DK), GLA_GATE_RANK ** -0.5),
        'gla_b_gate_f': nrm((L, N_GLA_HEADS * GLA_DK), 0.1),
        'gla_w_gate_b': nrm((L, GLA_GATE_RANK, N_GLA_HEADS * GLA_DK), GLA_GATE_RANK ** -0.5),
        'gla_b_gate_b': nrm((L, N_GLA_HEADS * GLA_DK), 0.1),
        'gla_norm': 1.0 + nrm((L, GLA_DV), 0.1),
        'gqa_q_norm': 1.0 + nrm((L, HEAD_DIM), 0.1),
        'gqa_k_norm': 1.0 + nrm((L, HEAD_DIM), 0.1),
        'w_out': nrm((L, D_MIX, D), D_MIX ** -0.5),
        'ffn2_w1': nrm((L, D, D_FF), D ** -0.5),
        'ffn2_w3': nrm((L, D, D_FF), D ** -0.5),
        'ffn2_w2': nrm((L, D_FF, D), D_FF ** -0.5),
    }


def reference(x, c, ctx, c_ctx, w_mod, b_mod, ffn1_w1, ffn1_w3, ffn1_w2, w_in,
              na_q_norm, na_k_norm, na_rpb, gla_w_gate_f, gla_b_gate_f, gla_w_gate_b, gla_b_gate_b,
              gla_norm, gqa_q_norm, gqa_k_norm, w_out, ffn2_w1, ffn2_w3, ffn2_w2):
    cos, sin = axial_rope(x.shape[1])
    z = ctx
    for l in range(DEPTH):
        last = l == DEPTH - 1
        mx = jnp.split((jax.nn.silu(c) @ w_mod[l] + b_mod[l])[:, None, :], N_MOD, axis=-1)
        mz = jnp.split(jax.nn.silu(c_ctx) @ w_mod[l] + b_mod[l], N_MOD, axis=-1)
        x = x + 0.5 * mx[2] * swiglu(modulate(rms_norm(x), mx[0], mx[1]), ffn1_w1[l], ffn1_w3[l], ffn1_w2[l])
        z = z + 0.5 * mz[2] * swiglu(modulate(rms_norm(z), mz[0], mz[1]), ffn1_w1[l], ffn1_w3[l], ffn1_w2[l])
        hx = modulate(rms_norm(x), mx[3], mx[4]) @ w_in[l]
        hz = modulate(rms_norm(z), mz[3], mz[4]) @ w_in[l]
        mix_x, mix_z = token_mixing(hx, hz, cos, sin, na_q_norm[l], na_k_norm[l], na_rpb[l],
                                    gla_w_gate_f[l], gla_b_gate_f[l], gla_w_gate_b[l], gla_b_gate_b[l],
                                    gla_norm[l], gqa_q_norm[l], gqa_k_norm[l], not last)
        x = x + mx[5] * (mix_x @ w_out[l])
        x = x + 0.5 * mx[8] * swiglu(modulate(rms_norm(x), mx[6], mx[7]), ffn2_w1[l], ffn2_w3[l], ffn2_w2[l])
        if not last:
            z = z + mz[5] * (mix_z @ w_out[l])
            z = z + 0.5 * mz[8] * swiglu(modulate(rms_norm(z), mz[6], mz[7]), ffn2_w1[l], ffn2_w3[l], ffn2_w2[l])
    return x
```

```python
import contextlib
import numpy as np
import concourse.bass as bass
import concourse.mybir as mybir
from concourse.bass_utils import run_bass_kernel_spmd

F32 = mybir.dt.float32
BF16 = mybir.dt.bfloat16
AF = mybir.ActivationFunctionType
ALU = mybir.AluOpType
AX = mybir.AxisListType

D = 1024
SEQ = 4096
LC = 256
T = SEQ + LC
DFF = 2816
NJ = DFF // 128
DEPTH = 2
EPS = 1e-6
NDS = 40
GLA_STOP = None
OPT_T4 = True
OPT_PHT = True
OPT_FQ = True
OPT_PREFETCH = False

O_F, O_NQ, O_NK, O_NV, O_GQ, O_GK, O_GV, O_GR, O_GF, O_GB, O_AQ, O_AK, O_AV = (
    0, 256, 512, 768, 1024, 1152, 1280, 1536, 1792, 1808, 1824, 2080, 2208)
FM_GROUPS = [("F", 0), ("F", 1), ("NQ", 0), ("NQ", 1), ("NK", 0), ("NK", 1), ("AQ", 0), ("AQ", 1),
             ("AK", 0), ("AK", 1), ("GQ", 0), ("GK", 0), ("GG", 0)]
NFM = 12 * 128 + 32
NTM = 1024


def _colperm():
    r = lambda a, n: list(range(a, a + n))
    cols = []
    cols += r(O_F, 256) + r(O_NQ, 256) + r(O_NK, 256) + r(O_AQ, 256)
    cols += r(O_AK, 64) * 2 + r(O_AK + 64, 64) * 2
    cols += r(O_GQ, 128) + r(O_GK, 128) + r(O_GF, 32)
    cols += r(O_NV, 256) + r(O_GR, 256)
    cols += r(O_GK, 128) + r(O_GV, 256) + r(O_AV, 128)
    assert len(cols) == NFM + NTM
    return np.array(cols)


CST_OFF = {}


def make_consts():
    f = np.float64
    ident = np.eye(128)
    ones = np.ones((128, 128))
    blk64 = np.kron(np.eye(2), np.ones((64, 64)))
    prot = np.zeros((128, 128))
    for hb in range(2):
        for i in range(32):
            prot[hb * 64 + i + 32, hb * 64 + i] = -1.0
            prot[hb * 64 + i, hb * 64 + i + 32] = 1.0
    a = np.arange(64)
    ang = 2 * np.pi * ((a[:, None] * a[None, :]) % 64) / 64
    C64, S64 = np.cos(ang), np.sin(ang)
    bd = np.zeros((128, 256))
    for gl in range(2):
        bd[gl * 64:(gl + 1) * 64, gl * 64:(gl + 1) * 64] = C64
        bd[gl * 64:(gl + 1) * 64, 128 + gl * 64:128 + (gl + 1) * 64] = -S64
    m1 = np.zeros((128, 128))
    m1[0:64, 0:64] = C64
    m1[64:128, 0:64] = S64
    m1[0:64, 64:128] = -S64
    m1[64:128, 64:128] = C64
    bmask = np.zeros((128, 256))
    for p in range(128):
        bmask[p, (p // 32) * 64:(p // 32 + 1) * 64] = 1.0
    trimask = np.zeros((128, 2, 4, 64))
    j = np.arange(64)[:, None]
    i = np.arange(64)[None, :]
    trimask[0:64, 0, :, :] = (j <= i)[:, None, :]
    trimask[0:64, 1, :, :] = (j >= i)[:, None, :]
    tri4 = np.zeros((128, 4, 64))
    tt = np.arange(64)[:, None]
    ii = np.arange(64)[None, :]
    tri4[0:64, 0] = (tt <= ii) * (-1.0 / 16)
    tri4[0:64, 1] = (tt >= ii) * (-1.0 / 16)
    tri4[0:64, 2] = (tt > ii) * (-1.0 / 16)
    tri4[0:64, 3] = (tt < ii) * (-1.0 / 16)
    parts = [("ident", ident), ("ones", ones), ("blk64", blk64), ("prot", prot), ("bd", bd), ("m1", m1),
             ("bmask", bmask), ("trimask", trimask.reshape(128, 512)), ("tri4", tri4.reshape(128, 256))]
    off = 0
    for n, arr in parts:
        CST_OFF[n] = (off, arr.shape[1])
        off += arr.shape[1]
    cst = np.concatenate([p[1] for p in parts] + [np.zeros((128, 2048 - off))], axis=1).astype(np.float32)
    n2 = np.arange(64)[:, None, None]
    k1 = np.arange(64)[None, :, None]
    k2 = np.arange(64)[None, None, :]
    th = 2 * np.pi * ((n2 * (64 * k2 + k1)) % 4096) / 4096
    gtab = np.concatenate([np.cos(th), np.sin(th)], axis=0) / 512.0
    gtab = gtab.reshape(128, 4096).astype(np.float32)
    nl = np.arange(128)[:, None, None]
    bk = np.arange(2)[None, :, None]
    kk = np.arange(256)[None, None, :]
    thz = 2 * np.pi * (((bk * 128 + nl) * kk) % 256) / 256
    czsz = np.concatenate([np.cos(thz) / 128.0, np.sin(thz) / 128.0], axis=1).reshape(128, 1024).astype(np.float32)
    t = np.arange(SEQ)
    row = (t // 64).astype(np.float32)
    col = (t % 64).astype(np.float32)
    inv = (np.float32(10000.0) ** (-np.arange(16, dtype=np.float32) / np.float32(16))).astype(np.float32)
    angr = np.concatenate([row[:, None] * inv[None, :], col[:, None] * inv[None, :]], axis=1).astype(np.float32)
    cosf = np.cos(angr).astype(np.float32).T
    sinf = np.sin(angr).astype(np.float32).T
    cosT = np.tile(cosf, (4, 1)).astype(np.float32)
    sinT = np.tile(sinf, (4, 1)).astype(np.float32)
    return dict(cst=cst, gtab=gtab, czsz=czsz, cosT=np.ascontiguousarray(cosT), sinT=np.ascontiguousarray(sinT))


def make_na_bias(rpb):
    L = rpb.shape[0]
    out = np.full((L, 128, 4, 21, 128), -30000.0, dtype=np.float32)
    pats = [(10, 10 + d) for d in (-2, -1, 0, 1, 2)]
    for rq in (0, 1):
        pats += [(rq, kt) for kt in range(4)]
    for rq in (30, 31):
        pats += [(rq, kt) for kt in range(28, 32)]
    kr_l = np.arange(2)[:, None, None, None]
    kc = np.arange(64)[None, :, None, None]
    qr_l = np.arange(2)[None, None, :, None]
    qc = np.arange(64)[None, None, None, :]
    for pi, (rq, kt) in enumerate(pats):
        qr = 2 * rq + qr_l
        kr = 2 * kt + kr_l
        rs = np.clip(qr - 4, 0, 56)
        cs = np.clip(qc - 8, 0, 48)
        valid = (kr >= rs) & (kr < rs + 8) & (kc >= cs) & (kc < cs + 16)
        valid = np.broadcast_to(valid, (2, 64, 2, 64))
        ri = np.broadcast_to(np.clip(kr - qr + 7, 0, 14), (2, 64, 2, 64))
        ci = np.broadcast_to(np.clip(kc - qc + 15, 0, 30), (2, 64, 2, 64))
        for l in range(L):
            for h in range(4):
                g = rpb[l, h][ri, ci]
                tile = np.where(valid, g, np.float32(-30000.0)).astype(np.float32)
                out[l, :, h, pi, :] = tile.reshape(128, 128)
    return out


class Res:
    __slots__ = ("w", "rd", "name")

    def __init__(self, name=""):
        self.w = []
        self.rd = []
        self.name = name


class Eng:
    def __init__(self, name, h, sem):
        self.name, self.h, self.sem, self.cnt, self.known = name, h, sem, 0, {}


class Emit:
    def __init__(self, nc, es):
        self.nc = nc
        mk = lambda n: es.enter_context(nc.semaphore(n))
        self.pe = Eng("pe", nc.tensor, mk("s_pe"))
        self.act = Eng("act", nc.scalar, mk("s_act"))
        self.dve = Eng("dve", nc.vector, mk("s_dve"))
        self.pool = Eng("pool", nc.gpsimd, mk("s_pool"))
        self.sp = Eng("sp", nc.sync, mk("s_sp"))
        self.engs = [self.pe, self.act, self.dve, self.pool, self.sp]
        self.dsem = [mk(f"s_d{i}") for i in range(NDS)]
        self.dtot = [0] * NDS
        self.dnext = 0
        self.n_ops = 0

    def _wait(self, eng, evs, same_ok=True):
        for sem, val in evs:
            if same_ok and sem is eng.sem:
                continue
            if eng.known.get(id(sem), 0) < val:
                eng.h.wait_ge(sem, val)
                eng.known[id(sem)] = val

    @staticmethod
    def _deps(rd, wr, wa):
        evs = []
        for r in rd:
            evs += r.w
        for r in wr:
            evs += r.w
            evs += r.rd
        for r in wa:
            evs += r.rd
        return evs

    @staticmethod
    def _addrd(r, ev):
        r.rd = [e for e in r.rd if not (e[0] is ev[0] and e[1] <= ev[1])]
        r.rd.append(ev)

    def op(self, eng, fn, rd=(), wr=(), inc=True):
        self._wait(eng, self._deps(rd, wr, ()), same_ok=(eng is self.pe))
        ins = fn()
        self.n_ops += 1
        if inc:
            eng.cnt += 1
            ins.then_inc(eng.sem, 1)
            ev = (eng.sem, eng.cnt)
        else:
            ev = (eng.sem, eng.cnt + 1)
        for r in rd:
            self._addrd(r, ev)
        for r in wr:
            r.w = [ev]
            r.rd = []
        return ins

    def dma(self, q, out, in_, rd=(), wr=(), wa=()):
        self._wait(q, self._deps(rd, wr, wa), same_ok=False)
        i = self.dnext
        self.dnext = (self.dnext + 1) % NDS
        sem = self.dsem[i]
        if q.known.get(id(sem), 0) < self.dtot[i]:
            q.h.wait_ge(sem, self.dtot[i])
            q.known[id(sem)] = self.dtot[i]
        self.dtot[i] += 16
        q.h.dma_start(out=out, in_=in_).then_inc(sem, 16)
        self.n_ops += 1
        ev = (sem, self.dtot[i])
        for r in rd:
            r.rd.append(ev)
        for r in wr:
            r.w = [ev]
            r.rd = []
        for r in wa:
            r.w.append(ev)
        return ev

    def barrier(self):
        evs = [(e.sem, e.cnt) for e in self.engs if e.cnt > 0]
        evs += [(self.dsem[i], self.dtot[i]) for i in range(NDS) if self.dtot[i] > 0]
        for e in self.engs:
            self._wait(e, evs)


def tiles_all():
    return [(0, 256, 1)] + [(256 + 512 * i, 512, 0) for i in range(8)]


def build(dbg=(), stop=None):
    nc = bass.Bass("TRN2", target_bir_lowering=False)
    es = contextlib.ExitStack()
    with es:
        _build(nc, es, set(dbg), stop)
    return nc


def _build(nc, es, dbg, stop):
    em = Emit(nc, es)
    pe, act, dve, pool, sp = em.pe, em.act, em.dve, em.pool, em.sp
    T_, V_, S_, G_ = nc.tensor, nc.vector, nc.scalar, nc.gpsimd

    def din(name, shape, dt=F32):
        return nc.dram_tensor(name, list(shape), dt, kind="ExternalInput").ap()

    def dscr(name, shape, dt):
        kind = "ExternalOutput" if name in dbg else "Internal"
        return nc.dram_tensor(name, list(shape), dt, kind=kind).ap()

    x_in = din("x", [SEQ, D])
    ctx_in = din("ctx", [LC, D])
    cc_in = din("cc", [128, 8, 2])
    w_mod = din("w_mod", [DEPTH, D, 9 * D])
    bmod_in = din("bmod_c", [DEPTH, 128, 72])
    fw = {}
    for f_ in (1, 2):
        fw[f_] = (din(f"f{f_}w1", [DEPTH, D, DFF]), din(f"f{f_}w3", [DEPTH, D, DFF]), din(f"f{f_}w2", [DEPTH, DFF, D]))
    w_in_r = din("w_in_r", [DEPTH, D, NFM + NTM])
    w_out = din("w_out", [DEPTH, D, D])
    gcol_in = din("gcol", [DEPTH, 128, 4])
    glan_in = din("glan", [DEPTH, 256])
    wg_in = din("wg", [DEPTH, 33, 256])
    braw_in = din("braw", [DEPTH, 128, 4 * 21 * 128])
    cst_in = din("cst", [128, 2048])
    gtab_in = din("gtab", [128, 4096])
    czsz_in = din("czsz", [128, 1024])
    cosT_in = din("cosT", [128, SEQ])
    sinT_in = din("sinT", [128, SEQ])
    out_d = nc.dram_tensor("out", [SEQ, D], F32, kind="ExternalOutput").ap()

    XS = dscr("XS", [128, 8, T], F32)
    MIXT = dscr("MIXT", [128, 8, T], BF16)
    VSX = dscr("VSX", [SEQ, 512], BF16)
    AS_ = dscr("AS", [128, 64, 256], BF16)
    NQs = dscr("NQs", [128, 2, T], BF16)
    NKs = dscr("NKs", [128, 2, T], BF16)
    NVs = dscr("NVs", [T, 256], BF16)
    AQs = dscr("AQs", [128, 2, T], BF16)
    AKs = dscr("AKs", [128, 2, T], BF16)
    AVs = dscr("AVs", [T, 128], BF16)
    GQs = dscr("GQs", [128, T], F32)
    GKs = dscr("GKs", [128, T], F32)
    GGs = dscr("GGs", [32, T], F32)
    GKts = dscr("GKts", [T, 128], BF16)
    GVs = dscr("GVs", [T, 256], BF16)
    GRs = dscr("GRs", [T, 256], BF16)
    r_XS = [Res(f"XS{i}") for i in range(9)]
    r_MIXT = Res("MIXT")
    r_scr = {n: Res(n) for n in ("VSX", "AS", "NQ", "NK", "NV", "AQ", "AK", "AV", "GQ", "GK", "GG", "GKt", "GV", "GR")}

    uid = [0]

    def sb(st, name, shape, dt=F32):
        uid[0] += 1
        return st.enter_context(nc.sbuf_tensor(f"sb{uid[0]}_{name}", list(shape), dt))

    ps = es.enter_context(nc.psum_tensor("ps", [128, 8, 512], F32))
    psr = [Res(f"ps{i}") for i in range(8)]
    cst = sb(es, "cst", [128, 256], F32)
    cstb = sb(es, "cstb", [128, 896], BF16)
    modv = sb(es, "modv", [128, DEPTH, 72, 2], F32)
    r_cst = Res("cst")
    r_mod = Res("mod")

    def C(name, rows=128):
        o, n = CST_OFF[name]
        return cst[0:rows, o:o + n]

    def CB(name, rows=128):
        o, n = CST_OFF[name]
        return cstb[0:rows, o:o + n]

    em.dma(sp, cst[:], cst_in[:, 0:256], wr=[r_cst])
    em.dma(pool, cstb[:], cst_in[:, 0:896], wa=[r_cst])
    ident = C("ident")
    ones_b = CB("ones")
    blk64_b = CB("blk64")
    prot_b = CB("prot")
    bd_b = CB("bd")
    m1_b = CB("m1")

    def mcol(l, i, k, s):
        return modv[:, l, i * 8 + k, s:s + 1]

    def phase_mod():
        with contextlib.ExitStack() as ph:
            ccs = sb(ph, "ccs", [128, 8, 2])
            bm = sb(ph, "bm", [128, DEPTH, 72])
            wm = [sb(ph, f"wm{i}", [128, 8, 512]) for i in range(3)]
            rows = sb(ph, "mrows", [2, 9 * D])
            r_cc, r_bm, r_rows = Res(), Res(), Res()
            r_wm = [Res(), Res(), Res()]
            em.dma(sp, ccs[:], cc_in, wr=[r_cc])
            em.dma(sp, bm[:], bmod_in.rearrange("l p j -> p l j"), wr=[r_bm])
            em.op(act, lambda: S_.activation(ccs[:], ccs[:], AF.Silu), wr=[r_cc])
            ns = 0
            for l in range(DEPTH):
                for slab in range(18):
                    b = ns % 3
                    ns += 1
                    src = w_mod[l, :, slab * 512:(slab + 1) * 512].rearrange("(k p) n -> p k n", p=128)
                    em.dma(sp, wm[b][:], src, wr=[r_wm[b]])
                    pb = slab % 2
                    for k in range(8):
                        em.op(pe, lambda: T_.matmul(ps[0:2, pb, :], ccs[:, k, :], wm[b][:, k, :], start=(k == 0), stop=(k == 7)),
                              rd=[r_wm[b], r_cc], wr=[psr[pb]], inc=(k == 7))
                    em.op(act, lambda: S_.copy(rows[:, slab * 512:(slab + 1) * 512], ps[0:2, pb, :]), rd=[psr[pb]], wr=[r_rows])
                for j in range(72):
                    em.op(pe, lambda: T_.transpose(ps[:, 2 + l, 2 * j:2 * j + 2], rows[0:2, j * 128:(j + 1) * 128], ident[0:2, 0:2]),
                          rd=[r_rows, r_cst], wr=[psr[2 + l]], inc=(j == 71))
                em.op(dve, lambda: V_.tensor_tensor(modv[:, l, :, :], ps[:, 2 + l, 0:144].rearrange("p (j s) -> p j s", s=2),
                                                    bm[:, l, :].unsqueeze(2).to_broadcast([128, 72, 2]), ALU.add),
                      rd=[psr[2 + l], r_bm], wr=[r_mod])
                for i in (1, 4, 7):
                    em.op(dve, lambda: V_.tensor_scalar(modv[:, l, i * 8:(i + 1) * 8, :], modv[:, l, i * 8:(i + 1) * 8, :],
                                                        1.0, None, ALU.add), wr=[r_mod])
                for i in (2, 8):
                    em.op(dve, lambda: V_.tensor_scalar(modv[:, l, i * 8:(i + 1) * 8, :], modv[:, l, i * 8:(i + 1) * 8, :],
                                                        0.5, None, ALU.mult), wr=[r_mod])
            if "modv_o" in dbg:
                mo = nc.dram_tensor("modv_o", [128, DEPTH * 144], F32, kind="ExternalOutput").ap()
                em.dma(sp, mo, modv[:].rearrange("p l j s -> p (l j s)"), rd=[r_mod])
            em.barrier()

    def phase_T():
        NBUF = 4 if OPT_T4 else 2
        with contextlib.ExitStack() as ph:
            xin = [sb(ph, f"xin{i}", [128, D]) for i in range(NBUF)]
            xtb = [sb(ph, f"xtb{i}", [128, 8, 128]) for i in range(NBUF)]
            r_xin = [Res() for _ in range(NBUF)]
            r_xtb = [Res() for _ in range(NBUF)]

            def load(blk):
                b = blk % NBUF
                src = ctx_in[blk * 128:(blk + 1) * 128, :] if blk < 2 else x_in[(blk - 2) * 128:(blk - 1) * 128, :]
                em.dma(sp, xin[b][:], src, wr=[r_xin[b]])

            for blk in range(NBUF - 1):
                load(blk)
            for blk in range(34):
                b = blk % NBUF
                if blk + NBUF - 1 < 34:
                    load(blk + NBUF - 1)
                for k in range(8):
                    bank = b * 2 + k // 4
                    em.op(pe, lambda: T_.transpose(ps[:, bank, (k % 4) * 128:(k % 4 + 1) * 128], xin[b][:, k * 128:(k + 1) * 128], ident),
                          rd=[r_xin[b], r_cst], wr=[psr[bank]], inc=(k % 4 == 3))
                em.op(act, lambda: S_.copy(xtb[b][:, 0:4, :], ps[:, b * 2, :].rearrange("p (k t) -> p k t", k=4)),
                      rd=[psr[b * 2]], wr=[r_xtb[b]])
                em.op(dve, lambda: V_.tensor_copy(xtb[b][:, 4:8, :], ps[:, b * 2 + 1, :].rearrange("p (k t) -> p k t", k=4)),
                      rd=[psr[b * 2 + 1]], wr=[r_xtb[b]])
                ti = 0 if blk < 2 else 1 + (blk - 2) // 4
                em.dma(sp, XS[:, :, blk * 128:(blk + 1) * 128], xtb[b][:], rd=[r_xtb[b]], wa=[r_XS[ti]])
            em.barrier()

    def norm_mod(xT, r_x, hT, r_h, rstd, r_rstd, tmp, r_tmp, N, l, i0, s, ssbank, part="all"):
        if part in ("all", "sq"):
            for k in range(8):
                em.op(act, lambda: S_.activation(hT[:, k, :N], xT[:, k, :N], AF.Square), rd=[r_x], wr=[r_h])
        if part == "sq":
            return
        for k in range(8):
            em.op(pe, lambda: T_.matmul(ps[:, ssbank, :N], ones_b, hT[:, k, :N], start=(k == 0), stop=(k == 7)),
                  rd=[r_h, r_cst], wr=[psr[ssbank]], inc=(k == 7))
        em.op(act, lambda: S_.activation(rstd[:, :N], ps[:, ssbank, :N], AF.Ln, bias=eps_c[:, 0:1], scale=1.0 / D),
              rd=[psr[ssbank]], wr=[r_rstd])
        em.op(act, lambda: S_.activation(rstd[:, :N], rstd[:, :N], AF.Exp, scale=-0.5), wr=[r_rstd])
        for k in range(8):
            tb = tmp[k % 2]
            em.op(dve, lambda: V_.tensor_tensor(tb[:, :N], xT[:, k, :N], rstd[:, :N], ALU.mult),
                  rd=[r_x, r_rstd], wr=[r_tmp[k % 2]])
            em.op(act, lambda: S_.activation(hT[:, k, :N], tb[:, :N], AF.Identity, bias=mcol(l, i0, k, s), scale=mcol(l, i0 + 1, k, s)),
                  rd=[r_tmp[k % 2], r_mod], wr=[r_h])

    eps_c = sb(es, "eps_c", [128, 1])
    em.op(dve, lambda: V_.memset(eps_c[:], EPS), wr=[r_cst])

    def ffn_weights(st, l, which):
        W1, W3, W2 = fw[which]
        w1 = sb(st, "w1", [128, 8, DFF], BF16)
        w3 = sb(st, "w3", [128, 8, DFF], BF16)
        w2 = sb(st, "w2", [128, NJ, D], BF16)
        r_w1, r_w3, r_w2 = [Res(), Res()], [Res(), Res()], [Res(), Res()]
        H = DFF // 2
        for hh in range(2):
            em.dma(pool, w1[:, :, hh * H:(hh + 1) * H], W1[l, :, hh * H:(hh + 1) * H].rearrange("(k p) n -> p k n", p=128), wr=[r_w1[hh]])
            em.dma(pool, w3[:, :, hh * H:(hh + 1) * H], W3[l, :, hh * H:(hh + 1) * H].rearrange("(k p) n -> p k n", p=128), wr=[r_w3[hh]])
        for hh in range(2):
            em.dma(pool, w2[:, hh * 11:(hh + 1) * 11, :], W2[l, hh * H:(hh + 1) * H, :].rearrange("(j p) n -> p j n", p=128), wr=[r_w2[hh]])
        return w1, w3, w2, r_w1, r_w3, r_w2

    def phase_ffn(l, which, final, pre=None):
        i0 = 0 if which == 1 else 6
        tl = tiles_all()
        if final:
            tl = tl[1:]
        with contextlib.ExitStack() as ph:
            w1, w3, w2, r_w1, r_w3, r_w2 = pre if pre is not None else ffn_weights(ph, l, which)
            xT = [sb(ph, f"xT{i}", [128, 8, 512]) for i in range(2)]
            gT = sb(ph, "gT", [128, NJ, 512], BF16)
            hT = sb(ph, "hT", [128, 8, 512], BF16)
            rstd = sb(ph, "rstd", [128, 512])
            tmp = [sb(ph, f"tmp{i}", [128, 512]) for i in range(2)]
            r_x = [Res(), Res()]
            r_g, r_h, r_rstd = Res(), Res(), Res()
            r_tmp = [Res(), Res()]
            if final:
                ot = sb(ph, "ot", [128, 512])
                r_ot = Res()

            def load(ti):
                t0, N, s = tl[ti]
                gi = ti if not final else ti + 1
                em.dma(sp, xT[ti % 2][:, :, :N], XS[:, :, t0:t0 + N], rd=[r_XS[gi]], wr=[r_x[ti % 2]])

            load(0)
            norm_mod(xT[0], r_x[0], hT, r_h, rstd, r_rstd, tmp, r_tmp, tl[0][1], l, i0, tl[0][2], 6)
            for ti, (t0, N, s) in enumerate(tl):
                b = ti % 2
                gi = ti if not final else ti + 1
                if ti + 1 < len(tl):
                    load(ti + 1)
                for j in range(NJ):
                    hh = j // 11
                    pu1, pu3 = (j % 2), 2 + (j % 2)
                    for k in range(8):
                        em.op(pe, lambda: T_.matmul(ps[:, pu1, :N], w1[:, k, j * 128:(j + 1) * 128], hT[:, k, :N], start=(k == 0), stop=(k == 7)),
                              rd=[r_w1[hh], r_h], wr=[psr[pu1]], inc=(k == 7))
                    for k in range(8):
                        em.op(pe, lambda: T_.matmul(ps[:, pu3, :N], w3[:, k, j * 128:(j + 1) * 128], hT[:, k, :N], start=(k == 0), stop=(k == 7)),
                              rd=[r_w3[hh], r_h], wr=[psr[pu3]], inc=(k == 7))
                    tb = tmp[j % 2]
                    em.op(act, lambda: S_.activation(tb[:, :N], ps[:, pu1, :N], AF.Silu), rd=[psr[pu1]], wr=[r_tmp[j % 2]])
                    em.op(dve, lambda: V_.tensor_tensor(gT[:, j, :N], tb[:, :N], ps[:, pu3, :N], ALU.mult),
                          rd=[r_tmp[j % 2], psr[pu3]], wr=[r_g])
                for m in range(8):
                    if m == 2 and ti + 1 < len(tl):
                        nb_ = (ti + 1) % 2
                        norm_mod(xT[nb_], r_x[nb_], hT, r_h, rstd, r_rstd, tmp, r_tmp, tl[ti + 1][1], l, i0, tl[ti + 1][2], 6)
                    py = 4 + (m % 2)
                    for j in range(NJ):
                        em.op(pe, lambda: T_.matmul(ps[:, py, :N], w2[:, j, m * 128:(m + 1) * 128], gT[:, j, :N], start=(j == 0), stop=(j == NJ - 1)),
                              rd=[r_w2[j // 11], r_g], wr=[psr[py]], inc=(j == NJ - 1))
                    em.op(dve, lambda: V_.scalar_tensor_tensor(xT[b][:, m, :N], ps[:, py, :N], mcol(l, i0 + 2, m, s), xT[b][:, m, :N], ALU.mult, ALU.add),
                          rd=[psr[py], r_mod], wr=[r_x[b]])
                if not final:
                    em.dma(sp, XS[:, :, t0:t0 + N], xT[b][:, :, :N], rd=[r_x[b]], wr=[r_XS[gi]])
                else:
                    for tb_ in range(N // 128):
                        for half in range(2):
                            for mm in range(4):
                                m = half * 4 + mm
                                em.op(pe, lambda: T_.transpose(ps[:, 7, mm * 128:(mm + 1) * 128], xT[b][:, m, tb_ * 128:(tb_ + 1) * 128], ident),
                                      rd=[r_x[b]], wr=[psr[7]], inc=(mm == 3))
                            em.op(act, lambda: S_.copy(ot[:], ps[:, 7, :]), rd=[psr[7]], wr=[r_ot])
                            r0 = t0 - LC + tb_ * 128
                            em.dma(sp, out_d[r0:r0 + 128, half * 512:(half + 1) * 512], ot[:], rd=[r_ot])
            em.barrier()

    def phase_P(l):
        last = (l == DEPTH - 1)
        with contextlib.ExitStack() as ph:
            wfm = sb(ph, "wfm", [128, 8, NFM], BF16)
            wtm = sb(ph, "wtm", [128, 8, NTM], BF16)
            xT = [sb(ph, f"pxT{i}", [128, 8, 512]) for i in range(2)]
            hTs = [sb(ph, f"phT{i}", [128, 8, 512], BF16) for i in range(2)]
            r_hs = [Res(), Res()]
            rstd = sb(ph, "prstd", [128, 512])
            tmp = [sb(ph, f"ptmp{i}", [128, 512]) for i in range(2)]
            cosT = sb(ph, "cosT", [128, SEQ])
            sinT = sb(ph, "sinT", [128, SEQ])
            gcol = sb(ph, "gcol", [128, 4])
            czsz = sb(ph, "czsz", [128, 1024], BF16)
            hxf = sb(ph, "hxf", [128, 2, 512], BF16)
            vz = sb(ph, "vz", [128, 2, 512], BF16)
            NB = 10
            stb = [sb(ph, f"stb{i}", [128, 512], BF16) for i in range(NB)]
            stf = [sb(ph, f"stf{i}", [128, 512]) for i in range(NB)]
            r_stb = [Res() for _ in range(NB)]
            r_stf = [Res() for _ in range(NB)]
            cnt = {"b": 0, "f": 0}

            def nb():
                i = cnt["b"] % NB
                cnt["b"] += 1
                return stb[i], r_stb[i]

            def nf():
                i = cnt["f"] % NB
                cnt["f"] += 1
                return stf[i], r_stf[i]

            r_wfm, r_wtm, r_tab, r_x = Res(), Res(), Res(), [Res(), Res()]
            r_rstd, r_tmp, r_hxf, r_vz = Res(), [Res(), Res()], Res(), Res()
            em.dma(pool, wfm[:], w_in_r[l, :, 0:NFM].rearrange("(k p) n -> p k n", p=128), wr=[r_wfm])
            em.dma(pool, wtm[:], w_in_r[l, :, NFM:NFM + NTM].rearrange("(k p) n -> p k n", p=128), wr=[r_wtm])
            em.dma(sp, cosT[:], cosT_in, wr=[r_tab])
            em.dma(sp, sinT[:], sinT_in, wa=[r_tab])
            em.dma(sp, gcol[:], gcol_in[l], wa=[r_tab])
            em.dma(pool, czsz[:], czsz_in, wa=[r_tab])
            tl = tiles_all()

            def load(ti):
                t0, N, s = tl[ti]
                em.dma(sp, xT[ti % 2][:, :, :N], XS[:, :, t0:t0 + N], rd=[r_XS[ti]], wr=[r_x[ti % 2]])

            load(0)
            goff = {}
            o = 0
            for gi, (kind, c) in enumerate(FM_GROUPS):
                goff[gi] = o
                o += 32 if kind == "GG" else 128
            if OPT_PHT:
                norm_mod(xT[0], r_x[0], hTs[0], r_hs[0], rstd, r_rstd, tmp, r_tmp, tl[0][1], l, 3, tl[0][2], 0)
            for ti, (t0, N, s) in enumerate(tl):
                b = ti % 2
                hT, r_h = hTs[b], r_hs[b]
                if ti + 1 < len(tl):
                    load(ti + 1)
                if not OPT_PHT:
                    norm_mod(xT[b], r_x[b], hT, r_h, rstd, r_rstd, tmp, r_tmp, N, l, 3, s, 0)
                isx = (s == 0)
                xt0 = t0 - LC
                ACC = [1, 2, 5, 6]
                pend = []

                def stage_A(gi, kind, c, pb):
                    st8 = {}
                    if kind == "F":
                        em.op(act, lambda: S_.copy(hxf[:, c, :N], ps[:, pb, :N]), rd=[psr[pb]], wr=[r_hxf])
                        return None
                    if kind in ("GQ", "GK"):
                        st, rs = nf()
                        em.op(act, lambda: S_.copy(st[:, :N], ps[:, pb, :N]), rd=[psr[pb]], wr=[rs])
                        dst, rr = (GQs, r_scr["GQ"]) if kind == "GQ" else (GKs, r_scr["GK"])
                        em.dma(sp, dst[:, t0:t0 + N], st[:, :N], rd=[rs], wa=[rr])
                        return None
                    if kind == "GG":
                        st, rs = nf()
                        em.op(act, lambda: S_.copy(st[0:32, :N], ps[0:32, pb, :N]), rd=[psr[pb]], wr=[rs])
                        em.dma(sp, GGs[:, t0:t0 + N], st[0:32, :N], rd=[rs], wa=[r_scr["GG"]])
                        return None
                    sq, rsq = nb()
                    em.op(act, lambda: S_.activation(sq[:, :N], ps[:, pb, :N], AF.Square), rd=[psr[pb]], wr=[rsq])
                    st8.update(kind=kind, c=c, pb=pb, sq=sq, rsq=rsq)
                    return st8

                def stage_B(st8):
                    kind, c, pb, sq, rsq = st8["kind"], st8["c"], st8["pb"], st8["sq"], st8["rsq"]
                    gidx = {"NQ": 0, "NK": 1, "AQ": 2, "AK": 3}[kind]
                    em.op(pe, lambda: T_.matmul(ps[:, 3, :N], blk64_b, sq[:, :N], start=True, stop=True), rd=[rsq], wr=[psr[3]])
                    r64, rr64 = nf()
                    em.op(act, lambda: S_.activation(r64[:, :N], ps[:, 3, :N], AF.Ln, bias=eps_c[:, 0:1], scale=1.0 / 64), rd=[psr[3]], wr=[rr64])
                    em.op(act, lambda: S_.activation(r64[:, :N], r64[:, :N], AF.Exp, scale=-0.5), wr=[rr64])
                    qn, rqn = nb()
                    em.op(dve, lambda: V_.scalar_tensor_tensor(qn[:, :N], ps[:, pb, :N], gcol[:, gidx:gidx + 1], r64[:, :N], ALU.mult, ALU.mult),
                          rd=[psr[pb], rr64, r_tab], wr=[rqn])
                    st8.update(qn=qn, rqn=rqn)

                def stage_C(st8):
                    kind, c, qn, rqn = st8["kind"], st8["c"], st8["qn"], st8["rqn"]
                    res, rres = qn, rqn
                    if kind in ("AQ", "AK") and isx:
                        em.op(pe, lambda: T_.matmul(ps[:, 7, :N], prot_b, qn[:, :N], start=True, stop=True), rd=[rqn], wr=[psr[7]])
                        t1, rt1 = nf()
                        t2, rt2 = nf()
                        em.op(pool, lambda: G_.tensor_tensor(t1[:, :N], qn[:, :N], cosT[:, xt0:xt0 + N], ALU.mult), rd=[rqn, r_tab], wr=[rt1])
                        em.op(dve, lambda: V_.tensor_tensor(t2[:, :N], ps[:, 7, :N], sinT[:, xt0:xt0 + N], ALU.mult), rd=[psr[7], r_tab], wr=[rt2])
                        res, rres = nb()
                        em.op(pool, lambda: G_.tensor_tensor(res[:, :N], t1[:, :N], t2[:, :N], ALU.add), rd=[rt1, rt2], wr=[rres])
                    dst, rr = {"NQ": (NQs, r_scr["NQ"]), "NK": (NKs, r_scr["NK"]), "AQ": (AQs, r_scr["AQ"]), "AK": (AKs, r_scr["AK"])}[kind]
                    em.dma(sp, dst[:, c, t0:t0 + N], res[:, :N], rd=[rres], wa=[rr])

                def drain(keep):
                    while pend and len(pend) > keep:
                        st8 = pend[0]
                        if st8["stage"] == 1:
                            stage_C(st8)
                            pend.pop(0)
                        else:
                            break
                    for st8 in pend:
                        if st8["stage"] == 0 and st8["age"] >= 1:
                            stage_B(st8)
                            st8["stage"] = 1

                for gi, (kind, c) in enumerate(FM_GROUPS):
                    M = 32 if kind == "GG" else 128
                    pb = ACC[gi % 4]
                    for k in range(8):
                        em.op(pe, lambda: T_.matmul(ps[0:M, pb, :N], wfm[:, k, goff[gi]:goff[gi] + M], hT[:, k, :N], start=(k == 0), stop=(k == 7)),
                              rd=[r_wfm, r_h], wr=[psr[pb]], inc=(k == 7))
                    for st8 in list(pend):
                        st8["age"] += 1
                    for st8 in list(pend):
                        if st8["stage"] == 1 and st8["age"] >= 2:
                            stage_C(st8)
                            pend.remove(st8)
                    for st8 in pend:
                        if st8["stage"] == 0 and st8["age"] >= 1:
                            stage_B(st8)
                            st8["stage"] = 1
                    st8 = stage_A(gi, kind, c, pb)
                    if st8 is not None:
                        st8["stage"], st8["age"] = 0, 0
                        pend.append(st8)
                    if OPT_PHT and gi == 5 and ti + 1 < len(tl):
                        nb_ = (ti + 1) % 2
                        norm_mod(xT[nb_], r_x[nb_], hTs[nb_], r_hs[nb_], rstd, r_rstd, tmp, r_tmp, tl[ti + 1][1], l, 3, tl[ti + 1][2], 0, part="sq")
                    if OPT_PHT and gi == 9 and ti + 1 < len(tl):
                        nb_ = (ti + 1) % 2
                        norm_mod(xT[nb_], r_x[nb_], hTs[nb_], r_hs[nb_], rstd, r_rstd, tmp, r_tmp, tl[ti + 1][1], l, 3, tl[ti + 1][2], 0, part="rest")
                for st8 in list(pend):
                    if st8["stage"] == 0:
                        stage_B(st8)
                        st8["stage"] = 1
                for st8 in list(pend):
                    stage_C(st8)
                pend.clear()
                for tb_ in range(N // 128):
                    for c in range(2):
                        em.op(pe, lambda: T_.matmul(ps[:, 4, :].rearrange("p (r c f) -> p r c f", r=2, c=2)[:, :, c, :],
                                                    hxf[:, c, tb_ * 128:(tb_ + 1) * 128], bd_b.rearrange("p (r f) -> p r f", r=2),
                                                    start=True, stop=True),
                              rd=[r_hxf], wr=[psr[4]], inc=(c == 1))
                    if isx:
                        st, rs = nb()
                        em.op(act, lambda: S_.copy(st[:], ps[:, 4, :]), rd=[psr[4]], wr=[rs])
                        r0 = xt0 + tb_ * 128
                        em.dma(sp, VSX[r0:r0 + 128, :], st[:], rd=[rs], wa=[r_scr["VSX"]])
                    elif not last:
                        em.op(act, lambda: S_.copy(vz[:, tb_, :], ps[:, 4, :]), rd=[psr[4]], wr=[r_vz])
                if (not isx) and (not last):
                    for fc in range(2):
                        n_ = 0
                        for tb_ in range(2):
                            for ri in range(2):
                                em.op(pe, lambda: T_.matmul(ps[:, 7, 0:256], vz[:, tb_, ri * 256 + fc * 128: ri * 256 + (fc + 1) * 128],
                                                            czsz[:, ri * 512 + tb_ * 256: ri * 512 + (tb_ + 1) * 256],
                                                            start=(n_ == 0), stop=(n_ == 3)),
                                      rd=[r_vz, r_tab], wr=[psr[7]], inc=(n_ == 3))
                                n_ += 1
                        st, rs = nb()
                        em.op(act, lambda: S_.copy(st[:, 0:256], ps[:, 7, 0:256]), rd=[psr[7]], wr=[rs])
                        em.dma(sp, MIXT[:, fc, 0:256], st[:, 0:256], rd=[rs], wa=[r_MIXT])
                for tb_ in range(N // 128):
                    r0 = t0 + tb_ * 128
                    for g in range(2):
                        for k in range(8):
                            em.op(pe, lambda: T_.matmul(ps[:, 5 + g, :], hT[:, k, tb_ * 128:(tb_ + 1) * 128], wtm[:, k, g * 512:(g + 1) * 512],
                                                        start=(k == 0), stop=(k == 7)),
                                  rd=[r_wtm, r_h], wr=[psr[5 + g]], inc=(k == 7))
                    st, rs = nb()
                    em.op(act, lambda: S_.copy(st[:, 0:256], ps[:, 5, 0:256]), rd=[psr[5]], wr=[rs])
                    em.op(act, lambda: S_.activation(st[:, 256:512], ps[:, 5, 256:512], AF.Silu), rd=[psr[5]], wr=[rs])
                    em.dma(sp, NVs[r0:r0 + 128, :], st[:, 0:256], rd=[rs], wa=[r_scr["NV"]])
                    em.dma(sp, GRs[r0:r0 + 128, :], st[:, 256:512], rd=[rs], wa=[r_scr["GR"]])
                    st2, rs2 = nb()
                    em.op(dve, lambda: V_.tensor_copy(st2[:], ps[:, 6, :]), rd=[psr[6]], wr=[rs2])
                    em.dma(sp, GKts[r0:r0 + 128, :], st2[:, 0:128], rd=[rs2], wa=[r_scr["GKt"]])
                    em.dma(sp, GVs[r0:r0 + 128, :], st2[:, 128:384], rd=[rs2], wa=[r_scr["GV"]])
                    em.dma(sp, AVs[r0:r0 + 128, :], st2[:, 384:512], rd=[rs2], wa=[r_scr["AV"]])
            em.barrier()

    def phase_O(l):
        last = (l == DEPTH - 1)
        tl = tiles_all()
        with contextlib.ExitStack() as ph:
            wo = sb(ph, "wo", [128, 8, D], BF16)
            xT = [sb(ph, f"oxT{i}", [128, 8, 512]) for i in range(2)]
            mT = [sb(ph, f"omT{i}", [128, 8, 512], BF16) for i in range(2)]
            r_wo, r_x, r_m = Res(), [Res(), Res()], [Res(), Res()]
            em.dma(pool, wo[:], w_out[l].rearrange("(k p) n -> p k n", p=128), wr=[r_wo])
            idx = list(range(1, 9)) if last else list(range(9))

            def load(n):
                ti = idx[n]
                t0, N, s = tl[ti]
                em.dma(sp, xT[n % 2][:, :, :N], XS[:, :, t0:t0 + N], rd=[r_XS[ti]], wr=[r_x[n % 2]])
                em.dma(sp, mT[n % 2][:, :, :N], MIXT[:, :, t0:t0 + N], rd=[r_MIXT], wr=[r_m[n % 2]])

            load(0)
            for n, ti in enumerate(idx):
                t0, N, s = tl[ti]
                b = n % 2
                if n + 1 < len(idx):
                    load(n + 1)
                for m in range(8):
                    pb = m % 2
                    for k in range(8):
                        em.op(pe, lambda: T_.matmul(ps[:, pb, :N], wo[:, k, m * 128:(m + 1) * 128], mT[b][:, k, :N], start=(k == 0), stop=(k == 7)),
                              rd=[r_wo, r_m[b]], wr=[psr[pb]], inc=(k == 7))
                    em.op(dve, lambda: V_.scalar_tensor_tensor(xT[b][:, m, :N], ps[:, pb, :N], mcol(l, 5, m, s), xT[b][:, m, :N], ALU.mult, ALU.add),
                          rd=[psr[pb], r_mod], wr=[r_x[b]])
                em.dma(sp, XS[:, :, t0:t0 + N], xT[b][:, :, :N], rd=[r_x[b]], wr=[r_XS[ti]])
            em.barrier()

    def mix_gqa(l):
        last = (l == DEPTH - 1)
        with contextlib.ExitStack() as ph:
            AK = sb(ph, "gAK", [128, 2, T], BF16)
            V1 = sb(ph, "gV1", [128, 34, 2, 65], BF16)
            Q = [sb(ph, f"gQ{i}", [128, 512], BF16) for i in range(2)]
            PT = [sb(ph, f"gPT{i}", [128, 1024], BF16) for i in range(3)]
            rden = [sb(ph, f"grden{i}", [128, 512]) for i in range(2)]
            On = [sb(ph, f"gOn{i}", [64, 512]) for i in range(2)]
            osb = [sb(ph, f"gosb{i}", [64, 512], BF16) for i in range(2)]
            r_AK, r_V1, r_Q, r_PT = Res(), Res(), [Res(), Res()], [Res(), Res(), Res()]
            r_rden, r_On, r_osb = [Res(), Res()], [Res(), Res()], [Res(), Res()]
            em.dma(sp, AK[:], AKs, rd=[r_scr["AK"]], wr=[r_AK])
            em.op(pool, lambda: G_.memset(V1[:, :, :, 64:65], 1.0), wr=[r_V1])
            for q4 in range(2):
                a, b_ = q4 * 17, q4 * 17 + 17
                for j_ in range(2):
                    em.dma(sp, V1[:, a:b_, j_, 0:64], AVs[a * 128:b_ * 128, j_ * 64:(j_ + 1) * 64].rearrange("(t p) d -> p t d", p=128),
                           rd=[r_scr["AV"]], wa=[r_V1])
            qtiles = [(256 + 512 * i, 512, list(range(34))) for i in range(8)]
            if not last:
                qtiles = [(0, 256, [0, 1])] + qtiles
            steps = []
            n = 0
            for j in range(2):
                for (t0, N, keys) in qtiles:
                    for ki, kt in enumerate(keys):
                        steps.append((n, j, t0, N, ki, kt, len(keys)))
                    n += 1
            cur = {}

            def qk_ex(si):
                n_, j, t0, N, ki, kt, nk = steps[si]
                Qt, rq = Q[n_ % 2], r_Q[n_ % 2]
                if ki == 0:
                    em.dma(sp, Qt[:, :N], AQs[:, j, t0:t0 + N], rd=[r_scr["AQ"]], wr=[rq])
                sbk = 2 * (si % 2)
                for g in range(2):
                    em.op(pe, lambda: T_.matmul(ps[:, sbk + g, :N], AK[64 * g:64 * g + 64, j, kt * 128:(kt + 1) * 128],
                                                Qt[64 * g:64 * g + 64, :N], start=True, stop=True, tile_position=(64 * g, 0)),
                          rd=[r_AK, rq], wr=[psr[sbk + g]])
                pt, rpt = PT[si % 3], r_PT[si % 3]
                em.op(act, lambda: S_.activation(pt[:, 0:2 * N].rearrange("p (g n) -> p g n", g=2), ps[:, sbk:sbk + 2, :N], AF.Exp, scale=0.125),
                      rd=[psr[sbk], psr[sbk + 1]], wr=[rpt])

            def pv(si):
                n_, j, t0, N, ki, kt, nk = steps[si]
                pt, rpt = PT[si % 3], r_PT[si % 3]
                for g in range(2):
                    em.op(pe, lambda: T_.matmul(ps[0:65, 4 + g, :N], V1[:, kt, j, :], pt[:, g * N:(g + 1) * N],
                                                start=(ki == 0), stop=(ki == nk - 1)),
                          rd=[r_V1, rpt], wr=[psr[4 + g]], inc=(ki == nk - 1))
                if ki == nk - 1:
                    for g in range(2):
                        ob, rob = osb[g], r_osb[g]
                        em.op(dve, lambda: V_.reciprocal(rden[g][64:65, :N], ps[64:65, 4 + g, :N]), rd=[psr[4 + g]], wr=[r_rden[g]])
                        em.op(pe, lambda: T_.matmul(ps[0:64, 6 + g, :N], C("ones")[64:65, 0:64], rden[g][64:65, :N], start=True, stop=True, tile_position=(64, 0)),
                              rd=[r_rden[g]], wr=[psr[6 + g]])
                        em.op(act, lambda: S_.copy(On[g][:, :N], ps[0:64, 4 + g, :N]), rd=[psr[4 + g]], wr=[r_On[g]])
                        em.op(dve, lambda: V_.tensor_tensor(ob[:, :N], On[g][:, :N], ps[0:64, 6 + g, :N], ALU.mult), rd=[r_On[g], psr[6 + g]], wr=[rob])
                        em.dma(sp, MIXT[64 * g:64 * g + 64, 6 + j, t0:t0 + N], ob[:, :N], rd=[rob], wa=[r_MIXT])

            qk_ex(0)
            for si in range(len(steps)):
                if si + 1 < len(steps):
                    qk_ex(si + 1)
                pv(si)
            em.barrier()

    def mix_na(l):
        last = (l == DEPTH - 1)
        with contextlib.ExitStack() as ph:
            NK = sb(ph, "nNK", [128, 2, T], BF16)
            NQ = sb(ph, "nNQ", [128, 2, T], BF16)
            V1 = sb(ph, "nV1", [128, 34, 4, 65], BF16)
            E = sb(ph, "nE", [128, 4, 21 * 128], BF16)
            PT = [sb(ph, f"nPT{i}", [128, 896], BF16) for i in range(3)]
            rden = [sb(ph, f"nrden{i}", [128, 512]) for i in range(2)]
            On = [sb(ph, f"nOn{i}", [64, 512]) for i in range(2)]
            osb = [sb(ph, f"nosb{i}", [64, 512], BF16) for i in range(2)]
            r_NK, r_NQ, r_V1, r_E, r_PT = Res(), Res(), Res(), Res(), [Res(), Res(), Res()]
            r_rden, r_On, r_osb = [Res(), Res()], [Res(), Res()], [Res(), Res()]
            em.dma(sp, NK[:], NKs, rd=[r_scr["NK"]], wr=[r_NK])
            em.dma(sp, NQ[:], NQs, rd=[r_scr["NQ"]], wr=[r_NQ])
            em.op(pool, lambda: G_.memset(V1[:, :, :, 64:65], 1.0), wr=[r_V1])
            for q4 in range(2):
                a, b_ = q4 * 17, q4 * 17 + 17
                for h_ in range(4):
                    em.dma(sp, V1[:, a:b_, h_, 0:64], NVs[a * 128:b_ * 128, h_ * 64:(h_ + 1) * 64].rearrange("(t p) d -> p t d", p=128),
                           rd=[r_scr["NV"]], wa=[r_V1])
            with contextlib.ExitStack() as ph2:
                stg = [sb(ph2, f"nstg{i}", [128, 21 * 128]) for i in range(2)]
                r_stg = [Res(), Res()]
                for h in range(4):
                    em.dma(sp, stg[h % 2][:], braw_in[l, :, h * 2688:(h + 1) * 2688], wr=[r_stg[h % 2]])
                    em.op(act, lambda: S_.activation(E[:, h, :], stg[h % 2][:], AF.Exp), rd=[r_stg[h % 2]], wr=[r_E])
                em.barrier()
            qts = []
            if not last:
                qts += [(0, [], 0, [0, 1]), (1, [], 0, [0, 1])]
            for rq in range(32):
                if 2 <= rq <= 29:
                    win, p0 = list(range(rq - 2, rq + 3)), 0
                elif rq < 2:
                    win, p0 = [0, 1, 2, 3], 5 + 4 * rq
                else:
                    win, p0 = [28, 29, 30, 31], 13 + 4 * (rq - 30)
                qts.append((2 + rq, [2 + w for w in win], p0, [0, 1]))
            steps = [(qn, h) for qn in range(len(qts)) for h in range(4)]

            def qk_ex(si):
                qn, h = steps[si]
                qt, wkeys, p0, ckeys = qts[qn]
                keys = wkeys + ckeys
                nk, nw = len(keys), len(wkeys)
                q0 = qt * 128
                c, g = h // 2, h % 2
                sbk = 2 * (si % 2)
                flat = ps[:, sbk:sbk + 2, :].rearrange("p b n -> p (b n)")
                for i, kt in enumerate(keys):
                    em.op(pe, lambda: T_.matmul(flat[:, i * 128:(i + 1) * 128], NK[64 * g:64 * g + 64, c, kt * 128:(kt + 1) * 128],
                                                NQ[64 * g:64 * g + 64, c, q0:q0 + 128], start=True, stop=True, tile_position=(64 * g, 0)),
                          rd=[r_NK, r_NQ], wr=[psr[sbk], psr[sbk + 1]], inc=(i == nk - 1))
                pt, rpt = PT[si % 3], r_PT[si % 3]
                em.op(act, lambda: S_.activation(pt[:, 0:nk * 128], flat[:, 0:nk * 128], AF.Exp, scale=0.125),
                      rd=[psr[sbk], psr[sbk + 1]], wr=[rpt])
                if nw:
                    em.op(dve, lambda: V_.tensor_tensor(pt[:, 0:nw * 128], pt[:, 0:nw * 128], E[:, h, p0 * 128:(p0 + nw) * 128], ALU.mult),
                          rd=[r_E], wr=[rpt])

            def pv(si):
                qn, h = steps[si]
                qt, wkeys, p0, ckeys = qts[qn]
                keys = wkeys + ckeys
                nk = len(keys)
                q0 = qt * 128
                ob_, bcb = 4 + (qn % 2), 6 + (qn % 2)
                pt, rpt = PT[si % 3], r_PT[si % 3]
                for i, kt in enumerate(keys):
                    em.op(pe, lambda: T_.matmul(ps[0:65, ob_, h * 128:(h + 1) * 128], V1[:, kt, h, :], pt[:, i * 128:(i + 1) * 128],
                                                start=(i == 0), stop=(i == nk - 1)),
                          rd=[r_V1, rpt], wr=[psr[ob_]], inc=(i == nk - 1))
                if h == 3:
                    ob, rob = osb[qn % 2], r_osb[qn % 2]
                    rd_, rrd = rden[qn % 2], r_rden[qn % 2]
                    on_, ron = On[qn % 2], r_On[qn % 2]
                    em.op(dve, lambda: V_.reciprocal(rd_[64:65, :], ps[64:65, ob_, :]), rd=[psr[ob_]], wr=[rrd])
                    em.op(pe, lambda: T_.matmul(ps[0:64, bcb, :], C("ones")[64:65, 0:64], rd_[64:65, :], start=True, stop=True, tile_position=(64, 0)),
                          rd=[rrd], wr=[psr[bcb]])
                    em.op(act, lambda: S_.copy(on_[:, :], ps[0:64, ob_, :]), rd=[psr[ob_]], wr=[ron])
                    em.op(dve, lambda: V_.tensor_tensor(ob[:, :], on_[:, :], ps[0:64, bcb, :], ALU.mult), rd=[ron, psr[bcb]], wr=[rob])
                    for hh in range(4):
                        c, g = hh // 2, hh % 2
                        em.dma(sp, MIXT[64 * g:64 * g + 64, 2 + c, q0:q0 + 128], ob[:, hh * 128:(hh + 1) * 128], rd=[rob], wa=[r_MIXT])

            qk_ex(0)
            for si in range(len(steps)):
                if si + 1 < len(steps):
                    qk_ex(si + 1)
                pv(si)
            em.barrier()

    def mix_fourier(l):
        with contextlib.ExitStack() as ph:
            Vst = sb(ph, "fVst", [128, 64, 256], BF16)
            Bst = sb(ph, "fBst", [128, 64, 256], BF16)
            Asb = sb(ph, "fAsb", [128, 64, 256], BF16)
            gtb = sb(ph, "fgtb", [128, 64, 64], BF16)
            Yt = sb(ph, "fYt", [128, 2, SEQ], BF16)
            r_Vq = [Res() for _ in range(4)]
            r_Aq = [Res() for _ in range(4)]
            r_Bq = [Res() for _ in range(4)]
            r_g, r_Y = Res(), Res()
            em.dma(pool, gtb[:], gtab_in.rearrange("p (a b) -> p a b", a=64), wr=[r_g])
            vsx4 = VSX.rearrange("(a b) (r f) -> a b r f", b=64, r=2)
            for q in range(4):
                for ri in range(2):
                    em.dma(sp, Vst[ri * 64:(ri + 1) * 64, 16 * q:16 * q + 16, :], vsx4[:, 16 * q:16 * q + 16, ri, :], rd=[r_scr["VSX"]], wa=[r_Vq[q]])
            for q in range(4):
                for cg in range(8 * q, 8 * q + 8):
                    pb = cg % 2
                    em.op(pe, lambda: T_.matmul(ps[:, pb, :], m1_b, Vst[:, 2 * cg:2 * cg + 2, :].rearrange("p a f -> p (a f)"), start=True, stop=True),
                          rd=[r_Vq[q]], wr=[psr[pb]])
                    dst = Asb[:, 2 * cg:2 * cg + 2, :].rearrange("p a f -> p (a f)")
                    if cg % 2 == 0:
                        em.op(act, lambda: S_.copy(dst, ps[:, pb, :]), rd=[psr[pb]], wr=[r_Aq[q]])
                    else:
                        em.op(dve, lambda: V_.tensor_copy(dst, ps[:, pb, :]), rd=[psr[pb]], wr=[r_Aq[q]])
                em.dma(sp, AS_[:, 16 * q:16 * q + 16, :], Asb[:, 16 * q:16 * q + 16, :], rd=[r_Aq[q]], wa=[r_scr["AS"]])
            for kq in range(4):
                for ri in range(2):
                    em.dma(sp, Bst[ri * 64:(ri + 1) * 64, 16 * kq:16 * kq + 16, :],
                           AS_[ri * 64 + 16 * kq:ri * 64 + 16 * kq + 16, :, :].rearrange("k n f -> n k f"),
                           rd=[r_scr["AS"]], wa=[r_Bq[kq]])
            n_ev = 0
            for kq in range(4):
                for c in range(2):
                    ytv = Yt[:, c, :].rearrange("p (k2 k1) -> p k2 k1", k1=64)
                    for kg in (2 * kq, 2 * kq + 1):
                        pb = 2 + (n_ev % 2)
                        for kk in range(8):
                            k1 = kg * 8 + kk
                            em.op(pe, lambda: T_.matmul(ps[:, pb, kk * 64:(kk + 1) * 64], Bst[:, k1, c * 128:(c + 1) * 128], gtb[:, k1, :], start=True, stop=True),
                                  rd=[r_Bq[kq], r_g], wr=[psr[pb]], inc=(kk == 7))
                        src = ps[:, pb, :].rearrange("p (kk k2) -> p k2 kk", kk=8)
                        if n_ev % 2 == 0:
                            em.op(act, lambda: S_.copy(ytv[:, :, kg * 8:(kg + 1) * 8], src), rd=[psr[pb]], wr=[r_Y])
                        else:
                            em.op(dve, lambda: V_.tensor_copy(ytv[:, :, kg * 8:(kg + 1) * 8], src), rd=[psr[pb]], wr=[r_Y])
                        n_ev += 1
            em.dma(sp, MIXT[:, 0:2, LC:T], Yt[:], rd=[r_Y], wa=[r_MIXT])
            em.barrier()

    def mix_fourier_v1(l):
        with contextlib.ExitStack() as ph:
            Vst = sb(ph, "fVst", [128, 64, 256], BF16)
            Asb = sb(ph, "fAsb", [128, 64, 256], BF16)
            gtb = sb(ph, "fgtb", [128, 64, 64], BF16)
            Yt = sb(ph, "fYt", [128, 2, SEQ], BF16)
            r_V, r_A, r_g, r_Y = Res(), Res(), Res(), Res()
            em.dma(pool, gtb[:], gtab_in.rearrange("p (a b) -> p a b", a=64), wr=[r_g])
            vsx4 = VSX.rearrange("(a b) (r f) -> a b r f", b=64, r=2)
            for ri in range(2):
                em.dma(sp, Vst[ri * 64:(ri + 1) * 64, :, :], vsx4[:, :, ri, :], rd=[r_scr["VSX"]], wa=[r_V])
            for cg in range(32):
                pb = cg % 2
                em.op(pe, lambda: T_.matmul(ps[:, pb, :], m1_b, Vst[:, 2 * cg:2 * cg + 2, :].rearrange("p a f -> p (a f)"), start=True, stop=True),
                      rd=[r_V], wr=[psr[pb]])
                dst = Asb[:, 2 * cg:2 * cg + 2, :].rearrange("p a f -> p (a f)")
                if cg % 2 == 0:
                    em.op(act, lambda: S_.copy(dst, ps[:, pb, :]), rd=[psr[pb]], wr=[r_A])
                else:
                    em.op(dve, lambda: V_.tensor_copy(dst, ps[:, pb, :]), rd=[psr[pb]], wr=[r_A])
            em.dma(sp, AS_, Asb[:], rd=[r_A], wr=[r_scr["AS"]])
            for ri in range(2):
                em.dma(sp, Vst[ri * 64:(ri + 1) * 64, :, :], AS_[ri * 64:(ri + 1) * 64, :, :].rearrange("k n f -> n k f"),
                       rd=[r_scr["AS"]], wr=([r_V] if ri == 0 else []), wa=([r_V] if ri == 1 else []))
            for c in range(2):
                ytv = Yt[:, c, :].rearrange("p (k2 k1) -> p k2 k1", k1=64)
                for kg in range(8):
                    pb = 2 + (kg % 2)
                    for kk in range(8):
                        k1 = kg * 8 + kk
                        em.op(pe, lambda: T_.matmul(ps[:, pb, kk * 64:(kk + 1) * 64], Vst[:, k1, c * 128:(c + 1) * 128], gtb[:, k1, :], start=True, stop=True),
                              rd=[r_V, r_g], wr=[psr[pb]], inc=(kk == 7))
                    src = ps[:, pb, :].rearrange("p (kk k2) -> p k2 kk", kk=8)
                    if kg % 2 == 0:
                        em.op(act, lambda: S_.copy(ytv[:, :, kg * 8:(kg + 1) * 8], src), rd=[psr[pb]], wr=[r_Y])
                    else:
                        em.op(dve, lambda: V_.tensor_copy(ytv[:, :, kg * 8:(kg + 1) * 8], src), rd=[psr[pb]], wr=[r_Y])
            em.dma(sp, MIXT[:, 0:2, LC:T], Yt[:], rd=[r_Y], wa=[r_MIXT])
            em.barrier()

    def mix_gla(l):
        last = (l == DEPTH - 1)
        NCH = T // 64
        qscale = 32.0 ** -0.5
        with contextlib.ExitStack() as phA:
            gcst = sb(phA, "lcst", [128, 1024])
            r_gcst = Res()
            em.dma(sp, gcst[:], cst_in[:, 896:1920], wr=[r_gcst])
            bmask = gcst[:, 0:256]
            trimask = gcst[0:64, 256:768]
            tri4 = gcst[0:64, 768:1024].rearrange("p (a i) -> p a i", a=4)
            qf = sb(phA, "lqf", [128, T], BF16)
            kf = sb(phA, "lkf", [128, T], BF16)
            qb = sb(phA, "lqb", [128, T], BF16)
            kb = sb(phA, "lkb", [128, T], BF16)
            Vv = sb(phA, "lVv", [64, NCH, 256], BF16)
            Sall = [sb(phA, f"lS{i}", [128, NCH, 256], BF16) for i in range(2)]
            Dcol = sb(phA, "lD", [128, 2, NCH])
            r_qk, r_Vv, r_S, r_D = Res(), Res(), [Res(), Res()], Res()
            for q4 in range(4):
                a, b_ = q4 * 17, q4 * 17 + 17
                em.dma(sp, Vv[:, a:b_, :], GVs[a * 64:b_ * 64, :].rearrange("(c p) f -> p c f", p=64), rd=[r_scr["GV"]], wa=[r_Vv])
            with contextlib.ExitStack() as phB:
                Kh = [sb(phB, f"lKh{i}", [64, NCH, 128], BF16) for i in range(2)]
                r_Kh = Res()
                with contextlib.ExitStack() as phC:
                    Wg = sb(phC, "lWg", [33, 256])
                    GGt = [sb(phC, f"lGG{i}", [33, 256]) for i in range(2)]
                    gq = [sb(phC, f"lgq{i}", [128, 256]) for i in range(2)]
                    gk = [sb(phC, f"lgk{i}", [128, 256]) for i in range(2)]
                    Ktg = [sb(phC, f"lKt{i}", [64, 4, 128], BF16) for i in range(2)]
                    SP = sb(phC, "lSP", [64, 4, 256])
                    SPh = sb(phC, "lSPh", [64, 4, 256], BF16)
                    SPl = sb(phC, "lSPl", [64, 4, 256], BF16)
                    tri4b = sb(phC, "ltri4b", [64, 4, 64], BF16)
                    r_SPh, r_SPl = Res(), Res()
                    EQ = sb(phC, "lEQ", [128, 2, 256])
                    EK = sb(phC, "lEK", [128, 2, 256])
                    EE = sb(phC, "lEE", [64, 2, 512])
                    one_c = sb(phC, "lone", [128, 1])
                    r_Wg, r_GG, r_gq, r_gk, r_Kt = Res(), [Res(), Res()], [Res(), Res()], [Res(), Res()], [Res(), Res()]
                    r_e1, r_SP, r_EQ, r_EK, r_EE, r_one = Res(), Res(), Res(), Res(), Res(), Res()
                    em.dma(sp, Wg[:], wg_in[l], wr=[r_Wg])
                    em.op(dve, lambda: V_.tensor_copy(tri4b[:], tri4), rd=[r_gcst], wr=[r_gcst])
                    em.op(dve, lambda: V_.memset(one_c[:], 1.0), wr=[r_one])
                    for i in range(2):
                        em.op(pool, lambda: G_.memset(GGt[i][:], 1.0), wr=[r_GG[i]])
                    for gi in range(NCH // 4):
                        b = gi % 2
                        t0 = gi * 256
                        em.dma(sp, GGt[b][0:32, :], GGs[:, t0:t0 + 256], rd=[r_scr["GG"]], wr=[r_GG[b]])
                        em.dma(sp, gq[b][:], GQs[:, t0:t0 + 256], rd=[r_scr["GQ"]], wr=[r_gq[b]])
                        em.dma(sp, gk[b][:], GKs[:, t0:t0 + 256], rd=[r_scr["GK"]], wr=[r_gk[b]])
                        em.dma(sp, Ktg[b][:], GKts[t0:t0 + 256, :].rearrange("(c p) f -> p c f", p=64), rd=[r_scr["GKt"]], wr=[r_Kt[b]])
                        for ci in range(4):
                            em.op(pe, lambda: T_.matmul(ps[0:64, ci // 2, (ci % 2) * 256:(ci % 2 + 1) * 256], GGt[b][0:33, ci * 64:(ci + 1) * 64], Wg[0:33, :],
                                                        start=True, stop=True),
                                  rd=[r_GG[b], r_Wg], wr=[psr[ci // 2]], inc=(ci % 2 == 1))
                        em.op(act, lambda: S_.activation(SP[:].rearrange("p (b c) f -> p b (c f)", b=2), ps[0:64, 0:2, :], AF.Exp, scale=-1.0),
                              rd=[psr[0], psr[1]], wr=[r_SP])
                        em.op(act, lambda: S_.activation(SP[:].rearrange("p c f -> p (c f)"), SP[:].rearrange("p c f -> p (c f)"), AF.Ln, bias=one_c[0:64, 0:1], scale=1.0),
                              rd=[r_one], wr=[r_SP])
                        em.op(pool, lambda: G_.tensor_copy(SPh[:], SP[:]), rd=[r_SP], wr=[r_SPh])
                        em.op(dve, lambda: V_.tensor_tensor(SPl[:], SP[:], SPh[:], ALU.subtract), rd=[r_SP, r_SPh], wr=[r_SPl])
                        for ci in range(4):
                            for hl, (SPx, rSPx) in enumerate(((SPh, r_SPh), (SPl, r_SPl))):
                                em.op(pe, lambda: T_.matmul(ps[:, 2, ci * 64:(ci + 1) * 64], SPx[:, ci, 0:128], tri4b[:, 0, :], start=(hl == 0), stop=(hl == 1)),
                                      rd=[rSPx, r_gcst], wr=[psr[2]], inc=False)
                            for hl, (SPx, rSPx) in enumerate(((SPh, r_SPh), (SPl, r_SPl))):
                                em.op(pe, lambda: T_.matmul(ps[:, 2, 256 + ci * 64:256 + (ci + 1) * 64], SPx[:, ci, 128:256], tri4b[:, 1, :], start=(hl == 0), stop=(hl == 1)),
                                      rd=[rSPx, r_gcst], wr=[psr[2]], inc=(ci == 3 and hl == 1))
                        for ci in range(4):
                            for hl, (SPx, rSPx) in enumerate(((SPh, r_SPh), (SPl, r_SPl))):
                                em.op(pe, lambda: T_.matmul(ps[0:64, 3, ci * 128:(ci + 1) * 128], tri4b[:, 2, :], SPx[:, ci, 0:128], start=(hl == 0), stop=(hl == 1)),
                                      rd=[rSPx, r_gcst], wr=[psr[3]], inc=False)
                            for hl, (SPx, rSPx) in enumerate(((SPh, r_SPh), (SPl, r_SPl))):
                                em.op(pe, lambda: T_.matmul(ps[0:64, 4, ci * 128:(ci + 1) * 128], tri4b[:, 3, :], SPx[:, ci, 128:256], start=(hl == 0), stop=(hl == 1)),
                                      rd=[rSPx, r_gcst], wr=[psr[4]], inc=(ci == 3 and hl == 1))
                        flatEQ = EQ[:].rearrange("p d n -> p (d n)")
                        flatEK = EK[:].rearrange("p d n -> p (d n)")
                        em.op(act, lambda: S_.activation(flatEQ, ps[:, 2, :], AF.Exp), rd=[psr[2]], wr=[r_EQ])
                        em.op(act, lambda: S_.activation(flatEK, ps[:, 2, :], AF.Exp, scale=-1.0), rd=[psr[2]], wr=[r_EK])
                        em.op(act, lambda: S_.activation(EE[:, 0, :], ps[0:64, 3, :], AF.Exp), rd=[psr[3]], wr=[r_EE])
                        em.op(act, lambda: S_.activation(EE[:, 1, :], ps[0:64, 4, :], AF.Exp), rd=[psr[4]], wr=[r_EE])
                        ts_ = slice(t0, t0 + 256)
                        em.op(dve, lambda: V_.scalar_tensor_tensor(qf[:, ts_], gq[b][:], qscale, EQ[:, 0, :], ALU.mult, ALU.mult), rd=[r_gq[b], r_EQ], wr=[r_qk])
                        em.op(dve, lambda: V_.scalar_tensor_tensor(qb[:, ts_], gq[b][:], qscale, EQ[:, 1, :], ALU.mult, ALU.mult), rd=[r_gq[b], r_EQ], wr=[r_qk])
                        em.op(dve, lambda: V_.tensor_tensor(kf[:, ts_], gk[b][:], EK[:, 0, :], ALU.mult), rd=[r_gk[b], r_EK], wr=[r_qk])
                        em.op(dve, lambda: V_.tensor_tensor(kb[:, ts_], gk[b][:], EK[:, 1, :], ALU.mult), rd=[r_gk[b], r_EK], wr=[r_qk])
                        em.op(pool, lambda: G_.tensor_copy(Dcol[:, 0, gi * 4:(gi + 1) * 4], EQ[:, 0, :].rearrange("p (c i) -> p c i", i=64)[:, :, 63]),
                              rd=[r_EQ], wr=[r_D])
                        em.op(pool, lambda: G_.tensor_copy(Dcol[:, 1, gi * 4:(gi + 1) * 4], EQ[:, 1, :].rearrange("p (c i) -> p c i", i=64)[:, :, 0]),
                              rd=[r_EQ], wr=[r_D])
                        em.op(pool, lambda: G_.tensor_tensor(Kh[0][:, gi * 4:(gi + 1) * 4, :], Ktg[b][:], EE[:, 0, :].rearrange("p (c f) -> p c f", c=4), ALU.mult),
                              rd=[r_Kt[b], r_EE], wr=[r_Kh])
                        em.op(pool, lambda: G_.tensor_tensor(Kh[1][:, gi * 4:(gi + 1) * 4, :], Ktg[b][:], EE[:, 1, :].rearrange("p (c f) -> p c f", c=4), ALU.mult),
                              rd=[r_Kt[b], r_EE], wr=[r_Kh])
                    em.barrier()
                if GLA_STOP == "G":
                    return
                with contextlib.ExitStack() as phC:
                    Sc = [[sb(phC, f"lSc{d}{i}", [128, 256]) for i in range(2)] for d in range(2)]
                    tm = [sb(phC, f"ltm{d}", [128, 256]) for d in range(2)]
                    r_Sc = [[Res(), Res()], [Res(), Res()]]
                    r_tm = [Res(), Res()]
                    order = [list(range(NCH)), [3, 2, 1, 0] + list(range(NCH - 1, 3, -1))]
                    for d in range(2):
                        em.op(dve, lambda: V_.memset(Sc[d][0][:], 0.0), wr=[r_Sc[d][0]])
                    for step in range(NCH):
                        for d in range(2):
                            c = order[d][step]
                            pb = 2 * d + (step % 2)
                            cur, nxt = step % 2, (step + 1) % 2
                            em.op(pe, lambda: T_.matmul(ps[:, pb, 0:256], Kh[d][:, c, :], Vv[:, c, :], start=True, stop=True),
                                  rd=[r_Kh, r_Vv], wr=[psr[pb]])
                            em.op(act, lambda: S_.copy(Sall[d][:, c, :], Sc[d][cur][:]), rd=[r_Sc[d][cur]], wr=[r_S[d]])
                            em.op(dve, lambda: V_.tensor_tensor(tm[d][:], ps[:, pb, 0:256], bmask, ALU.mult), rd=[psr[pb]], wr=[r_tm[d]])
                            em.op(dve, lambda: V_.scalar_tensor_tensor(Sc[d][nxt][:], Sc[d][cur][:], Dcol[:, d, c:c + 1], tm[d][:], ALU.mult, ALU.add),
                                  rd=[r_Sc[d][cur], r_tm[d], r_D], wr=[r_Sc[d][nxt]])
                    em.barrier()
            if GLA_STOP == "B1":
                return
            with contextlib.ExitStack() as phC:
                STm = [sb(phC, f"lST{i}", [64, 512], BF16) for i in range(2)]
                sq = sb(phC, "lsq", [64, 512])
                ss = sb(phC, "lss", [64, 8])
                tt = sb(phC, "ltt", [64, 512])
                glan = sb(phC, "lgl", [64, 2, 256])
                Rg = [sb(phC, f"lRg{i}", [64, 2, 256], BF16) for i in range(2)]
                cx = [sb(phC, f"lcx{i}", [64, 2, 256], BF16) for i in range(2)]
                cxT = [sb(phC, f"lcxT{i}", [128, 2, 128], BF16) for i in range(2)]
                identb = CB("ident", 64)[:, 0:64]
                r_ST, r_sq, r_ss, r_tt, r_gl, r_Rg, r_cx, r_cxT = [Res(), Res()], Res(), Res(), Res(), Res(), [Res(), Res()], [Res(), Res()], [Res(), Res()]
                for a in range(2):
                    em.dma(sp, glan[:, a, :], glan_in[l].partition_broadcast(64), wa=[r_gl])
                c0 = 4 if last else 0
                qT = [qf, qb]
                kT = [kf, kb]
                chunks = list(range(c0, NCH))

                def st_mask(c):
                    ci = c % 2
                    cs = slice(c * 64, (c + 1) * 64)
                    for d in range(2):
                        for h in range(4):
                            em.op(pe, lambda: T_.matmul(ps[0:64, h, d * 64:(d + 1) * 64], kT[d][32 * h:32 * h + 32, cs],
                                                        qT[d][32 * h:32 * h + 32, cs], start=True, stop=True, tile_position=(32 * h, 0)),
                                  rd=[r_qk], wr=[psr[h]], inc=(d == 1 and h == 3))
                    em.op(dve, lambda: V_.tensor_tensor(STm[ci][:].rearrange("p (h d i) -> p h d i", h=4, d=2),
                                                        ps[0:64, 0:4, 0:128].rearrange("p h (d i) -> p h d i", d=2),
                                                        trimask.rearrange("p (d h i) -> p h d i", d=2, h=4), ALU.mult),
                          rd=[psr[0], psr[1], psr[2], psr[3]], wr=[r_ST[ci]])

                def o_mm(c):
                    ci = c % 2
                    p = c // 2
                    pi = p - c0 // 2
                    ob = 4 + (pi % 2)
                    cs = slice(c * 64, (c + 1) * 64)
                    if ci == 0:
                        em.dma(sp, Rg[pi % 2][:], GRs[p * 128:(p + 1) * 128, :].rearrange("(c q) f -> q c f", q=64), rd=[r_scr["GR"]], wr=[r_Rg[pi % 2]])
                    oc = slice(ci * 256, (ci + 1) * 256)
                    em.op(pe, lambda: T_.matmul(ps[0:64, ob, oc], qf[:, cs], Sall[0][:, c, :], start=True, stop=False),
                          rd=[r_qk, r_S[0]], wr=[psr[ob]], inc=False)
                    em.op(pe, lambda: T_.matmul(ps[0:64, ob, oc], qb[:, cs], Sall[1][:, c, :], start=False, stop=False),
                          rd=[r_qk, r_S[1]], wr=[psr[ob]], inc=False)
                    for d in range(2):
                        for h in range(4):
                            fin = (d == 1 and h == 3)
                            em.op(pe, lambda: T_.matmul(ps[0:64, ob, ci * 256 + h * 64:ci * 256 + (h + 1) * 64],
                                                        STm[ci][:, h * 128 + d * 64:h * 128 + (d + 1) * 64], Vv[:, c, h * 64:(h + 1) * 64],
                                                        start=False, stop=fin),
                                  rd=[r_ST[ci], r_Vv], wr=[psr[ob]], inc=fin)

                def post(p):
                    pi = p - c0 // 2
                    ob = 4 + (pi % 2)
                    em.op(act, lambda: S_.activation(sq[:], ps[0:64, ob, :], AF.Square), rd=[psr[ob]], wr=[r_sq])
                    em.op(dve, lambda: V_.tensor_reduce(ss[:], sq[:].rearrange("p (g e) -> p g e", e=64), AX.X, ALU.add), rd=[r_sq], wr=[r_ss])
                    em.op(act, lambda: S_.activation(ss[:], ss[:], AF.Ln, bias=eps_c[0:64, 0:1], scale=1.0 / 64), wr=[r_ss])
                    em.op(act, lambda: S_.activation(ss[:], ss[:], AF.Exp, scale=-0.5), wr=[r_ss])
                    em.op(dve, lambda: V_.tensor_tensor(tt[:].rearrange("p (g e) -> p g e", e=64), ps[0:64, ob, :].rearrange("p (g e) -> p g e", e=64),
                                                        ss[:].unsqueeze(2).to_broadcast([64, 8, 64]), ALU.mult),
                          rd=[psr[ob], r_ss], wr=[r_tt])
                    em.op(pool, lambda: G_.tensor_tensor(tt[:], tt[:], glan[:].rearrange("p a f -> p (a f)"), ALU.mult), rd=[r_gl], wr=[r_tt])
                    cxp, rcx = cx[pi % 2], r_cx[pi % 2]
                    em.op(pool, lambda: G_.tensor_tensor(cxp[:].rearrange("p a f -> p (a f)"), tt[:], Rg[pi % 2][:].rearrange("p a f -> p (a f)"), ALU.mult),
                          rd=[r_tt, r_Rg[pi % 2]], wr=[rcx])

                def transp(p):
                    pi = p - c0 // 2
                    cxp, rcx = cx[pi % 2], r_cx[pi % 2]
                    tbk = 6 + (pi % 2)
                    pst = ps[:, tbk, :].bitcast(BF16)
                    for ci in range(2):
                        for fc in range(2):
                            em.op(pe, lambda: T_.transpose(pst[:, fc * 128 + ci * 64:fc * 128 + (ci + 1) * 64], cxp[:, ci, fc * 128:(fc + 1) * 128], identb),
                                  rd=[rcx], wr=[psr[tbk]], inc=(ci == 1 and fc == 1))
                    xo = cxT[pi % 2]
                    em.op(act, lambda: S_.copy(xo[:].rearrange("p a t -> p (a t)"), pst[:, 0:256]), rd=[psr[tbk]], wr=[r_cxT[pi % 2]])
                    em.dma(sp, MIXT[:, 4:6, p * 128:(p + 1) * 128], xo[:], rd=[r_cxT[pi % 2]], wa=[r_MIXT])

                todo_tr = []
                st_mask(chunks[0])
                for i, c in enumerate(chunks):
                    if i + 1 < len(chunks):
                        st_mask(chunks[i + 1])
                    o_mm(c)
                    if todo_tr and c % 2 == 0:
                        transp(todo_tr.pop(0))
                    if c % 2 == 1:
                        post(c // 2)
                        todo_tr.append(c // 2)
                while todo_tr:
                    transp(todo_tr.pop(0))
                em.barrier()

    if stop == "INIT":
        em.barrier()
        return
    phase_mod()
    if stop == "MOD":
        return
    phase_T()
    if stop == "T":
        return
    for l in range(DEPTH):
        last = (l == DEPTH - 1)
        phase_ffn(l, 1, False)
        if stop == f"F1_{l}":
            return
        phase_P(l)
        if stop == f"P_{l}":
            return
        r_MIXT.w, r_MIXT.rd = [], []
        sel = stop.split(":")[1].split(",") if (stop and ":" in stop and stop.startswith(f"MIX_{l}")) else ["gqa", "na", "fourier", "gla"]
        if "gla" in sel:
            mix_gla(l)
        if "fourier" in sel:
            (mix_fourier if OPT_FQ else mix_fourier_v1)(l)
        if "na" in sel:
            mix_na(l)
        with contextlib.ExitStack() as wst:
            pre2 = ffn_weights(wst, l, 2) if (OPT_PREFETCH and not (stop and stop.startswith(f"MIX_{l}"))) else None
            if "gqa" in sel:
                mix_gqa(l)
            if stop and stop.startswith(f"MIX_{l}"):
                return
            phase_O(l)
            if stop == f"O_{l}":
                return
            phase_ffn(l, 2, last, pre=pre2)


_CACHE = {}


def _prep_shared(inputs):
    f32 = lambda a: np.ascontiguousarray(np.asarray(a, dtype=np.float32))
    sh = dict(make_consts())
    sh["w_mod"] = f32(inputs["w_mod"])
    sh["bmod_c"] = f32(np.asarray(inputs["b_mod"]).reshape(DEPTH, 72, 128).transpose(0, 2, 1))
    for f_, nm in ((1, "ffn1"), (2, "ffn2")):
        sh[f"f{f_}w1"] = f32(inputs[f"{nm}_w1"])
        sh[f"f{f_}w3"] = f32(inputs[f"{nm}_w3"])
        sh[f"f{f_}w2"] = f32(inputs[f"{nm}_w2"])
    sh["w_in_r"] = f32(np.asarray(inputs["w_in"])[:, :, _colperm()])
    sh["w_out"] = f32(inputs["w_out"])
    t2 = lambda a: np.tile(np.asarray(a), (1, 2))
    sh["gcol"] = f32(np.stack([t2(inputs["na_q_norm"]), t2(inputs["na_k_norm"]), t2(inputs["gqa_q_norm"]), t2(inputs["gqa_k_norm"])], axis=2))
    sh["glan"] = f32(np.tile(np.asarray(inputs["gla_norm"]), (1, 4)))
    wg = np.zeros((DEPTH, 33, 256), np.float32)
    wg[:, 0:16, 0:128] = np.asarray(inputs["gla_w_gate_f"])
    wg[:, 16:32, 128:256] = np.asarray(inputs["gla_w_gate_b"])
    wg[:, 32, 0:128] = np.asarray(inputs["gla_b_gate_f"])
    wg[:, 32, 128:256] = np.asarray(inputs["gla_b_gate_b"])
    sh["wg"] = wg
    sh["braw"] = make_na_bias(np.asarray(inputs["na_rpb"], dtype=np.float32)).reshape(DEPTH, 128, 4 * 21 * 128)
    return sh


def _in_maps(inputs, n_cores=8):
    sh = _prep_shared(inputs)
    x = np.asarray(inputs["x"], dtype=np.float32)
    ctx = np.asarray(inputs["ctx"], dtype=np.float32)
    c = np.asarray(inputs["c"], dtype=np.float32)
    c_ctx = np.asarray(inputs["c_ctx"], dtype=np.float32)
    maps = []
    for b in range(n_cores):
        m = dict(sh)
        m["x"] = np.ascontiguousarray(x[b])
        m["ctx"] = np.ascontiguousarray(ctx[b])
        cc = np.stack([c[b].reshape(8, 128).T, c_ctx.reshape(8, 128).T], axis=2)
        m["cc"] = np.ascontiguousarray(cc.astype(np.float32))
        maps.append(m)
    return maps


def kernel(**inputs):
    make_consts()
    if "nc" not in _CACHE:
        _CACHE["nc"] = build()
    nc = _CACHE["nc"]
    maps = _in_maps(inputs)
    res = run_bass_kernel_spmd(nc, maps, core_ids=list(range(8)))
    return np.stack([np.asarray(r["out"], dtype=np.float32) for r in res.results], axis=0)
```

```python
import contextlib
import numpy as np
import concourse.bass as bass
import concourse.mybir as mybir
from concourse.bass_utils import run_bass_kernel_spmd

F32 = mybir.dt.float32
BF16 = mybir.dt.bfloat16
AF = mybir.ActivationFunctionType
ALU = mybir.AluOpType
AX = mybir.AxisListType

D = 1024
SEQ = 4096
LC = 256
T = SEQ + LC
DFF = 2816
NJ = DFF // 128
DEPTH = 2
EPS = 1e-6
NDS = 40
GLA_STOP = None
OPT_T4 = True
OPT_PHT = True
OPT_FQ = True
OPT_PREFETCH = False

O_F, O_NQ, O_NK, O_NV, O_GQ, O_GK, O_GV, O_GR, O_GF, O_GB, O_AQ, O_AK, O_AV = (
    0, 256, 512, 768, 1024, 1152, 1280, 1536, 1792, 1808, 1824, 2080, 2208)
FM_GROUPS = [("F", 0), ("F", 1), ("NQ", 0), ("NQ", 1), ("NK", 0), ("NK", 1), ("AQ", 0), ("AQ", 1),
             ("AK", 0), ("AK", 1), ("GQ", 0), ("GK", 0), ("GG", 0)]
NFM = 12 * 128 + 32
NTM = 1024


def _colperm():
    r = lambda a, n: list(range(a, a + n))
    cols = []
    cols += r(O_F, 256) + r(O_NQ, 256) + r(O_NK, 256) + r(O_AQ, 256)
    cols += r(O_AK, 64) * 2 + r(O_AK + 64, 64) * 2
    cols += r(O_GQ, 128) + r(O_GK, 128) + r(O_GF, 32)
    cols += r(O_NV, 256) + r(O_GR, 256)
    cols += r(O_GK, 128) + r(O_GV, 256) + r(O_AV, 128)
    assert len(cols) == NFM + NTM
    return np.array(cols)


CST_OFF = {}


def make_consts():
    f = np.float64
    ident = np.eye(128)
    ones = np.ones((128, 128))
    blk64 = np.kron(np.eye(2), np.ones((64, 64)))
    prot = np.zeros((128, 128))
    for hb in range(2):
        for i in range(32):
            prot[hb * 64 + i + 32, hb * 64 + i] = -1.0
            prot[hb * 64 + i, hb * 64 + i + 32] = 1.0
    a = np.arange(64)
    ang = 2 * np.pi * ((a[:, None] * a[None, :]) % 64) / 64
    C64, S64 = np.cos(ang), np.sin(ang)
    bd = np.zeros((128, 256))
    for gl in range(2):
        bd[gl * 64:(gl + 1) * 64, gl * 64:(gl + 1) * 64] = C64
        bd[gl * 64:(gl + 1) * 64, 128 + gl * 64:128 + (gl + 1) * 64] = -S64
    m1 = np.zeros((128, 128))
    m1[0:64, 0:64] = C64
    m1[64:128, 0:64] = S64
    m1[0:64, 64:128] = -S64
    m1[64:128, 64:128] = C64
    bmask = np.zeros((128, 256))
    for p in range(128):
        bmask[p, (p // 32) * 64:(p // 32 + 1) * 64] = 1.0
    trimask = np.zeros((128, 2, 4, 64))
    j = np.arange(64)[:, None]
    i = np.arange(64)[None, :]
    trimask[0:64, 0, :, :] = (j <= i)[:, None, :]
    trimask[0:64, 1, :, :] = (j >= i)[:, None, :]
    tri4 = np.zeros((128, 4, 64))
    tt = np.arange(64)[:, None]
    ii = np.arange(64)[None, :]
    tri4[0:64, 0] = (tt <= ii) * (-1.0 / 16)
    tri4[0:64, 1] = (tt >= ii) * (-1.0 / 16)
    tri4[0:64, 2] = (tt > ii) * (-1.0 / 16)
    tri4[0:64, 3] = (tt < ii) * (-1.0 / 16)
    parts = [("ident", ident), ("ones", ones), ("blk64", blk64), ("prot", prot), ("bd", bd), ("m1", m1),
             ("bmask", bmask), ("trimask", trimask.reshape(128, 512)), ("tri4", tri4.reshape(128, 256))]
    off = 0
    for n, arr in parts:
        CST_OFF[n] = (off, arr.shape[1])
        off += arr.shape[1]
    cst = np.concatenate([p[1] for p in parts] + [np.zeros((128, 2048 - off))], axis=1).astype(np.float32)
    n2 = np.arange(64)[:, None, None]
    k1 = np.arange(64)[None, :, None]
    k2 = np.arange(64)[None, None, :]
    th = 2 * np.pi * ((n2 * (64 * k2 + k1)) % 4096) / 4096
    gtab = np.concatenate([np.cos(th), np.sin(th)], axis=0) / 512.0
    gtab = gtab.reshape(128, 4096).astype(np.float32)
    nl = np.arange(128)[:, None, None]
    bk = np.arange(2)[None, :, None]
    kk = np.arange(256)[None, None, :]
    thz = 2 * np.pi * (((bk * 128 + nl) * kk) % 256) / 256
    czsz = np.concatenate([np.cos(thz) / 128.0, np.sin(thz) / 128.0], axis=1).reshape(128, 1024).astype(np.float32)
    t = np.arange(SEQ)
    row = (t // 64).astype(np.float32)
    col = (t % 64).astype(np.float32)
    inv = (np.float32(10000.0) ** (-np.arange(16, dtype=np.float32) / np.float32(16))).astype(np.float32)
    angr = np.concatenate([row[:, None] * inv[None, :], col[:, None] * inv[None, :]], axis=1).astype(np.float32)
    cosf = np.cos(angr).astype(np.float32).T
    sinf = np.sin(angr).astype(np.float32).T
    cosT = np.tile(cosf, (4, 1)).astype(np.float32)
    sinT = np.tile(sinf, (4, 1)).astype(np.float32)
    return dict(cst=cst, gtab=gtab, czsz=czsz, cosT=np.ascontiguousarray(cosT), sinT=np.ascontiguousarray(sinT))


def make_na_bias(rpb):
    L = rpb.shape[0]
    out = np.full((L, 128, 4, 21, 128), -30000.0, dtype=np.float32)
    pats = [(10, 10 + d) for d in (-2, -1, 0, 1, 2)]
    for rq in (0, 1):
        pats += [(rq, kt) for kt in range(4)]
    for rq in (30, 31):
        pats += [(rq, kt) for kt in range(28, 32)]
    kr_l = np.arange(2)[:, None, None, None]
    kc = np.arange(64)[None, :, None, None]
    qr_l = np.arange(2)[None, None, :, None]
    qc = np.arange(64)[None, None, None, :]
    for pi, (rq, kt) in enumerate(pats):
        qr = 2 * rq + qr_l
        kr = 2 * kt + kr_l
        rs = np.clip(qr - 4, 0, 56)
        cs = np.clip(qc - 8, 0, 48)
        valid = (kr >= rs) & (kr < rs + 8) & (kc >= cs) & (kc < cs + 16)
        valid = np.broadcast_to(valid, (2, 64, 2, 64))
        ri = np.broadcast_to(np.clip(kr - qr + 7, 0, 14), (2, 64, 2, 64))
        ci = np.broadcast_to(np.clip(kc - qc + 15, 0, 30), (2, 64, 2, 64))
        for l in range(L):
            for h in range(4):
                g = rpb[l, h][ri, ci]
                tile = np.where(valid, g, np.float32(-30000.0)).astype(np.float32)
                out[l, :, h, pi, :] = tile.reshape(128, 128)
    return out


class Res:
    __slots__ = ("w", "rd", "name")

    def __init__(self, name=""):
        self.w = []
        self.rd = []
        self.name = name


class Eng:
    def __init__(self, name, h, sem):
        self.name, self.h, self.sem, self.cnt, self.known = name, h, sem, 0, {}


class Emit:
    def __init__(self, nc, es):
        self.nc = nc
        mk = lambda n: es.enter_context(nc.semaphore(n))
        self.pe = Eng("pe", nc.tensor, mk("s_pe"))
        self.act = Eng("act", nc.scalar, mk("s_act"))
        self.dve = Eng("dve", nc.vector, mk("s_dve"))
        self.pool = Eng("pool", nc.gpsimd, mk("s_pool"))
        self.sp = Eng("sp", nc.sync, mk("s_sp"))
        self.engs = [self.pe, self.act, self.dve, self.pool, self.sp]
        self.dsem = [mk(f"s_d{i}") for i in range(NDS)]
        self.dtot = [0] * NDS
        self.dnext = 0
        self.n_ops = 0

    def _wait(self, eng, evs, same_ok=True):
        for sem, val in evs:
            if same_ok and sem is eng.sem:
                continue
            if eng.known.get(id(sem), 0) < val:
                eng.h.wait_ge(sem, val)
                eng.known[id(sem)] = val

    @staticmethod
    def _deps(rd, wr, wa):
        evs = []
        for r in rd:
            evs += r.w
        for r in wr:
            evs += r.w
            evs += r.rd
        for r in wa:
            evs += r.rd
        return evs

    @staticmethod
    def _addrd(r, ev):
        r.rd = [e for e in r.rd if not (e[0] is ev[0] and e[1] <= ev[1])]
        r.rd.append(ev)

    def op(self, eng, fn, rd=(), wr=(), inc=True):
        self._wait(eng, self._deps(rd, wr, ()), same_ok=(eng is self.pe))
        ins = fn()
        self.n_ops += 1
        if inc:
            eng.cnt += 1
            ins.then_inc(eng.sem, 1)
            ev = (eng.sem, eng.cnt)
        else:
            ev = (eng.sem, eng.cnt + 1)
        for r in rd:
            self._addrd(r, ev)
        for r in wr:
            r.w = [ev]
            r.rd = []
        return ins

    def dma(self, q, out, in_, rd=(), wr=(), wa=()):
        self._wait(q, self._deps(rd, wr, wa), same_ok=False)
        i = self.dnext
        self.dnext = (self.dnext + 1) % NDS
        sem = self.dsem[i]
        if q.known.get(id(sem), 0) < self.dtot[i]:
            q.h.wait_ge(sem, self.dtot[i])
            q.known[id(sem)] = self.dtot[i]
        self.dtot[i] += 16
        q.h.dma_start(out=out, in_=in_).then_inc(sem, 16)
        self.n_ops += 1
        ev = (sem, self.dtot[i])
        for r in rd:
            r.rd.append(ev)
        for r in wr:
            r.w = [ev]
            r.rd = []
        for r in wa:
            r.w.append(ev)
        return ev

    def barrier(self):
        evs = [(e.sem, e.cnt) for e in self.engs if e.cnt > 0]
        evs += [(self.dsem[i], self.dtot[i]) for i in range(NDS) if self.dtot[i] > 0]
        for e in self.engs:
            self._wait(e, evs)


def tiles_all():
    return [(0, 256, 1)] + [(256 + 512 * i, 512, 0) for i in range(8)]


def build(dbg=(), stop=None):
    nc = bass.Bass("TRN2", target_bir_lowering=False)
    es = contextlib.ExitStack()
    with es:
        _build(nc, es, set(dbg), stop)
    return nc


def _build(nc, es, dbg, stop):
    em = Emit(nc, es)
    pe, act, dve, pool, sp = em.pe, em.act, em.dve, em.pool, em.sp
    T_, V_, S_, G_ = nc.tensor, nc.vector, nc.scalar, nc.gpsimd

    def din(name, shape, dt=F32):
        return nc.dram_tensor(name, list(shape), dt, kind="ExternalInput").ap()

    def dscr(name, shape, dt):
        kind = "ExternalOutput" if name in dbg else "Internal"
        return nc.dram_tensor(name, list(shape), dt, kind=kind).ap()

    x_in = din("x", [SEQ, D])
    ctx_in = din("ctx", [LC, D])
    cc_in = din("cc", [128, 8, 2])
    w_mod = din("w_mod", [DEPTH, D, 9 * D])
    bmod_in = din("bmod_c", [DEPTH, 128, 72])
    fw = {}
    for f_ in (1, 2):
        fw[f_] = (din(f"f{f_}w1", [DEPTH, D, DFF]), din(f"f{f_}w3", [DEPTH, D, DFF]), din(f"f{f_}w2", [DEPTH, DFF, D]))
    w_in_r = din("w_in_r", [DEPTH, D, NFM + NTM])
    w_out = din("w_out", [DEPTH, D, D])
    gcol_in = din("gcol", [DEPTH, 128, 4])
    glan_in = din("glan", [DEPTH, 256])
    wg_in = din("wg", [DEPTH, 33, 256])
    braw_in = din("braw", [DEPTH, 128, 4 * 21 * 128])
    cst_in = din("cst", [128, 2048])
    gtab_in = din("gtab", [128, 4096])
    czsz_in = din("czsz", [128, 1024])
    cosT_in = din("cosT", [128, SEQ])
    sinT_in = din("sinT", [128, SEQ])
    out_d = nc.dram_tensor("out", [SEQ, D], F32, kind="ExternalOutput").ap()

    XS = dscr("XS", [128, 8, T], F32)
    MIXT = dscr("MIXT", [128, 8, T], BF16)
    VSX = dscr("VSX", [SEQ, 512], BF16)
    AS_ = dscr("AS", [128, 64, 256], BF16)
    NQs = dscr("NQs", [128, 2, T], BF16)
    NKs = dscr("NKs", [128, 2, T], BF16)
    NVs = dscr("NVs", [T, 256], BF16)
    AQs = dscr("AQs", [128, 2, T], BF16)
    AKs = dscr("AKs", [128, 2, T], BF16)
    AVs = dscr("AVs", [T, 128], BF16)
    GQs = dscr("GQs", [128, T], F32)
    GKs = dscr("GKs", [128, T], F32)
    GGs = dscr("GGs", [32, T], F32)
    GKts = dscr("GKts", [T, 128], BF16)
    GVs = dscr("GVs", [T, 256], BF16)
    GRs = dscr("GRs", [T, 256], BF16)
    r_XS = [Res(f"XS{i}") for i in range(9)]
    r_MIXT = Res("MIXT")
    r_scr = {n: Res(n) for n in ("VSX", "AS", "NQ", "NK", "NV", "AQ", "AK", "AV", "GQ", "GK", "GG", "GKt", "GV", "GR")}

    uid = [0]

    def sb(st, name, shape, dt=F32):
        uid[0] += 1
        return st.enter_context(nc.sbuf_tensor(f"sb{uid[0]}_{name}", list(shape), dt))

    ps = es.enter_context(nc.psum_tensor("ps", [128, 8, 512], F32))
    psr = [Res(f"ps{i}") for i in range(8)]
    cst = sb(es, "cst", [128, 256], F32)
    cstb = sb(es, "cstb", [128, 896], BF16)
    modv = sb(es, "modv", [128, DEPTH, 72, 2], F32)
    r_cst = Res("cst")
    r_mod = Res("mod")

    def C(name, rows=128):
        o, n = CST_OFF[name]
        return cst[0:rows, o:o + n]

    def CB(name, rows=128):
        o, n = CST_OFF[name]
        return cstb[0:rows, o:o + n]

    em.dma(sp, cst[:], cst_in[:, 0:256], wr=[r_cst])
    em.dma(pool, cstb[:], cst_in[:, 0:896], wa=[r_cst])
    ident = C("ident")
    ones_b = CB("ones")
    blk64_b = CB("blk64")
    prot_b = CB("prot")
    bd_b = CB("bd")
    m1_b = CB("m1")

    def mcol(l, i, k, s):
        return modv[:, l, i * 8 + k, s:s + 1]

    def phase_mod():
        with contextlib.ExitStack() as ph:
            ccs = sb(ph, "ccs", [128, 8, 2])
            bm = sb(ph, "bm", [128, DEPTH, 72])
            wm = [sb(ph, f"wm{i}", [128, 8, 512]) for i in range(3)]
            rows = sb(ph, "mrows", [2, 9 * D])
            r_cc, r_bm, r_rows = Res(), Res(), Res()
            r_wm = [Res(), Res(), Res()]
            em.dma(sp, ccs[:], cc_in, wr=[r_cc])
            em.dma(sp, bm[:], bmod_in.rearrange("l p j -> p l j"), wr=[r_bm])
            em.op(act, lambda: S_.activation(ccs[:], ccs[:], AF.Silu), wr=[r_cc])
            ns = 0
            for l in range(DEPTH):
                for slab in range(18):
                    b = ns % 3
                    ns += 1
                    src = w_mod[l, :, slab * 512:(slab + 1) * 512].rearrange("(k p) n -> p k n", p=128)
                    em.dma(sp, wm[b][:], src, wr=[r_wm[b]])
                    pb = slab % 2
                    for k in range(8):
                        em.op(pe, lambda: T_.matmul(ps[0:2, pb, :], ccs[:, k, :], wm[b][:, k, :], start=(k == 0), stop=(k == 7)),
                              rd=[r_wm[b], r_cc], wr=[psr[pb]], inc=(k == 7))
                    em.op(act, lambda: S_.copy(rows[:, slab * 512:(slab + 1) * 512], ps[0:2, pb, :]), rd=[psr[pb]], wr=[r_rows])
                for j in range(72):
                    em.op(pe, lambda: T_.transpose(ps[:, 2 + l, 2 * j:2 * j + 2], rows[0:2, j * 128:(j + 1) * 128], ident[0:2, 0:2]),
                          rd=[r_rows, r_cst], wr=[psr[2 + l]], inc=(j == 71))
                em.op(dve, lambda: V_.tensor_tensor(modv[:, l, :, :], ps[:, 2 + l, 0:144].rearrange("p (j s) -> p j s", s=2),
                                                    bm[:, l, :].unsqueeze(2).to_broadcast([128, 72, 2]), ALU.add),
                      rd=[psr[2 + l], r_bm], wr=[r_mod])
                for i in (1, 4, 7):
                    em.op(dve, lambda: V_.tensor_scalar(modv[:, l, i * 8:(i + 1) * 8, :], modv[:, l, i * 8:(i + 1) * 8, :],
                                                        1.0, None, ALU.add), wr=[r_mod])
                for i in (2, 8):
                    em.op(dve, lambda: V_.tensor_scalar(modv[:, l, i * 8:(i + 1) * 8, :], modv[:, l, i * 8:(i + 1) * 8, :],
                                                        0.5, None, ALU.mult), wr=[r_mod])
            if "modv_o" in dbg:
                mo = nc.dram_tensor("modv_o", [128, DEPTH * 144], F32, kind="ExternalOutput").ap()
                em.dma(sp, mo, modv[:].rearrange("p l j s -> p (l j s)"), rd=[r_mod])
            em.barrier()

    def phase_T():
        NBUF = 4 if OPT_T4 else 2
        with contextlib.ExitStack() as ph:
            xin = [sb(ph, f"xin{i}", [128, D]) for i in range(NBUF)]
            xtb = [sb(ph, f"xtb{i}", [128, 8, 128]) for i in range(NBUF)]
            r_xin = [Res() for _ in range(NBUF)]
            r_xtb = [Res() for _ in range(NBUF)]

            def load(blk):
                b = blk % NBUF
                src = ctx_in[blk * 128:(blk + 1) * 128, :] if blk < 2 else x_in[(blk - 2) * 128:(blk - 1) * 128, :]
                em.dma(sp, xin[b][:], src, wr=[r_xin[b]])

            for blk in range(NBUF - 1):
                load(blk)
            for blk in range(34):
                b = blk % NBUF
                if blk + NBUF - 1 < 34:
                    load(blk + NBUF - 1)
                for k in range(8):
                    bank = b * 2 + k // 4
                    em.op(pe, lambda: T_.transpose(ps[:, bank, (k % 4) * 128:(k % 4 + 1) * 128], xin[b][:, k * 128:(k + 1) * 128], ident),
                          rd=[r_xin[b], r_cst], wr=[psr[bank]], inc=(k % 4 == 3))
                em.op(act, lambda: S_.copy(xtb[b][:, 0:4, :], ps[:, b * 2, :].rearrange("p (k t) -> p k t", k=4)),
                      rd=[psr[b * 2]], wr=[r_xtb[b]])
                em.op(dve, lambda: V_.tensor_copy(xtb[b][:, 4:8, :], ps[:, b * 2 + 1, :].rearrange("p (k t) -> p k t", k=4)),
                      rd=[psr[b * 2 + 1]], wr=[r_xtb[b]])
                ti = 0 if blk < 2 else 1 + (blk - 2) // 4
                em.dma(sp, XS[:, :, blk * 128:(blk + 1) * 128], xtb[b][:], rd=[r_xtb[b]], wa=[r_XS[ti]])
            em.barrier()

    def norm_mod(xT, r_x, hT, r_h, rstd, r_rstd, tmp, r_tmp, N, l, i0, s, ssbank, part="all"):
        if part in ("all", "sq"):
            for k in range(8):
                em.op(act, lambda: S_.activation(hT[:, k, :N], xT[:, k, :N], AF.Square), rd=[r_x], wr=[r_h])
        if part == "sq":
            return
        for k in range(8):
            em.op(pe, lambda: T_.matmul(ps[:, ssbank, :N], ones_b, hT[:, k, :N], start=(k == 0), stop=(k == 7)),
                  rd=[r_h, r_cst], wr=[psr[ssbank]], inc=(k == 7))
        em.op(act, lambda: S_.activation(rstd[:, :N], ps[:, ssbank, :N], AF.Ln, bias=eps_c[:, 0:1], scale=1.0 / D),
              rd=[psr[ssbank]], wr=[r_rstd])
        em.op(act, lambda: S_.activation(rstd[:, :N], rstd[:, :N], AF.Exp, scale=-0.5), wr=[r_rstd])
        for k in range(8):
            tb = tmp[k % 2]
            em.op(dve, lambda: V_.tensor_tensor(tb[:, :N], xT[:, k, :N], rstd[:, :N], ALU.mult),
                  rd=[r_x, r_rstd], wr=[r_tmp[k % 2]])
            em.op(act, lambda: S_.activation(hT[:, k, :N], tb[:, :N], AF.Identity, bias=mcol(l, i0, k, s), scale=mcol(l, i0 + 1, k, s)),
                  rd=[r_tmp[k % 2], r_mod], wr=[r_h])

    eps_c = sb(es, "eps_c", [128, 1])
    em.op(dve, lambda: V_.memset(eps_c[:], EPS), wr=[r_cst])

    def ffn_weights(st, l, which):
        W1, W3, W2 = fw[which]
        w1 = sb(st, "w1", [128, 8, DFF], BF16)
        w3 = sb(st, "w3", [128, 8, DFF], BF16)
        w2 = sb(st, "w2", [128, NJ, D], BF16)
        r_w1, r_w3, r_w2 = [Res(), Res()], [Res(), Res()], [Res(), Res()]
        H = DFF // 2
        for hh in range(2):
            em.dma(pool, w1[:, :, hh * H:(hh + 1) * H], W1[l, :, hh * H:(hh + 1) * H].rearrange("(k p) n -> p k n", p=128), wr=[r_w1[hh]])
            em.dma(pool, w3[:, :, hh * H:(hh + 1) * H], W3[l, :, hh * H:(hh + 1) * H].rearrange("(k p) n -> p k n", p=128), wr=[r_w3[hh]])
        for hh in range(2):
            em.dma(pool, w2[:, hh * 11:(hh + 1) * 11, :], W2[l, hh * H:(hh + 1) * H, :].rearrange("(j p) n -> p j n", p=128), wr=[r_w2[hh]])
        return w1, w3, w2, r_w1, r_w3, r_w2

    def phase_ffn(l, which, final, pre=None):
        i0 = 0 if which == 1 else 6
        tl = tiles_all()
        if final:
            tl = tl[1:]
        with contextlib.ExitStack() as ph:
            w1, w3, w2, r_w1, r_w3, r_w2 = pre if pre is not None else ffn_weights(ph, l, which)
            xT = [sb(ph, f"xT{i}", [128, 8, 512]) for i in range(2)]
            gT = sb(ph, "gT", [128, NJ, 512], BF16)
            hT = sb(ph, "hT", [128, 8, 512], BF16)
            rstd = sb(ph, "rstd", [128, 512])
            tmp = [sb(ph, f"tmp{i}", [128, 512]) for i in range(2)]
            r_x = [Res(), Res()]
            r_g, r_h, r_rstd = Res(), Res(), Res()
            r_tmp = [Res(), Res()]
            if final:
                ot = sb(ph, "ot", [128, 512])
                r_ot = Res()

            def load(ti):
                t0, N, s = tl[ti]
                gi = ti if not final else ti + 1
                em.dma(sp, xT[ti % 2][:, :, :N], XS[:, :, t0:t0 + N], rd=[r_XS[gi]], wr=[r_x[ti % 2]])

            load(0)
            norm_mod(xT[0], r_x[0], hT, r_h, rstd, r_rstd, tmp, r_tmp, tl[0][1], l, i0, tl[0][2], 6)
            for ti, (t0, N, s) in enumerate(tl):
                b = ti % 2
                gi = ti if not final else ti + 1
                if ti + 1 < len(tl):
                    load(ti + 1)
                for j in range(NJ):
                    hh = j // 11
                    pu1, pu3 = (j % 2), 2 + (j % 2)
                    for k in range(8):
                        em.op(pe, lambda: T_.matmul(ps[:, pu1, :N], w1[:, k, j * 128:(j + 1) * 128], hT[:, k, :N], start=(k == 0), stop=(k == 7)),
                              rd=[r_w1[hh], r_h], wr=[psr[pu1]], inc=(k == 7))
                    for k in range(8):
                        em.op(pe, lambda: T_.matmul(ps[:, pu3, :N], w3[:, k, j * 128:(j + 1) * 128], hT[:, k, :N], start=(k == 0), stop=(k == 7)),
                              rd=[r_w3[hh], r_h], wr=[psr[pu3]], inc=(k == 7))
                    tb = tmp[j % 2]
                    em.op(act, lambda: S_.activation(tb[:, :N], ps[:, pu1, :N], AF.Silu), rd=[psr[pu1]], wr=[r_tmp[j % 2]])
                    em.op(dve, lambda: V_.tensor_tensor(gT[:, j, :N], tb[:, :N], ps[:, pu3, :N], ALU.mult),
                          rd=[r_tmp[j % 2], psr[pu3]], wr=[r_g])
                for m in range(8):
                    if m == 2 and ti + 1 < len(tl):
                        nb_ = (ti + 1) % 2
                        norm_mod(xT[nb_], r_x[nb_], hT, r_h, rstd, r_rstd, tmp, r_tmp, tl[ti + 1][1], l, i0, tl[ti + 1][2], 6)
                    py = 4 + (m % 2)
                    for j in range(NJ):
                        em.op(pe, lambda: T_.matmul(ps[:, py, :N], w2[:, j, m * 128:(m + 1) * 128], gT[:, j, :N], start=(j == 0), stop=(j == NJ - 1)),
                              rd=[r_w2[j // 11], r_g], wr=[psr[py]], inc=(j == NJ - 1))
                    em.op(dve, lambda: V_.scalar_tensor_tensor(xT[b][:, m, :N], ps[:, py, :N], mcol(l, i0 + 2, m, s), xT[b][:, m, :N], ALU.mult, ALU.add),
                          rd=[psr[py], r_mod], wr=[r_x[b]])
                if not final:
                    em.dma(sp, XS[:, :, t0:t0 + N], xT[b][:, :, :N], rd=[r_x[b]], wr=[r_XS[gi]])
                else:
                    for tb_ in range(N // 128):
                        for half in range(2):
                            for mm in range(4):
                                m = half * 4 + mm
                                em.op(pe, lambda: T_.transpose(ps[:, 7, mm * 128:(mm + 1) * 128], xT[b][:, m, tb_ * 128:(tb_ + 1) * 128], ident),
                                      rd=[r_x[b]], wr=[psr[7]], inc=(mm == 3))
                            em.op(act, lambda: S_.copy(ot[:], ps[:, 7, :]), rd=[psr[7]], wr=[r_ot])
                            r0 = t0 - LC + tb_ * 128
                            em.dma(sp, out_d[r0:r0 + 128, half * 512:(half + 1) * 512], ot[:], rd=[r_ot])
            em.barrier()

    def phase_P(l):
        last = (l == DEPTH - 1)
        with contextlib.ExitStack() as ph:
            wfm = sb(ph, "wfm", [128, 8, NFM], BF16)
            wtm = sb(ph, "wtm", [128, 8, NTM], BF16)
            xT = [sb(ph, f"pxT{i}", [128, 8, 512]) for i in range(2)]
            hTs = [sb(ph, f"phT{i}", [128, 8, 512], BF16) for i in range(2)]
            r_hs = [Res(), Res()]
            rstd = sb(ph, "prstd", [128, 512])
            tmp = [sb(ph, f"ptmp{i}", [128, 512]) for i in range(2)]
            cosT = sb(ph, "cosT", [128, SEQ])
            sinT = sb(ph, "sinT", [128, SEQ])
            gcol = sb(ph, "gcol", [128, 4])
            czsz = sb(ph, "czsz", [128, 1024], BF16)
            hxf = sb(ph, "hxf", [128, 2, 512], BF16)
            vz = sb(ph, "vz", [128, 2, 512], BF16)
            NB = 10
            stb = [sb(ph, f"stb{i}", [128, 512], BF16) for i in range(NB)]
            stf = [sb(ph, f"stf{i}", [128, 512]) for i in range(NB)]
            r_stb = [Res() for _ in range(NB)]
            r_stf = [Res() for _ in range(NB)]
            cnt = {"b": 0, "f": 0}

            def nb():
                i = cnt["b"] % NB
                cnt["b"] += 1
                return stb[i], r_stb[i]

            def nf():
                i = cnt["f"] % NB
                cnt["f"] += 1
                return stf[i], r_stf[i]

            r_wfm, r_wtm, r_tab, r_x = Res(), Res(), Res(), [Res(), Res()]
            r_rstd, r_tmp, r_hxf, r_vz = Res(), [Res(), Res()], Res(), Res()
            em.dma(pool, wfm[:], w_in_r[l, :, 0:NFM].rearrange("(k p) n -> p k n", p=128), wr=[r_wfm])
            em.dma(pool, wtm[:], w_in_r[l, :, NFM:NFM + NTM].rearrange("(k p) n -> p k n", p=128), wr=[r_wtm])
            em.dma(sp, cosT[:], cosT_in, wr=[r_tab])
            em.dma(sp, sinT[:], sinT_in, wa=[r_tab])
            em.dma(sp, gcol[:], gcol_in[l], wa=[r_tab])
            em.dma(pool, czsz[:], czsz_in, wa=[r_tab])
            tl = tiles_all()

            def load(ti):
                t0, N, s = tl[ti]
                em.dma(sp, xT[ti % 2][:, :, :N], XS[:, :, t0:t0 + N], rd=[r_XS[ti]], wr=[r_x[ti % 2]])

            load(0)
            goff = {}
            o = 0
            for gi, (kind, c) in enumerate(FM_GROUPS):
                goff[gi] = o
                o += 32 if kind == "GG" else 128
            if OPT_PHT:
                norm_mod(xT[0], r_x[0], hTs[0], r_hs[0], rstd, r_rstd, tmp, r_tmp, tl[0][1], l, 3, tl[0][2], 0)
            for ti, (t0, N, s) in enumerate(tl):
                b = ti % 2
                hT, r_h = hTs[b], r_hs[b]
                if ti + 1 < len(tl):
                    load(ti + 1)
                if not OPT_PHT:
                    norm_mod(xT[b], r_x[b], hT, r_h, rstd, r_rstd, tmp, r_tmp, N, l, 3, s, 0)
                isx = (s == 0)
                xt0 = t0 - LC
                ACC = [1, 2, 5, 6]
                pend = []

                def stage_A(gi, kind, c, pb):
                    st8 = {}
                    if kind == "F":
                        em.op(act, lambda: S_.copy(hxf[:, c, :N], ps[:, pb, :N]), rd=[psr[pb]], wr=[r_hxf])
                        return None
                    if kind in ("GQ", "GK"):
                        st, rs = nf()
                        em.op(act, lambda: S_.copy(st[:, :N], ps[:, pb, :N]), rd=[psr[pb]], wr=[rs])
                        dst, rr = (GQs, r_scr["GQ"]) if kind == "GQ" else (GKs, r_scr["GK"])
                        em.dma(sp, dst[:, t0:t0 + N], st[:, :N], rd=[rs], wa=[rr])
                        return None
                    if kind == "GG":
                        st, rs = nf()
                        em.op(act, lambda: S_.copy(st[0:32, :N], ps[0:32, pb, :N]), rd=[psr[pb]], wr=[rs])
                        em.dma(sp, GGs[:, t0:t0 + N], st[0:32, :N], rd=[rs], wa=[r_scr["GG"]])
                        return None
                    sq, rsq = nb()
                    em.op(act, lambda: S_.activation(sq[:, :N], ps[:, pb, :N], AF.Square), rd=[psr[pb]], wr=[rsq])
                    st8.update(kind=kind, c=c, pb=pb, sq=sq, rsq=rsq)
                    return st8

                def stage_B(st8):
                    kind, c, pb, sq, rsq = st8["kind"], st8["c"], st8["pb"], st8["sq"], st8["rsq"]
                    gidx = {"NQ": 0, "NK": 1, "AQ": 2, "AK": 3}[kind]
                    em.op(pe, lambda: T_.matmul(ps[:, 3, :N], blk64_b, sq[:, :N], start=True, stop=True), rd=[rsq], wr=[psr[3]])
                    r64, rr64 = nf()
                    em.op(act, lambda: S_.activation(r64[:, :N], ps[:, 3, :N], AF.Ln, bias=eps_c[:, 0:1], scale=1.0 / 64), rd=[psr[3]], wr=[rr64])
                    em.op(act, lambda: S_.activation(r64[:, :N], r64[:, :N], AF.Exp, scale=-0.5), wr=[rr64])
                    qn, rqn = nb()
                    em.op(dve, lambda: V_.scalar_tensor_tensor(qn[:, :N], ps[:, pb, :N], gcol[:, gidx:gidx + 1], r64[:, :N], ALU.mult, ALU.mult),
                          rd=[psr[pb], rr64, r_tab], wr=[rqn])
                    st8.update(qn=qn, rqn=rqn)

                def stage_C(st8):
                    kind, c, qn, rqn = st8["kind"], st8["c"], st8["qn"], st8["rqn"]
                    res, rres = qn, rqn
                    if kind in ("AQ", "AK") and isx:
                        em.op(pe, lambda: T_.matmul(ps[:, 7, :N], prot_b, qn[:, :N], start=True, stop=True), rd=[rqn], wr=[psr[7]])
                        t1, rt1 = nf()
                        t2, rt2 = nf()
                        em.op(pool, lambda: G_.tensor_tensor(t1[:, :N], qn[:, :N], cosT[:, xt0:xt0 + N], ALU.mult), rd=[rqn, r_tab], wr=[rt1])
                        em.op(dve, lambda: V_.tensor_tensor(t2[:, :N], ps[:, 7, :N], sinT[:, xt0:xt0 + N], ALU.mult), rd=[psr[7], r_tab], wr=[rt2])
                        res, rres = nb()
                        em.op(pool, lambda: G_.tensor_tensor(res[:, :N], t1[:, :N], t2[:, :N], ALU.add), rd=[rt1, rt2], wr=[rres])
                    dst, rr = {"NQ": (NQs, r_scr["NQ"]), "NK": (NKs, r_scr["NK"]), "AQ": (AQs, r_scr["AQ"]), "AK": (AKs, r_scr["AK"])}[kind]
                    em.dma(sp, dst[:, c, t0:t0 + N], res[:, :N], rd=[rres], wa=[rr])

                def drain(keep):
                    while pend and len(pend) > keep:
                        st8 = pend[0]
                        if st8["stage"] == 1:
                            stage_C(st8)
                            pend.pop(0)
                        else:
                            break
                    for st8 in pend:
                        if st8["stage"] == 0 and st8["age"] >= 1:
                            stage_B(st8)
                            st8["stage"] = 1

                for gi, (kind, c) in enumerate(FM_GROUPS):
                    M = 32 if kind == "GG" else 128
                    pb = ACC[gi % 4]
                    for k in range(8):
                        em.op(pe, lambda: T_.matmul(ps[0:M, pb, :N], wfm[:, k, goff[gi]:goff[gi] + M], hT[:, k, :N], start=(k == 0), stop=(k == 7)),
                              rd=[r_wfm, r_h], wr=[psr[pb]], inc=(k == 7))
                    for st8 in list(pend):
                        st8["age"] += 1
                    for st8 in list(pend):
                        if st8["stage"] == 1 and st8["age"] >= 2:
                            stage_C(st8)
                            pend.remove(st8)
                    for st8 in pend:
                        if st8["stage"] == 0 and st8["age"] >= 1:
                            stage_B(st8)
                            st8["stage"] = 1
                    st8 = stage_A(gi, kind, c, pb)
                    if st8 is not None:
                        st8["stage"], st8["age"] = 0, 0
                        pend.append(st8)
                    if OPT_PHT and gi == 5 and ti + 1 < len(tl):
                        nb_ = (ti + 1) % 2
                        norm_mod(xT[nb_], r_x[nb_], hTs[nb_], r_hs[nb_], rstd, r_rstd, tmp, r_tmp, tl[ti + 1][1], l, 3, tl[ti + 1][2], 0, part="sq")
                    if OPT_PHT and gi == 9 and ti + 1 < len(tl):
                        nb_ = (ti + 1) % 2
                        norm_mod(xT[nb_], r_x[nb_], hTs[nb_], r_hs[nb_], rstd, r_rstd, tmp, r_tmp, tl[ti + 1][1], l, 3, tl[ti + 1][2], 0, part="rest")
                for st8 in list(pend):
                    if st8["stage"] == 0:
                        stage_B(st8)
                        st8["stage"] = 1
                for st8 in list(pend):
                    stage_C(st8)
                pend.clear()
                for tb_ in range(N // 128):
                    for c in range(2):
                        em.op(pe, lambda: T_.matmul(ps[:, 4, :].rearrange("p (r c f) -> p r c f", r=2, c=2)[:, :, c, :],
                                                    hxf[:, c, tb_ * 128:(tb_ + 1) * 128], bd_b.rearrange("p (r f) -> p r f", r=2),
                                                    start=True, stop=True),
                              rd=[r_hxf], wr=[psr[4]], inc=(c == 1))
                    if isx:
                        st, rs = nb()
                        em.op(act, lambda: S_.copy(st[:], ps[:, 4, :]), rd=[psr[4]], wr=[rs])
                        r0 = xt0 + tb_ * 128
                        em.dma(sp, VSX[r0:r0 + 128, :], st[:], rd=[rs], wa=[r_scr["VSX"]])
                    elif not last:
                        em.op(act, lambda: S_.copy(vz[:, tb_, :], ps[:, 4, :]), rd=[psr[4]], wr=[r_vz])
                if (not isx) and (not last):
                    for fc in range(2):
                        n_ = 0
                        for tb_ in range(2):
                            for ri in range(2):
                                em.op(pe, lambda: T_.matmul(ps[:, 7, 0:256], vz[:, tb_, ri * 256 + fc * 128: ri * 256 + (fc + 1) * 128],
                                                            czsz[:, ri * 512 + tb_ * 256: ri * 512 + (tb_ + 1) * 256],
                                                            start=(n_ == 0), stop=(n_ == 3)),
                                      rd=[r_vz, r_tab], wr=[psr[7]], inc=(n_ == 3))
                                n_ += 1
                        st, rs = nb()
                        em.op(act, lambda: S_.copy(st[:, 0:256], ps[:, 7, 0:256]), rd=[psr[7]], wr=[rs])
                        em.dma(sp, MIXT[:, fc, 0:256], st[:, 0:256], rd=[rs], wa=[r_MIXT])
                for tb_ in range(N // 128):
                    r0 = t0 + tb_ * 128
                    for g in range(2):
                        for k in range(8):
                            em.op(pe, lambda: T_.matmul(ps[:, 5 + g, :], hT[:, k, tb_ * 128:(tb_ + 1) * 128], wtm[:, k, g * 512:(g + 1) * 512],
                                                        start=(k == 0), stop=(k == 7)),
                                  rd=[r_wtm, r_h], wr=[psr[5 + g]], inc=(k == 7))
                    st, rs = nb()
                    em.op(act, lambda: S_.copy(st[:, 0:256], ps[:, 5, 0:256]), rd=[psr[5]], wr=[rs])
                    em.op(act, lambda: S_.activation(st[:, 256:512], ps[:, 5, 256:512], AF.Silu), rd=[psr[5]], wr=[rs])
                    em.dma(sp, NVs[r0:r0 + 128, :], st[:, 0:256], rd=[rs], wa=[r_scr["NV"]])
                    em.dma(sp, GRs[r0:r0 + 128, :], st[:, 256:512], rd=[rs], wa=[r_scr["GR"]])
                    st2, rs2 = nb()
                    em.op(dve, lambda: V_.tensor_copy(st2[:], ps[:, 6, :]), rd=[psr[6]], wr=[rs2])
                    em.dma(sp, GKts[r0:r0 + 128, :], st2[:, 0:128], rd=[rs2], wa=[r_scr["GKt"]])
                    em.dma(sp, GVs[r0:r0 + 128, :], st2[:, 128:384], rd=[rs2], wa=[r_scr["GV"]])
                    em.dma(sp, AVs[r0:r0 + 128, :], st2[:, 384:512], rd=[rs2], wa=[r_scr["AV"]])
            em.barrier()

    def phase_O(l):
        last = (l == DEPTH - 1)
        tl = tiles_all()
        with contextlib.ExitStack() as ph:
            wo = sb(ph, "wo", [128, 8, D], BF16)
            xT = [sb(ph, f"oxT{i}", [128, 8, 512]) for i in range(2)]
            mT = [sb(ph, f"omT{i}", [128, 8, 512], BF16) for i in range(2)]
            r_wo, r_x, r_m = Res(), [Res(), Res()], [Res(), Res()]
            em.dma(pool, wo[:], w_out[l].rearrange("(k p) n -> p k n", p=128), wr=[r_wo])
            idx = list(range(1, 9)) if last else list(range(9))

            def load(n):
                ti = idx[n]
                t0, N, s = tl[ti]
                em.dma(sp, xT[n % 2][:, :, :N], XS[:, :, t0:t0 + N], rd=[r_XS[ti]], wr=[r_x[n % 2]])
                em.dma(sp, mT[n % 2][:, :, :N], MIXT[:, :, t0:t0 + N], rd=[r_MIXT], wr=[r_m[n % 2]])

            load(0)
            for n, ti in enumerate(idx):
                t0, N, s = tl[ti]
                b = n % 2
                if n + 1 < len(idx):
                    load(n + 1)
                for m in range(8):
                    pb = m % 2
                    for k in range(8):
                        em.op(pe, lambda: T_.matmul(ps[:, pb, :N], wo[:, k, m * 128:(m + 1) * 128], mT[b][:, k, :N], start=(k == 0), stop=(k == 7)),
                              rd=[r_wo, r_m[b]], wr=[psr[pb]], inc=(k == 7))
                    em.op(dve, lambda: V_.scalar_tensor_tensor(xT[b][:, m, :N], ps[:, pb, :N], mcol(l, 5, m, s), xT[b][:, m, :N], ALU.mult, ALU.add),
                          rd=[psr[pb], r_mod], wr=[r_x[b]])
                em.dma(sp, XS[:, :, t0:t0 + N], xT[b][:, :, :N], rd=[r_x[b]], wr=[r_XS[ti]])
            em.barrier()

    def mix_gqa(l):
        last = (l == DEPTH - 1)
        with contextlib.ExitStack() as ph:
            AK = sb(ph, "gAK", [128, 2, T], BF16)
            V1 = sb(ph, "gV1", [128, 34, 2, 65], BF16)
            Q = [sb(ph, f"gQ{i}", [128, 512], BF16) for i in range(2)]
            PT = [sb(ph, f"gPT{i}", [128, 1024], BF16) for i in range(4)]
            rden = [sb(ph, f"grden{i}", [128, 512]) for i in range(2)]
            On = [sb(ph, f"gOn{i}", [64, 512]) for i in range(2)]
            osb = [sb(ph, f"gosb{i}", [64, 512], BF16) for i in range(2)]
            r_AK, r_V1, r_Q, r_PT = Res(), Res(), [Res(), Res()], [Res(), Res(), Res(), Res()]
            r_rden, r_On, r_osb = [Res(), Res()], [Res(), Res()], [Res(), Res()]
            em.dma(sp, AK[:], AKs, rd=[r_scr["AK"]], wr=[r_AK])
            em.op(pool, lambda: G_.memset(V1[:, :, :, 64:65], 1.0), wr=[r_V1])
            for q4 in range(2):
                a, b_ = q4 * 17, q4 * 17 + 17
                for j_ in range(2):
                    em.dma(sp, V1[:, a:b_, j_, 0:64], AVs[a * 128:b_ * 128, j_ * 64:(j_ + 1) * 64].rearrange("(t p) d -> p t d", p=128),
                           rd=[r_scr["AV"]], wa=[r_V1])
            qtiles = [(256 + 512 * i, 512, list(range(34))) for i in range(8)]
            if not last:
                qtiles = [(0, 256, [0, 1])] + qtiles
            steps = []
            n = 0
            for j in range(2):
                for (t0, N, keys) in qtiles:
                    for ki, kt in enumerate(keys):
                        steps.append((n, j, t0, N, ki, kt, len(keys)))
                    n += 1
            cur = {}

            SB = [0, 2, 6]

            def qk_ex(si):
                n_, j, t0, N, ki, kt, nk = steps[si]
                Qt, rq = Q[n_ % 2], r_Q[n_ % 2]
                if ki == 0:
                    em.dma(sp, Qt[:, :N], AQs[:, j, t0:t0 + N], rd=[r_scr["AQ"]], wr=[rq])
                sbk = SB[si % 3]
                for g in range(2):
                    em.op(pe, lambda: T_.matmul(ps[:, sbk + g, :N], AK[64 * g:64 * g + 64, j, kt * 128:(kt + 1) * 128],
                                                Qt[64 * g:64 * g + 64, :N], start=True, stop=True, tile_position=(64 * g, 0)),
                          rd=[r_AK, rq], wr=[psr[sbk + g]])
                pt, rpt = PT[si % 4], r_PT[si % 4]
                em.op(act, lambda: S_.activation(pt[:, 0:2 * N].rearrange("p (g n) -> p g n", g=2), ps[:, sbk:sbk + 2, :N], AF.Exp, scale=0.125),
                      rd=[psr[sbk], psr[sbk + 1]], wr=[rpt])

            def pv(si):
                n_, j, t0, N, ki, kt, nk = steps[si]
                pt, rpt = PT[si % 4], r_PT[si % 4]
                for g in range(2):
                    em.op(pe, lambda: T_.matmul(ps[0:65, 4 + g, :N], V1[:, kt, j, :], pt[:, g * N:(g + 1) * N],
                                                start=(ki == 0), stop=(ki == nk - 1)),
                          rd=[r_V1, rpt], wr=[psr[4 + g]], inc=(ki == nk - 1))
                if ki == nk - 1:
                    for g in range(2):
                        ob, rob = osb[g], r_osb[g]
                        em.op(dve, lambda: V_.reciprocal(rden[g][64:65, :N], ps[64:65, 4 + g, :N]), rd=[psr[4 + g]], wr=[r_rden[g]])
                        em.op(act, lambda: S_.copy(On[g][:, :N], ps[0:64, 4 + g, :N]), rd=[psr[4 + g]], wr=[r_On[g]])
                    for g in range(2):
                        ob, rob = osb[g], r_osb[g]
                        em.op(pe, lambda: T_.matmul(ps[0:64, 4 + g, :N], C("ones")[64:65, 0:64], rden[g][64:65, :N], start=True, stop=True, tile_position=(64, 0)),
                              rd=[r_rden[g]], wr=[psr[4 + g]])
                        em.op(dve, lambda: V_.tensor_tensor(ob[:, :N], On[g][:, :N], ps[0:64, 4 + g, :N], ALU.mult), rd=[r_On[g], psr[4 + g]], wr=[rob])
                        em.dma(sp, MIXT[64 * g:64 * g + 64, 6 + j, t0:t0 + N], ob[:, :N], rd=[rob], wa=[r_MIXT])

            LA = 2
            for si in range(min(LA, len(steps))):
                qk_ex(si)
            for si in range(len(steps)):
                if si + LA < len(steps):
                    qk_ex(si + LA)
                pv(si)
            em.barrier()

    def mix_na(l):
        last = (l == DEPTH - 1)
        with contextlib.ExitStack() as ph:
            NK = sb(ph, "nNK", [128, 2, T], BF16)
            NQ = sb(ph, "nNQ", [128, 2, T], BF16)
            V1 = sb(ph, "nV1", [128, 34, 4, 65], BF16)
            E = sb(ph, "nE", [128, 4, 21 * 128], BF16)
            PT = [sb(ph, f"nPT{i}", [128, 896], BF16) for i in range(3)]
            rden = [sb(ph, f"nrden{i}", [128, 512]) for i in range(2)]
            On = [sb(ph, f"nOn{i}", [64, 512]) for i in range(2)]
            osb = [sb(ph, f"nosb{i}", [64, 512], BF16) for i in range(2)]
            r_NK, r_NQ, r_V1, r_E, r_PT = Res(), Res(), Res(), Res(), [Res(), Res(), Res()]
            r_rden, r_On, r_osb = [Res(), Res()], [Res(), Res()], [Res(), Res()]
            em.dma(sp, NK[:], NKs, rd=[r_scr["NK"]], wr=[r_NK])
            em.dma(sp, NQ[:], NQs, rd=[r_scr["NQ"]], wr=[r_NQ])
            em.op(pool, lambda: G_.memset(V1[:, :, :, 64:65], 1.0), wr=[r_V1])
            for q4 in range(2):
                a, b_ = q4 * 17, q4 * 17 + 17
                for h_ in range(4):
                    em.dma(sp, V1[:, a:b_, h_, 0:64], NVs[a * 128:b_ * 128, h_ * 64:(h_ + 1) * 64].rearrange("(t p) d -> p t d", p=128),
                           rd=[r_scr["NV"]], wa=[r_V1])
            with contextlib.ExitStack() as ph2:
                stg = [sb(ph2, f"nstg{i}", [128, 21 * 128]) for i in range(2)]
                r_stg = [Res(), Res()]
                for h in range(4):
                    em.dma(sp, stg[h % 2][:], braw_in[l, :, h * 2688:(h + 1) * 2688], wr=[r_stg[h % 2]])
                    em.op(act, lambda: S_.activation(E[:, h, :], stg[h % 2][:], AF.Exp), rd=[r_stg[h % 2]], wr=[r_E])
                em.barrier()
            qts = []
            if not last:
                qts += [(0, [], 0, [0, 1]), (1, [], 0, [0, 1])]
            for rq in range(32):
                if 2 <= rq <= 29:
                    win, p0 = list(range(rq - 2, rq + 3)), 0
                elif rq < 2:
                    win, p0 = [0, 1, 2, 3], 5 + 4 * rq
                else:
                    win, p0 = [28, 29, 30, 31], 13 + 4 * (rq - 30)
                qts.append((2 + rq, [2 + w for w in win], p0, [0, 1]))
            steps = [(qn, h) for qn in range(len(qts)) for h in range(4)]

            def qk_ex(si):
                qn, h = steps[si]
                qt, wkeys, p0, ckeys = qts[qn]
                keys = wkeys + ckeys
                nk, nw = len(keys), len(wkeys)
                q0 = qt * 128
                c, g = h // 2, h % 2
                sbk = 2 * (si % 2)
                flat = ps[:, sbk:sbk + 2, :].rearrange("p b n -> p (b n)")
                for i, kt in enumerate(keys):
                    em.op(pe, lambda: T_.matmul(flat[:, i * 128:(i + 1) * 128], NK[64 * g:64 * g + 64, c, kt * 128:(kt + 1) * 128],
                                                NQ[64 * g:64 * g + 64, c, q0:q0 + 128], start=True, stop=True, tile_position=(64 * g, 0)),
                          rd=[r_NK, r_NQ], wr=[psr[sbk], psr[sbk + 1]], inc=(i == nk - 1))
                pt, rpt = PT[si % 3], r_PT[si % 3]
                em.op(act, lambda: S_.activation(pt[:, 0:nk * 128], flat[:, 0:nk * 128], AF.Exp, scale=0.125),
                      rd=[psr[sbk], psr[sbk + 1]], wr=[rpt])
                if nw:
                    em.op(dve, lambda: V_.tensor_tensor(pt[:, 0:nw * 128], pt[:, 0:nw * 128], E[:, h, p0 * 128:(p0 + nw) * 128], ALU.mult),
                          rd=[r_E], wr=[rpt])

            def pv(si):
                qn, h = steps[si]
                qt, wkeys, p0, ckeys = qts[qn]
                keys = wkeys + ckeys
                nk = len(keys)
                q0 = qt * 128
                ob_, bcb = 4 + (qn % 2), 6 + (qn % 2)
                pt, rpt = PT[si % 3], r_PT[si % 3]
                for i, kt in enumerate(keys):
                    em.op(pe, lambda: T_.matmul(ps[0:65, ob_, h * 128:(h + 1) * 128], V1[:, kt, h, :], pt[:, i * 128:(i + 1) * 128],
                                                start=(i == 0), stop=(i == nk - 1)),
                          rd=[r_V1, rpt], wr=[psr[ob_]], inc=(i == nk - 1))
                if h == 3:
                    ob, rob = osb[qn % 2], r_osb[qn % 2]
                    rd_, rrd = rden[qn % 2], r_rden[qn % 2]
                    on_, ron = On[qn % 2], r_On[qn % 2]
                    em.op(dve, lambda: V_.reciprocal(rd_[64:65, :], ps[64:65, ob_, :]), rd=[psr[ob_]], wr=[rrd])
                    em.op(pe, lambda: T_.matmul(ps[0:64, bcb, :], C("ones")[64:65, 0:64], rd_[64:65, :], start=True, stop=True, tile_position=(64, 0)),
                          rd=[rrd], wr=[psr[bcb]])
                    em.op(act, lambda: S_.copy(on_[:, :], ps[0:64, ob_, :]), rd=[psr[ob_]], wr=[ron])
                    em.op(dve, lambda: V_.tensor_tensor(ob[:, :], on_[:, :], ps[0:64, bcb, :], ALU.mult), rd=[ron, psr[bcb]], wr=[rob])
                    for hh in range(4):
                        c, g = hh // 2, hh % 2
                        em.dma(sp, MIXT[64 * g:64 * g + 64, 2 + c, q0:q0 + 128], ob[:, hh * 128:(hh + 1) * 128], rd=[rob], wa=[r_MIXT])

            qk_ex(0)
            for si in range(len(steps)):
                if si + 1 < len(steps):
                    qk_ex(si + 1)
                pv(si)
            em.barrier()

    def mix_fourier(l):
        with contextlib.ExitStack() as ph:
            Vst = sb(ph, "fVst", [128, 64, 256], BF16)
            Bst = sb(ph, "fBst", [128, 64, 256], BF16)
            Asb = sb(ph, "fAsb", [128, 64, 256], BF16)
            gtb = sb(ph, "fgtb", [128, 64, 64], BF16)
            Yt = sb(ph, "fYt", [128, 2, SEQ], BF16)
            r_Vq = [Res() for _ in range(4)]
            r_Aq = [Res() for _ in range(4)]
            r_Bq = [Res() for _ in range(4)]
            r_g, r_Y = Res(), Res()
            em.dma(pool, gtb[:], gtab_in.rearrange("p (a b) -> p a b", a=64), wr=[r_g])
            vsx4 = VSX.rearrange("(a b) (r f) -> a b r f", b=64, r=2)
            for q in range(4):
                for ri in range(2):
                    em.dma(sp, Vst[ri * 64:(ri + 1) * 64, 16 * q:16 * q + 16, :], vsx4[:, 16 * q:16 * q + 16, ri, :], rd=[r_scr["VSX"]], wa=[r_Vq[q]])
            for q in range(4):
                for cg in range(8 * q, 8 * q + 8):
                    pb = cg % 2
                    em.op(pe, lambda: T_.matmul(ps[:, pb, :], m1_b, Vst[:, 2 * cg:2 * cg + 2, :].rearrange("p a f -> p (a f)"), start=True, stop=True),
                          rd=[r_Vq[q]], wr=[psr[pb]])
                    dst = Asb[:, 2 * cg:2 * cg + 2, :].rearrange("p a f -> p (a f)")
                    if cg % 2 == 0:
                        em.op(act, lambda: S_.copy(dst, ps[:, pb, :]), rd=[psr[pb]], wr=[r_Aq[q]])
                    else:
                        em.op(dve, lambda: V_.tensor_copy(dst, ps[:, pb, :]), rd=[psr[pb]], wr=[r_Aq[q]])
                em.dma(sp, AS_[:, 16 * q:16 * q + 16, :], Asb[:, 16 * q:16 * q + 16, :], rd=[r_Aq[q]], wa=[r_scr["AS"]])
            for kq in range(4):
                for ri in range(2):
                    em.dma(sp, Bst[ri * 64:(ri + 1) * 64, 16 * kq:16 * kq + 16, :],
                           AS_[ri * 64 + 16 * kq:ri * 64 + 16 * kq + 16, :, :].rearrange("k n f -> n k f"),
                           rd=[r_scr["AS"]], wa=[r_Bq[kq]])
            n_ev = 0
            for kq in range(4):
                for c in range(2):
                    ytv = Yt[:, c, :].rearrange("p (k2 k1) -> p k2 k1", k1=64)
                    for kg in (2 * kq, 2 * kq + 1):
                        pb = 2 + (n_ev % 2)
                        for kk in range(8):
                            k1 = kg * 8 + kk
                            em.op(pe, lambda: T_.matmul(ps[:, pb, kk * 64:(kk + 1) * 64], Bst[:, k1, c * 128:(c + 1) * 128], gtb[:, k1, :], start=True, stop=True),
                                  rd=[r_Bq[kq], r_g], wr=[psr[pb]], inc=(kk == 7))
                        src = ps[:, pb, :].rearrange("p (kk k2) -> p k2 kk", kk=8)
                        if n_ev % 2 == 0:
                            em.op(act, lambda: S_.copy(ytv[:, :, kg * 8:(kg + 1) * 8], src), rd=[psr[pb]], wr=[r_Y])
                        else:
                            em.op(dve, lambda: V_.tensor_copy(ytv[:, :, kg * 8:(kg + 1) * 8], src), rd=[psr[pb]], wr=[r_Y])
                        n_ev += 1
            em.dma(sp, MIXT[:, 0:2, LC:T], Yt[:], rd=[r_Y], wa=[r_MIXT])
            em.barrier()

    def mix_fourier_v1(l):
        with contextlib.ExitStack() as ph:
            Vst = sb(ph, "fVst", [128, 64, 256], BF16)
            Asb = sb(ph, "fAsb", [128, 64, 256], BF16)
            gtb = sb(ph, "fgtb", [128, 64, 64], BF16)
            Yt = sb(ph, "fYt", [128, 2, SEQ], BF16)
            r_V, r_A, r_g, r_Y = Res(), Res(), Res(), Res()
            em.dma(pool, gtb[:], gtab_in.rearrange("p (a b) -> p a b", a=64), wr=[r_g])
            vsx4 = VSX.rearrange("(a b) (r f) -> a b r f", b=64, r=2)
            for ri in range(2):
                em.dma(sp, Vst[ri * 64:(ri + 1) * 64, :, :], vsx4[:, :, ri, :], rd=[r_scr["VSX"]], wa=[r_V])
            for cg in range(32):
                pb = cg % 2
                em.op(pe, lambda: T_.matmul(ps[:, pb, :], m1_b, Vst[:, 2 * cg:2 * cg + 2, :].rearrange("p a f -> p (a f)"), start=True, stop=True),
                      rd=[r_V], wr=[psr[pb]])
                dst = Asb[:, 2 * cg:2 * cg + 2, :].rearrange("p a f -> p (a f)")
                if cg % 2 == 0:
                    em.op(act, lambda: S_.copy(dst, ps[:, pb, :]), rd=[psr[pb]], wr=[r_A])
                else:
                    em.op(dve, lambda: V_.tensor_copy(dst, ps[:, pb, :]), rd=[psr[pb]], wr=[r_A])
            em.dma(sp, AS_, Asb[:], rd=[r_A], wr=[r_scr["AS"]])
            for ri in range(2):
                em.dma(sp, Vst[ri * 64:(ri + 1) * 64, :, :], AS_[ri * 64:(ri + 1) * 64, :, :].rearrange("k n f -> n k f"),
                       rd=[r_scr["AS"]], wr=([r_V] if ri == 0 else []), wa=([r_V] if ri == 1 else []))
            for c in range(2):
                ytv = Yt[:, c, :].rearrange("p (k2 k1) -> p k2 k1", k1=64)
                for kg in range(8):
                    pb = 2 + (kg % 2)
                    for kk in range(8):
                        k1 = kg * 8 + kk
                        em.op(pe, lambda: T_.matmul(ps[:, pb, kk * 64:(kk + 1) * 64], Vst[:, k1, c * 128:(c + 1) * 128], gtb[:, k1, :], start=True, stop=True),
                              rd=[r_V, r_g], wr=[psr[pb]], inc=(kk == 7))
                    src = ps[:, pb, :].rearrange("p (kk k2) -> p k2 kk", kk=8)
                    if kg % 2 == 0:
                        em.op(act, lambda: S_.copy(ytv[:, :, kg * 8:(kg + 1) * 8], src), rd=[psr[pb]], wr=[r_Y])
                    else:
                        em.op(dve, lambda: V_.tensor_copy(ytv[:, :, kg * 8:(kg + 1) * 8], src), rd=[psr[pb]], wr=[r_Y])
            em.dma(sp, MIXT[:, 0:2, LC:T], Yt[:], rd=[r_Y], wa=[r_MIXT])
            em.barrier()

    def mix_gla(l):
        last = (l == DEPTH - 1)
        NCH = T // 64
        qscale = 32.0 ** -0.5
        with contextlib.ExitStack() as phA:
            gcst = sb(phA, "lcst", [128, 1024])
            r_gcst = Res()
            em.dma(sp, gcst[:], cst_in[:, 896:1920], wr=[r_gcst])
            bmask = gcst[:, 0:256]
            trimask = gcst[0:64, 256:768]
            tri4 = gcst[0:64, 768:1024].rearrange("p (a i) -> p a i", a=4)
            qf = sb(phA, "lqf", [128, T], BF16)
            kf = sb(phA, "lkf", [128, T], BF16)
            qb = sb(phA, "lqb", [128, T], BF16)
            kb = sb(phA, "lkb", [128, T], BF16)
            Vv = sb(phA, "lVv", [64, NCH, 256], BF16)
            Sall = [sb(phA, f"lS{i}", [128, NCH, 256], BF16) for i in range(2)]
            Dcol = sb(phA, "lD", [128, 2, NCH])
            r_qk, r_Vv, r_S, r_D = Res(), Res(), [Res(), Res()], Res()
            for q4 in range(4):
                a, b_ = q4 * 17, q4 * 17 + 17
                em.dma(sp, Vv[:, a:b_, :], GVs[a * 64:b_ * 64, :].rearrange("(c p) f -> p c f", p=64), rd=[r_scr["GV"]], wa=[r_Vv])
            with contextlib.ExitStack() as phB:
                Kh = [sb(phB, f"lKh{i}", [64, NCH, 128], BF16) for i in range(2)]
                r_Kh = Res()
                with contextlib.ExitStack() as phC:
                    Wg = sb(phC, "lWg", [33, 256])
                    GGt = [sb(phC, f"lGG{i}", [33, 256]) for i in range(2)]
                    gq = [sb(phC, f"lgq{i}", [128, 256]) for i in range(2)]
                    gk = [sb(phC, f"lgk{i}", [128, 256]) for i in range(2)]
                    Ktg = [sb(phC, f"lKt{i}", [64, 4, 128], BF16) for i in range(2)]
                    SP = sb(phC, "lSP", [64, 4, 256])
                    SPh = sb(phC, "lSPh", [64, 4, 256], BF16)
                    SPl = sb(phC, "lSPl", [64, 4, 256], BF16)
                    tri4b = sb(phC, "ltri4b", [64, 4, 64], BF16)
                    r_SPh, r_SPl = Res(), Res()
                    EQ = sb(phC, "lEQ", [128, 2, 256])
                    EK = sb(phC, "lEK", [128, 2, 256])
                    EE = sb(phC, "lEE", [64, 2, 512])
                    one_c = sb(phC, "lone", [128, 1])
                    r_Wg, r_GG, r_gq, r_gk, r_Kt = Res(), [Res(), Res()], [Res(), Res()], [Res(), Res()], [Res(), Res()]
                    r_e1, r_SP, r_EQ, r_EK, r_EE, r_one = Res(), Res(), Res(), Res(), Res(), Res()
                    em.dma(sp, Wg[:], wg_in[l], wr=[r_Wg])
                    em.op(dve, lambda: V_.tensor_copy(tri4b[:], tri4), rd=[r_gcst], wr=[r_gcst])
                    em.op(dve, lambda: V_.memset(one_c[:], 1.0), wr=[r_one])
                    for i in range(2):
                        em.op(pool, lambda: G_.memset(GGt[i][:], 1.0), wr=[r_GG[i]])
                    for gi in range(NCH // 4):
                        b = gi % 2
                        t0 = gi * 256
                        em.dma(sp, GGt[b][0:32, :], GGs[:, t0:t0 + 256], rd=[r_scr["GG"]], wr=[r_GG[b]])
                        em.dma(sp, gq[b][:], GQs[:, t0:t0 + 256], rd=[r_scr["GQ"]], wr=[r_gq[b]])
                        em.dma(sp, gk[b][:], GKs[:, t0:t0 + 256], rd=[r_scr["GK"]], wr=[r_gk[b]])
                        em.dma(sp, Ktg[b][:], GKts[t0:t0 + 256, :].rearrange("(c p) f -> p c f", p=64), rd=[r_scr["GKt"]], wr=[r_Kt[b]])
                        for ci in range(4):
                            em.op(pe, lambda: T_.matmul(ps[0:64, ci // 2, (ci % 2) * 256:(ci % 2 + 1) * 256], GGt[b][0:33, ci * 64:(ci + 1) * 64], Wg[0:33, :],
                                                        start=True, stop=True),
                                  rd=[r_GG[b], r_Wg], wr=[psr[ci // 2]], inc=(ci % 2 == 1))
                        em.op(act, lambda: S_.activation(SP[:].rearrange("p (b c) f -> p b (c f)", b=2), ps[0:64, 0:2, :], AF.Exp, scale=-1.0),
                              rd=[psr[0], psr[1]], wr=[r_SP])
                        em.op(act, lambda: S_.activation(SP[:].rearrange("p c f -> p (c f)"), SP[:].rearrange("p c f -> p (c f)"), AF.Ln, bias=one_c[0:64, 0:1], scale=1.0),
                              rd=[r_one], wr=[r_SP])
                        em.op(pool, lambda: G_.tensor_copy(SPh[:], SP[:]), rd=[r_SP], wr=[r_SPh])
                        em.op(dve, lambda: V_.tensor_tensor(SPl[:], SP[:], SPh[:], ALU.subtract), rd=[r_SP, r_SPh], wr=[r_SPl])
                        for ci in range(4):
                            for hl, (SPx, rSPx) in enumerate(((SPh, r_SPh), (SPl, r_SPl))):
                                em.op(pe, lambda: T_.matmul(ps[:, 2, ci * 64:(ci + 1) * 64], SPx[:, ci, 0:128], tri4b[:, 0, :], start=(hl == 0), stop=(hl == 1)),
                                      rd=[rSPx, r_gcst], wr=[psr[2]], inc=False)
                            for hl, (SPx, rSPx) in enumerate(((SPh, r_SPh), (SPl, r_SPl))):
                                em.op(pe, lambda: T_.matmul(ps[:, 2, 256 + ci * 64:256 + (ci + 1) * 64], SPx[:, ci, 128:256], tri4b[:, 1, :], start=(hl == 0), stop=(hl == 1)),
                                      rd=[rSPx, r_gcst], wr=[psr[2]], inc=(ci == 3 and hl == 1))
                        for ci in range(4):
                            for hl, (SPx, rSPx) in enumerate(((SPh, r_SPh), (SPl, r_SPl))):
                                em.op(pe, lambda: T_.matmul(ps[0:64, 3, ci * 128:(ci + 1) * 128], tri4b[:, 2, :], SPx[:, ci, 0:128], start=(hl == 0), stop=(hl == 1)),
                                      rd=[rSPx, r_gcst], wr=[psr[3]], inc=False)
                            for hl, (SPx, rSPx) in enumerate(((SPh, r_SPh), (SPl, r_SPl))):
                                em.op(pe, lambda: T_.matmul(ps[0:64, 4, ci * 128:(ci + 1) * 128], tri4b[:, 3, :], SPx[:, ci, 128:256], start=(hl == 0), stop=(hl == 1)),
                                      rd=[rSPx, r_gcst], wr=[psr[4]], inc=(ci == 3 and hl == 1))
                        flatEQ = EQ[:].rearrange("p d n -> p (d n)")
                        flatEK = EK[:].rearrange("p d n -> p (d n)")
                        em.op(act, lambda: S_.activation(flatEQ, ps[:, 2, :], AF.Exp), rd=[psr[2]], wr=[r_EQ])
                        em.op(act, lambda: S_.activation(flatEK, ps[:, 2, :], AF.Exp, scale=-1.0), rd=[psr[2]], wr=[r_EK])
                        em.op(act, lambda: S_.activation(EE[:, 0, :], ps[0:64, 3, :], AF.Exp), rd=[psr[3]], wr=[r_EE])
                        em.op(act, lambda: S_.activation(EE[:, 1, :], ps[0:64, 4, :], AF.Exp), rd=[psr[4]], wr=[r_EE])
                        ts_ = slice(t0, t0 + 256)
                        em.op(dve, lambda: V_.scalar_tensor_tensor(qf[:, ts_], gq[b][:], qscale, EQ[:, 0, :], ALU.mult, ALU.mult), rd=[r_gq[b], r_EQ], wr=[r_qk])
                        em.op(dve, lambda: V_.scalar_tensor_tensor(qb[:, ts_], gq[b][:], qscale, EQ[:, 1, :], ALU.mult, ALU.mult), rd=[r_gq[b], r_EQ], wr=[r_qk])
                        em.op(dve, lambda: V_.tensor_tensor(kf[:, ts_], gk[b][:], EK[:, 0, :], ALU.mult), rd=[r_gk[b], r_EK], wr=[r_qk])
                        em.op(dve, lambda: V_.tensor_tensor(kb[:, ts_], gk[b][:], EK[:, 1, :], ALU.mult), rd=[r_gk[b], r_EK], wr=[r_qk])
                        em.op(pool, lambda: G_.tensor_copy(Dcol[:, 0, gi * 4:(gi + 1) * 4], EQ[:, 0, :].rearrange("p (c i) -> p c i", i=64)[:, :, 63]),
                              rd=[r_EQ], wr=[r_D])
                        em.op(pool, lambda: G_.tensor_copy(Dcol[:, 1, gi * 4:(gi + 1) * 4], EQ[:, 1, :].rearrange("p (c i) -> p c i", i=64)[:, :, 0]),
                              rd=[r_EQ], wr=[r_D])
                        em.op(pool, lambda: G_.tensor_tensor(Kh[0][:, gi * 4:(gi + 1) * 4, :], Ktg[b][:], EE[:, 0, :].rearrange("p (c f) -> p c f", c=4), ALU.mult),
                              rd=[r_Kt[b], r_EE], wr=[r_Kh])
                        em.op(pool, lambda: G_.tensor_tensor(Kh[1][:, gi * 4:(gi + 1) * 4, :], Ktg[b][:], EE[:, 1, :].rearrange("p (c f) -> p c f", c=4), ALU.mult),
                              rd=[r_Kt[b], r_EE], wr=[r_Kh])
                    em.barrier()
                if GLA_STOP == "G":
                    return
                with contextlib.ExitStack() as phC:
                    Sc = [[sb(phC, f"lSc{d}{i}", [128, 256]) for i in range(2)] for d in range(2)]
                    tm = [sb(phC, f"ltm{d}", [128, 256]) for d in range(2)]
                    r_Sc = [[Res(), Res()], [Res(), Res()]]
                    r_tm = [Res(), Res()]
                    order = [list(range(NCH)), [3, 2, 1, 0] + list(range(NCH - 1, 3, -1))]
                    for d in range(2):
                        em.op(dve, lambda: V_.memset(Sc[d][0][:], 0.0), wr=[r_Sc[d][0]])
                    for step in range(NCH):
                        for d in range(2):
                            c = order[d][step]
                            pb = 2 * d + (step % 2)
                            cur, nxt = step % 2, (step + 1) % 2
                            em.op(pe, lambda: T_.matmul(ps[:, pb, 0:256], Kh[d][:, c, :], Vv[:, c, :], start=True, stop=True),
                                  rd=[r_Kh, r_Vv], wr=[psr[pb]])
                            em.op(act, lambda: S_.copy(Sall[d][:, c, :], Sc[d][cur][:]), rd=[r_Sc[d][cur]], wr=[r_S[d]])
                            em.op(dve, lambda: V_.tensor_tensor(tm[d][:], ps[:, pb, 0:256], bmask, ALU.mult), rd=[psr[pb]], wr=[r_tm[d]])
                            em.op(dve, lambda: V_.scalar_tensor_tensor(Sc[d][nxt][:], Sc[d][cur][:], Dcol[:, d, c:c + 1], tm[d][:], ALU.mult, ALU.add),
                                  rd=[r_Sc[d][cur], r_tm[d], r_D], wr=[r_Sc[d][nxt]])
                    em.barrier()
            if GLA_STOP == "B1":
                return
            with contextlib.ExitStack() as phC:
                STm = [sb(phC, f"lST{i}", [64, 512], BF16) for i in range(2)]
                sq2 = [sb(phC, f"lsq{i}", [64, 512]) for i in range(2)]
                ss2 = [sb(phC, f"lss{i}", [64, 8]) for i in range(2)]
                tt2 = [sb(phC, f"ltt{i}", [64, 512]) for i in range(2)]
                r_sq2, r_ss2, r_tt2 = [Res(), Res()], [Res(), Res()], [Res(), Res()]
                glan = sb(phC, "lgl", [64, 2, 256])
                Rg = [sb(phC, f"lRg{i}", [64, 2, 256], BF16) for i in range(2)]
                cx = [sb(phC, f"lcx{i}", [64, 2, 256], BF16) for i in range(2)]
                cxT = [sb(phC, f"lcxT{i}", [128, 2, 128], BF16) for i in range(2)]
                identb = CB("ident", 64)[:, 0:64]
                r_ST, r_sq, r_ss, r_tt, r_gl, r_Rg, r_cx, r_cxT = [Res(), Res()], Res(), Res(), Res(), Res(), [Res(), Res()], [Res(), Res()], [Res(), Res()]
                for a in range(2):
                    em.dma(sp, glan[:, a, :], glan_in[l].partition_broadcast(64), wa=[r_gl])
                c0 = 4 if last else 0
                qT = [qf, qb]
                kT = [kf, kb]
                chunks = list(range(c0, NCH))

                def st_mask(c):
                    ci = c % 2
                    cs = slice(c * 64, (c + 1) * 64)
                    for d in range(2):
                        for h in range(4):
                            em.op(pe, lambda: T_.matmul(ps[0:64, h, d * 64:(d + 1) * 64], kT[d][32 * h:32 * h + 32, cs],
                                                        qT[d][32 * h:32 * h + 32, cs], start=True, stop=True, tile_position=(32 * h, 0)),
                                  rd=[r_qk], wr=[psr[h]], inc=(d == 1 and h == 3))
                    em.op(dve, lambda: V_.tensor_tensor(STm[ci][:].rearrange("p (h d i) -> p h d i", h=4, d=2),
                                                        ps[0:64, 0:4, 0:128].rearrange("p h (d i) -> p h d i", d=2),
                                                        trimask.rearrange("p (d h i) -> p h d i", d=2, h=4), ALU.mult),
                          rd=[psr[0], psr[1], psr[2], psr[3]], wr=[r_ST[ci]])

                def o_mm(c):
                    ci = c % 2
                    p = c // 2
                    pi = p - c0 // 2
                    ob = 4 + (pi % 2)
                    cs = slice(c * 64, (c + 1) * 64)
                    if ci == 0:
                        em.dma(sp, Rg[pi % 2][:], GRs[p * 128:(p + 1) * 128, :].rearrange("(c q) f -> q c f", q=64), rd=[r_scr["GR"]], wr=[r_Rg[pi % 2]])
                    oc = slice(ci * 256, (ci + 1) * 256)
                    em.op(pe, lambda: T_.matmul(ps[0:64, ob, oc], qf[:, cs], Sall[0][:, c, :], start=True, stop=False),
                          rd=[r_qk, r_S[0]], wr=[psr[ob]], inc=False)
                    em.op(pe, lambda: T_.matmul(ps[0:64, ob, oc], qb[:, cs], Sall[1][:, c, :], start=False, stop=False),
                          rd=[r_qk, r_S[1]], wr=[psr[ob]], inc=False)
                    for d in range(2):
                        for h in range(4):
                            fin = (d == 1 and h == 3)
                            em.op(pe, lambda: T_.matmul(ps[0:64, ob, ci * 256 + h * 64:ci * 256 + (h + 1) * 64],
                                                        STm[ci][:, h * 128 + d * 64:h * 128 + (d + 1) * 64], Vv[:, c, h * 64:(h + 1) * 64],
                                                        start=False, stop=fin),
                                  rd=[r_ST[ci], r_Vv], wr=[psr[ob]], inc=fin)

                def post(p):
                    pi = p - c0 // 2
                    ob = 4 + (pi % 2)
                    sq, ss, tt = sq2[pi % 2], ss2[pi % 2], tt2[pi % 2]
                    r_sq, r_ss, r_tt = r_sq2[pi % 2], r_ss2[pi % 2], r_tt2[pi % 2]
                    em.op(act, lambda: S_.activation(sq[:], ps[0:64, ob, :], AF.Square), rd=[psr[ob]], wr=[r_sq])
                    em.op(dve, lambda: V_.tensor_reduce(ss[:], sq[:].rearrange("p (g e) -> p g e", e=64), AX.X, ALU.add), rd=[r_sq], wr=[r_ss])
                    em.op(act, lambda: S_.activation(ss[:], ss[:], AF.Ln, bias=eps_c[0:64, 0:1], scale=1.0 / 64), wr=[r_ss])
                    em.op(act, lambda: S_.activation(ss[:], ss[:], AF.Exp, scale=-0.5), wr=[r_ss])
                    em.op(dve, lambda: V_.tensor_tensor(tt[:].rearrange("p (g e) -> p g e", e=64), ps[0:64, ob, :].rearrange("p (g e) -> p g e", e=64),
                                                        ss[:].unsqueeze(2).to_broadcast([64, 8, 64]), ALU.mult),
                          rd=[psr[ob], r_ss], wr=[r_tt])
                    em.op(pool, lambda: G_.tensor_tensor(tt[:], tt[:], glan[:].rearrange("p a f -> p (a f)"), ALU.mult), rd=[r_gl], wr=[r_tt])
                    cxp, rcx = cx[pi % 2], r_cx[pi % 2]
                    em.op(pool, lambda: G_.tensor_tensor(cxp[:].rearrange("p a f -> p (a f)"), tt[:], Rg[pi % 2][:].rearrange("p a f -> p (a f)"), ALU.mult),
                          rd=[r_tt, r_Rg[pi % 2]], wr=[rcx])

                def transp(p):
                    pi = p - c0 // 2
                    cxp, rcx = cx[pi % 2], r_cx[pi % 2]
                    tbk = 6 + (pi % 2)
                    pst = ps[:, tbk, :].bitcast(BF16)
                    for ci in range(2):
                        for fc in range(2):
                            em.op(pe, lambda: T_.transpose(pst[:, fc * 128 + ci * 64:fc * 128 + (ci + 1) * 64], cxp[:, ci, fc * 128:(fc + 1) * 128], identb),
                                  rd=[rcx], wr=[psr[tbk]], inc=(ci == 1 and fc == 1))
                    xo = cxT[pi % 2]
                    em.op(act, lambda: S_.copy(xo[:].rearrange("p a t -> p (a t)"), pst[:, 0:256]), rd=[psr[tbk]], wr=[r_cxT[pi % 2]])
                    em.dma(sp, MIXT[:, 4:6, p * 128:(p + 1) * 128], xo[:], rd=[r_cxT[pi % 2]], wa=[r_MIXT])

                todo_tr = []
                st_mask(chunks[0])
                for i, c in enumerate(chunks):
                    if i + 1 < len(chunks):
                        st_mask(chunks[i + 1])
                    o_mm(c)
                    if todo_tr and c % 2 == 0:
                        transp(todo_tr.pop(0))
                    if c % 2 == 1:
                        post(c // 2)
                        todo_tr.append(c // 2)
                while todo_tr:
                    transp(todo_tr.pop(0))
                em.barrier()

    if stop == "INIT":
        em.barrier()
        return
    phase_mod()
    if stop == "MOD":
        return
    phase_T()
    if stop == "T":
        return
    for l in range(DEPTH):
        last = (l == DEPTH - 1)
        phase_ffn(l, 1, False)
        if stop == f"F1_{l}":
            return
        phase_P(l)
        if stop == f"P_{l}":
            return
        r_MIXT.w, r_MIXT.rd = [], []
        sel = stop.split(":")[1].split(",") if (stop and ":" in stop and stop.startswith(f"MIX_{l}")) else ["gqa", "na", "fourier", "gla"]
        if "gla" in sel:
            mix_gla(l)
        if "fourier" in sel:
            (mix_fourier if OPT_FQ else mix_fourier_v1)(l)
        if "na" in sel:
            mix_na(l)
        with contextlib.ExitStack() as wst:
            pre2 = ffn_weights(wst, l, 2) if (OPT_PREFETCH and not (stop and stop.startswith(f"MIX_{l}"))) else None
            if "gqa" in sel:
                mix_gqa(l)
            if stop and stop.startswith(f"MIX_{l}"):
                return
            phase_O(l)
            if stop == f"O_{l}":
                return
            phase_ffn(l, 2, last, pre=pre2)


_CACHE = {}


def _prep_shared(inputs):
    f32 = lambda a: np.ascontiguousarray(np.asarray(a, dtype=np.float32))
    sh = dict(make_consts())
    sh["w_mod"] = f32(inputs["w_mod"])
    sh["bmod_c"] = f32(np.asarray(inputs["b_mod"]).reshape(DEPTH, 72, 128).transpose(0, 2, 1))
    for f_, nm in ((1, "ffn1"), (2, "ffn2")):
        sh[f"f{f_}w1"] = f32(inputs[f"{nm}_w1"])
        sh[f"f{f_}w3"] = f32(inputs[f"{nm}_w3"])
        sh[f"f{f_}w2"] = f32(inputs[f"{nm}_w2"])
    sh["w_in_r"] = f32(np.asarray(inputs["w_in"])[:, :, _colperm()])
    sh["w_out"] = f32(inputs["w_out"])
    t2 = lambda a: np.tile(np.asarray(a), (1, 2))
    sh["gcol"] = f32(np.stack([t2(inputs["na_q_norm"]), t2(inputs["na_k_norm"]), t2(inputs["gqa_q_norm"]), t2(inputs["gqa_k_norm"])], axis=2))
    sh["glan"] = f32(np.tile(np.asarray(inputs["gla_norm"]), (1, 4)))
    wg = np.zeros((DEPTH, 33, 256), np.float32)
    wg[:, 0:16, 0:128] = np.asarray(inputs["gla_w_gate_f"])
    wg[:, 16:32, 128:256] = np.asarray(inputs["gla_w_gate_b"])
    wg[:, 32, 0:128] = np.asarray(inputs["gla_b_gate_f"])
    wg[:, 32, 128:256] = np.asarray(inputs["gla_b_gate_b"])
    sh["wg"] = wg
    sh["braw"] = make_na_bias(np.asarray(inputs["na_rpb"], dtype=np.float32)).reshape(DEPTH, 128, 4 * 21 * 128)
    return sh


def _in_maps(inputs, n_cores=8):
    sh = _prep_shared(inputs)
    x = np.asarray(inputs["x"], dtype=np.float32)
    ctx = np.asarray(inputs["ctx"], dtype=np.float32)
    c = np.asarray(inputs["c"], dtype=np.float32)
    c_ctx = np.asarray(inputs["c_ctx"], dtype=np.float32)
    maps = []
    for b in range(n_cores):
        m = dict(sh)
        m["x"] = np.ascontiguousarray(x[b])
        m["ctx"] = np.ascontiguousarray(ctx[b])
        cc = np.stack([c[b].reshape(8, 128).T, c_ctx.reshape(8, 128).T], axis=2)
        m["cc"] = np.ascontiguousarray(cc.astype(np.float32))
        maps.append(m)
    return maps


def kernel(**inputs):
    make_consts()
    if "nc" not in _CACHE:
        _CACHE["nc"] = build()
    nc = _CACHE["nc"]
    maps = _in_maps(inputs)
    res = run_bass_kernel_spmd(nc, maps, core_ids=list(range(8)))
    return np.stack([np.asarray(r["out"], dtype=np.float32) for r in res.results], axis=0)
```

```python
import contextlib
import numpy as np
import concourse.bass as bass
import concourse.mybir as mybir
from concourse.bass_utils import run_bass_kernel_spmd

F32 = mybir.dt.float32
BF16 = mybir.dt.bfloat16
AF = mybir.ActivationFunctionType
ALU = mybir.AluOpType
AX = mybir.AxisListType

D = 1024
SEQ = 4096
LC = 256
T = SEQ + LC
DFF = 2816
NJ = DFF // 128
DEPTH = 2
EPS = 1e-6
NDS = 40
GLA_STOP = None
OPT_T4 = True
OPT_PHT = True
OPT_FQ = True
OPT_PREFETCH = False

O_F, O_NQ, O_NK, O_NV, O_GQ, O_GK, O_GV, O_GR, O_GF, O_GB, O_AQ, O_AK, O_AV = (
    0, 256, 512, 768, 1024, 1152, 1280, 1536, 1792, 1808, 1824, 2080, 2208)
FM_GROUPS = [("F", 0), ("F", 1), ("NQ", 0), ("NQ", 1), ("NK", 0), ("NK", 1), ("AQ", 0), ("AQ", 1),
             ("AK", 0), ("AK", 1), ("GQ", 0), ("GK", 0), ("GG", 0)]
NFM = 12 * 128 + 32
NTM = 1024


def _colperm():
    r = lambda a, n: list(range(a, a + n))
    cols = []
    cols += r(O_F, 256) + r(O_NQ, 256) + r(O_NK, 256) + r(O_AQ, 256)
    cols += r(O_AK, 64) * 2 + r(O_AK + 64, 64) * 2
    cols += r(O_GQ, 128) + r(O_GK, 128) + r(O_GF, 32)
    cols += r(O_NV, 256) + r(O_GR, 256)
    cols += r(O_GK, 128) + r(O_GV, 256) + r(O_AV, 128)
    assert len(cols) == NFM + NTM
    return np.array(cols)


CST_OFF = {}


def make_consts():
    f = np.float64
    ident = np.eye(128)
    ones = np.ones((128, 128))
    blk64 = np.kron(np.eye(2), np.ones((64, 64)))
    prot = np.zeros((128, 128))
    for hb in range(2):
        for i in range(32):
            prot[hb * 64 + i + 32, hb * 64 + i] = -1.0
            prot[hb * 64 + i, hb * 64 + i + 32] = 1.0
    a = np.arange(64)
    ang = 2 * np.pi * ((a[:, None] * a[None, :]) % 64) / 64
    C64, S64 = np.cos(ang), np.sin(ang)
    bd = np.zeros((128, 256))
    for gl in range(2):
        bd[gl * 64:(gl + 1) * 64, gl * 64:(gl + 1) * 64] = C64
        bd[gl * 64:(gl + 1) * 64, 128 + gl * 64:128 + (gl + 1) * 64] = -S64
    m1 = np.zeros((128, 128))
    m1[0:64, 0:64] = C64
    m1[64:128, 0:64] = S64
    m1[0:64, 64:128] = -S64
    m1[64:128, 64:128] = C64
    bmask = np.zeros((128, 256))
    for p in range(128):
        bmask[p, (p // 32) * 64:(p // 32 + 1) * 64] = 1.0
    trimask = np.zeros((128, 2, 4, 64))
    j = np.arange(64)[:, None]
    i = np.arange(64)[None, :]
    trimask[0:64, 0, :, :] = (j <= i)[:, None, :]
    trimask[0:64, 1, :, :] = (j >= i)[:, None, :]
    tri4 = np.zeros((128, 4, 64))
    tt = np.arange(64)[:, None]
    ii = np.arange(64)[None, :]
    tri4[0:64, 0] = (tt <= ii) * (-1.0 / 16)
    tri4[0:64, 1] = (tt >= ii) * (-1.0 / 16)
    tri4[0:64, 2] = (tt > ii) * (-1.0 / 16)
    tri4[0:64, 3] = (tt < ii) * (-1.0 / 16)
    parts = [("ident", ident), ("ones", ones), ("blk64", blk64), ("prot", prot), ("bd", bd), ("m1", m1),
             ("bmask", bmask), ("trimask", trimask.reshape(128, 512)), ("tri4", tri4.reshape(128, 256))]
    off = 0
    for n, arr in parts:
        CST_OFF[n] = (off, arr.shape[1])
        off += arr.shape[1]
    cst = np.concatenate([p[1] for p in parts] + [np.zeros((128, 2048 - off))], axis=1).astype(np.float32)
    n2 = np.arange(64)[:, None, None]
    k1 = np.arange(64)[None, :, None]
    k2 = np.arange(64)[None, None, :]
    th = 2 * np.pi * ((n2 * (64 * k2 + k1)) % 4096) / 4096
    gtab = np.concatenate([np.cos(th), np.sin(th)], axis=0) / 512.0
    gtab = gtab.reshape(128, 4096).astype(np.float32)
    nl = np.arange(128)[:, None, None]
    bk = np.arange(2)[None, :, None]
    kk = np.arange(256)[None, None, :]
    thz = 2 * np.pi * (((bk * 128 + nl) * kk) % 256) / 256
    czsz = np.concatenate([np.cos(thz) / 128.0, np.sin(thz) / 128.0], axis=1).reshape(128, 1024).astype(np.float32)
    t = np.arange(SEQ)
    row = (t // 64).astype(np.float32)
    col = (t % 64).astype(np.float32)
    inv = (np.float32(10000.0) ** (-np.arange(16, dtype=np.float32) / np.float32(16))).astype(np.float32)
    angr = np.concatenate([row[:, None] * inv[None, :], col[:, None] * inv[None, :]], axis=1).astype(np.float32)
    cosf = np.cos(angr).astype(np.float32).T
    sinf = np.sin(angr).astype(np.float32).T
    cosT = np.tile(cosf, (4, 1)).astype(np.float32)
    sinT = np.tile(sinf, (4, 1)).astype(np.float32)
    return dict(cst=cst, gtab=gtab, czsz=czsz, cosT=np.ascontiguousarray(cosT), sinT=np.ascontiguousarray(sinT))


def make_na_bias(rpb):
    L = rpb.shape[0]
    out = np.full((L, 128, 4, 21, 128), -30000.0, dtype=np.float32)
    pats = [(10, 10 + d) for d in (-2, -1, 0, 1, 2)]
    for rq in (0, 1):
        pats += [(rq, kt) for kt in range(4)]
    for rq in (30, 31):
        pats += [(rq, kt) for kt in range(28, 32)]
    kr_l = np.arange(2)[:, None, None, None]
    kc = np.arange(64)[None, :, None, None]
    qr_l = np.arange(2)[None, None, :, None]
    qc = np.arange(64)[None, None, None, :]
    for pi, (rq, kt) in enumerate(pats):
        qr = 2 * rq + qr_l
        kr = 2 * kt + kr_l
        rs = np.clip(qr - 4, 0, 56)
        cs = np.clip(qc - 8, 0, 48)
        valid = (kr >= rs) & (kr < rs + 8) & (kc >= cs) & (kc < cs + 16)
        valid = np.broadcast_to(valid, (2, 64, 2, 64))
        ri = np.broadcast_to(np.clip(kr - qr + 7, 0, 14), (2, 64, 2, 64))
        ci = np.broadcast_to(np.clip(kc - qc + 15, 0, 30), (2, 64, 2, 64))
        for l in range(L):
            for h in range(4):
                g = rpb[l, h][ri, ci]
                tile = np.where(valid, g, np.float32(-30000.0)).astype(np.float32)
                out[l, :, h, pi, :] = tile.reshape(128, 128)
    return out


class Res:
    __slots__ = ("w", "rd", "name")

    def __init__(self, name=""):
        self.w = []
        self.rd = []
        self.name = name


class Eng:
    def __init__(self, name, h, sem):
        self.name, self.h, self.sem, self.cnt, self.known = name, h, sem, 0, {}


class Emit:
    def __init__(self, nc, es):
        self.nc = nc
        mk = lambda n: es.enter_context(nc.semaphore(n))
        self.pe = Eng("pe", nc.tensor, mk("s_pe"))
        self.act = Eng("act", nc.scalar, mk("s_act"))
        self.dve = Eng("dve", nc.vector, mk("s_dve"))
        self.pool = Eng("pool", nc.gpsimd, mk("s_pool"))
        self.sp = Eng("sp", nc.sync, mk("s_sp"))
        self.engs = [self.pe, self.act, self.dve, self.pool, self.sp]
        self.dsem = [mk(f"s_d{i}") for i in range(NDS)]
        self.dtot = [0] * NDS
        self.dnext = 0
        self.n_ops = 0

    def _wait(self, eng, evs, same_ok=True):
        for sem, val in evs:
            if same_ok and sem is eng.sem:
                continue
            if eng.known.get(id(sem), 0) < val:
                eng.h.wait_ge(sem, val)
                eng.known[id(sem)] = val

    @staticmethod
    def _deps(rd, wr, wa):
        evs = []
        for r in rd:
            evs += r.w
        for r in wr:
            evs += r.w
            evs += r.rd
        for r in wa:
            evs += r.rd
        return evs

    @staticmethod
    def _addrd(r, ev):
        r.rd = [e for e in r.rd if not (e[0] is ev[0] and e[1] <= ev[1])]
        r.rd.append(ev)

    def op(self, eng, fn, rd=(), wr=(), inc=True):
        self._wait(eng, self._deps(rd, wr, ()), same_ok=(eng is self.pe))
        ins = fn()
        self.n_ops += 1
        if inc:
            eng.cnt += 1
            ins.then_inc(eng.sem, 1)
            ev = (eng.sem, eng.cnt)
        else:
            ev = (eng.sem, eng.cnt + 1)
        for r in rd:
            self._addrd(r, ev)
        for r in wr:
            r.w = [ev]
            r.rd = []
        return ins

    def dma(self, q, out, in_, rd=(), wr=(), wa=()):
        self._wait(q, self._deps(rd, wr, wa), same_ok=False)
        i = self.dnext
        self.dnext = (self.dnext + 1) % NDS
        sem = self.dsem[i]
        if q.known.get(id(sem), 0) < self.dtot[i]:
            q.h.wait_ge(sem, self.dtot[i])
            q.known[id(sem)] = self.dtot[i]
        self.dtot[i] += 16
        q.h.dma_start(out=out, in_=in_).then_inc(sem, 16)
        self.n_ops += 1
        ev = (sem, self.dtot[i])
        for r in rd:
            r.rd.append(ev)
        for r in wr:
            r.w = [ev]
            r.rd = []
        for r in wa:
            r.w.append(ev)
        return ev

    def barrier(self):
        evs = [(e.sem, e.cnt) for e in self.engs if e.cnt > 0]
        evs += [(self.dsem[i], self.dtot[i]) for i in range(NDS) if self.dtot[i] > 0]
        for e in self.engs:
            self._wait(e, evs)


def tiles_all():
    return [(0, 256, 1)] + [(256 + 512 * i, 512, 0) for i in range(8)]


def build(dbg=(), stop=None):
    nc = bass.Bass("TRN2", target_bir_lowering=False)
    es = contextlib.ExitStack()
    with es:
        _build(nc, es, set(dbg), stop)
    return nc


def _build(nc, es, dbg, stop):
    em = Emit(nc, es)
    pe, act, dve, pool, sp = em.pe, em.act, em.dve, em.pool, em.sp
    T_, V_, S_, G_ = nc.tensor, nc.vector, nc.scalar, nc.gpsimd

    def din(name, shape, dt=F32):
        return nc.dram_tensor(name, list(shape), dt, kind="ExternalInput").ap()

    def dscr(name, shape, dt):
        kind = "ExternalOutput" if name in dbg else "Internal"
        return nc.dram_tensor(name, list(shape), dt, kind=kind).ap()

    x_in = din("x", [SEQ, D])
    ctx_in = din("ctx", [LC, D])
    cc_in = din("cc", [128, 8, 2])
    w_mod = din("w_mod", [DEPTH, D, 9 * D])
    bmod_in = din("bmod_c", [DEPTH, 128, 72])
    fw = {}
    for f_ in (1, 2):
        fw[f_] = (din(f"f{f_}w1", [DEPTH, D, DFF]), din(f"f{f_}w3", [DEPTH, D, DFF]), din(f"f{f_}w2", [DEPTH, DFF, D]))
    w_in_r = din("w_in_r", [DEPTH, D, NFM + NTM])
    w_out = din("w_out", [DEPTH, D, D])
    gcol_in = din("gcol", [DEPTH, 128, 4])
    glan_in = din("glan", [DEPTH, 256])
    wg_in = din("wg", [DEPTH, 33, 256])
    braw_in = din("braw", [DEPTH, 128, 4 * 21 * 128])
    cst_in = din("cst", [128, 2048])
    gtab_in = din("gtab", [128, 4096])
    czsz_in = din("czsz", [128, 1024])
    cosT_in = din("cosT", [128, SEQ])
    sinT_in = din("sinT", [128, SEQ])
    out_d = nc.dram_tensor("out", [SEQ, D], F32, kind="ExternalOutput").ap()

    XS = dscr("XS", [128, 8, T], F32)
    MIXT = dscr("MIXT", [128, 8, T], BF16)
    VSX = dscr("VSX", [SEQ, 512], BF16)
    AS_ = dscr("AS", [128, 64, 256], BF16)
    NQs = dscr("NQs", [128, 2, T], BF16)
    NKs = dscr("NKs", [128, 2, T], BF16)
    NVs = dscr("NVs", [T, 256], BF16)
    AQs = dscr("AQs", [128, 2, T], BF16)
    AKs = dscr("AKs", [128, 2, T], BF16)
    AVs = dscr("AVs", [T, 128], BF16)
    GQs = dscr("GQs", [128, T], F32)
    GKs = dscr("GKs", [128, T], F32)
    GGs = dscr("GGs", [32, T], F32)
    GKts = dscr("GKts", [T, 128], BF16)
    GVs = dscr("GVs", [T, 256], BF16)
    GRs = dscr("GRs", [T, 256], BF16)
    r_XS = [Res(f"XS{i}") for i in range(9)]
    r_MIXT = Res("MIXT")
    r_scr = {n: Res(n) for n in ("VSX", "AS", "NQ", "NK", "NV", "AQ", "AK", "AV", "GQ", "GK", "GG", "GKt", "GV", "GR")}

    uid = [0]

    def sb(st, name, shape, dt=F32):
        uid[0] += 1
        return st.enter_context(nc.sbuf_tensor(f"sb{uid[0]}_{name}", list(shape), dt))

    ps = es.enter_context(nc.psum_tensor("ps", [128, 8, 512], F32))
    psr = [Res(f"ps{i}") for i in range(8)]
    cst = sb(es, "cst", [128, 256], F32)
    cstb = sb(es, "cstb", [128, 896], BF16)
    modv = sb(es, "modv", [128, DEPTH, 72, 2], F32)
    r_cst = Res("cst")
    r_mod = Res("mod")

    def C(name, rows=128):
        o, n = CST_OFF[name]
        return cst[0:rows, o:o + n]

    def CB(name, rows=128):
        o, n = CST_OFF[name]
        return cstb[0:rows, o:o + n]

    em.dma(sp, cst[:], cst_in[:, 0:256], wr=[r_cst])
    em.dma(pool, cstb[:], cst_in[:, 0:896], wa=[r_cst])
    ident = C("ident")
    ones_b = CB("ones")
    blk64_b = CB("blk64")
    prot_b = CB("prot")
    bd_b = CB("bd")
    m1_b = CB("m1")

    def mcol(l, i, k, s):
        return modv[:, l, i * 8 + k, s:s + 1]

    def phase_mod():
        with contextlib.ExitStack() as ph:
            ccs = sb(ph, "ccs", [128, 8, 2])
            bm = sb(ph, "bm", [128, DEPTH, 72])
            wm = [sb(ph, f"wm{i}", [128, 8, 512]) for i in range(3)]
            rows = sb(ph, "mrows", [2, 9 * D])
            r_cc, r_bm, r_rows = Res(), Res(), Res()
            r_wm = [Res(), Res(), Res()]
            em.dma(sp, ccs[:], cc_in, wr=[r_cc])
            em.dma(sp, bm[:], bmod_in.rearrange("l p j -> p l j"), wr=[r_bm])
            em.op(act, lambda: S_.activation(ccs[:], ccs[:], AF.Silu), wr=[r_cc])
            ns = 0
            for l in range(DEPTH):
                for slab in range(18):
                    b = ns % 3
                    ns += 1
                    src = w_mod[l, :, slab * 512:(slab + 1) * 512].rearrange("(k p) n -> p k n", p=128)
                    em.dma(sp, wm[b][:], src, wr=[r_wm[b]])
                    pb = slab % 2
                    for k in range(8):
                        em.op(pe, lambda: T_.matmul(ps[0:2, pb, :], ccs[:, k, :], wm[b][:, k, :], start=(k == 0), stop=(k == 7)),
                              rd=[r_wm[b], r_cc], wr=[psr[pb]], inc=(k == 7))
                    em.op(act, lambda: S_.copy(rows[:, slab * 512:(slab + 1) * 512], ps[0:2, pb, :]), rd=[psr[pb]], wr=[r_rows])
                for j in range(72):
                    em.op(pe, lambda: T_.transpose(ps[:, 2 + l, 2 * j:2 * j + 2], rows[0:2, j * 128:(j + 1) * 128], ident[0:2, 0:2]),
                          rd=[r_rows, r_cst], wr=[psr[2 + l]], inc=(j == 71))
                em.op(dve, lambda: V_.tensor_tensor(modv[:, l, :, :], ps[:, 2 + l, 0:144].rearrange("p (j s) -> p j s", s=2),
                                                    bm[:, l, :].unsqueeze(2).to_broadcast([128, 72, 2]), ALU.add),
                      rd=[psr[2 + l], r_bm], wr=[r_mod])
                for i in (1, 4, 7):
                    em.op(dve, lambda: V_.tensor_scalar(modv[:, l, i * 8:(i + 1) * 8, :], modv[:, l, i * 8:(i + 1) * 8, :],
                                                        1.0, None, ALU.add), wr=[r_mod])
                for i in (2, 8):
                    em.op(dve, lambda: V_.tensor_scalar(modv[:, l, i * 8:(i + 1) * 8, :], modv[:, l, i * 8:(i + 1) * 8, :],
                                                        0.5, None, ALU.mult), wr=[r_mod])
            if "modv_o" in dbg:
                mo = nc.dram_tensor("modv_o", [128, DEPTH * 144], F32, kind="ExternalOutput").ap()
                em.dma(sp, mo, modv[:].rearrange("p l j s -> p (l j s)"), rd=[r_mod])
            em.barrier()

    def phase_T():
        NBUF = 4 if OPT_T4 else 2
        with contextlib.ExitStack() as ph:
            xin = [sb(ph, f"xin{i}", [128, D]) for i in range(NBUF)]
            xtb = [sb(ph, f"xtb{i}", [128, 8, 128]) for i in range(NBUF)]
            r_xin = [Res() for _ in range(NBUF)]
            r_xtb = [Res() for _ in range(NBUF)]

            def load(blk):
                b = blk % NBUF
                src = ctx_in[blk * 128:(blk + 1) * 128, :] if blk < 2 else x_in[(blk - 2) * 128:(blk - 1) * 128, :]
                em.dma(sp, xin[b][:], src, wr=[r_xin[b]])

            for blk in range(NBUF - 1):
                load(blk)
            for blk in range(34):
                b = blk % NBUF
                if blk + NBUF - 1 < 34:
                    load(blk + NBUF - 1)
                for k in range(8):
                    bank = b * 2 + k // 4
                    em.op(pe, lambda: T_.transpose(ps[:, bank, (k % 4) * 128:(k % 4 + 1) * 128], xin[b][:, k * 128:(k + 1) * 128], ident),
                          rd=[r_xin[b], r_cst], wr=[psr[bank]], inc=(k % 4 == 3))
                em.op(act, lambda: S_.copy(xtb[b][:, 0:4, :], ps[:, b * 2, :].rearrange("p (k t) -> p k t", k=4)),
                      rd=[psr[b * 2]], wr=[r_xtb[b]])
                em.op(dve, lambda: V_.tensor_copy(xtb[b][:, 4:8, :], ps[:, b * 2 + 1, :].rearrange("p (k t) -> p k t", k=4)),
                      rd=[psr[b * 2 + 1]], wr=[r_xtb[b]])
                ti = 0 if blk < 2 else 1 + (blk - 2) // 4
                em.dma(sp, XS[:, :, blk * 128:(blk + 1) * 128], xtb[b][:], rd=[r_xtb[b]], wa=[r_XS[ti]])
            em.barrier()

    def norm_mod(xT, r_x, hT, r_h, rstd, r_rstd, tmp, r_tmp, N, l, i0, s, ssbank, part="all"):
        if part in ("all", "sq"):
            for k in range(8):
                em.op(act, lambda: S_.activation(hT[:, k, :N], xT[:, k, :N], AF.Square), rd=[r_x], wr=[r_h])
        if part == "sq":
            return
        for k in range(8):
            em.op(pe, lambda: T_.matmul(ps[:, ssbank, :N], ones_b, hT[:, k, :N], start=(k == 0), stop=(k == 7)),
                  rd=[r_h, r_cst], wr=[psr[ssbank]], inc=(k == 7))
        em.op(act, lambda: S_.activation(rstd[:, :N], ps[:, ssbank, :N], AF.Ln, bias=eps_c[:, 0:1], scale=1.0 / D),
              rd=[psr[ssbank]], wr=[r_rstd])
        em.op(act, lambda: S_.activation(rstd[:, :N], rstd[:, :N], AF.Exp, scale=-0.5), wr=[r_rstd])
        for k in range(8):
            tb = tmp[k % 2]
            em.op(dve, lambda: V_.tensor_tensor(tb[:, :N], xT[:, k, :N], rstd[:, :N], ALU.mult),
                  rd=[r_x, r_rstd], wr=[r_tmp[k % 2]])
            em.op(act, lambda: S_.activation(hT[:, k, :N], tb[:, :N], AF.Identity, bias=mcol(l, i0, k, s), scale=mcol(l, i0 + 1, k, s)),
                  rd=[r_tmp[k % 2], r_mod], wr=[r_h])

    eps_c = sb(es, "eps_c", [128, 1])
    em.op(dve, lambda: V_.memset(eps_c[:], EPS), wr=[r_cst])

    def ffn_weights(st, l, which):
        W1, W3, W2 = fw[which]
        w1 = sb(st, "w1", [128, 8, DFF], BF16)
        w3 = sb(st, "w3", [128, 8, DFF], BF16)
        w2 = sb(st, "w2", [128, NJ, D], BF16)
        r_w1, r_w3, r_w2 = [Res(), Res()], [Res(), Res()], [Res(), Res()]
        H = DFF // 2
        for hh in range(2):
            em.dma(pool, w1[:, :, hh * H:(hh + 1) * H], W1[l, :, hh * H:(hh + 1) * H].rearrange("(k p) n -> p k n", p=128), wr=[r_w1[hh]])
            em.dma(pool, w3[:, :, hh * H:(hh + 1) * H], W3[l, :, hh * H:(hh + 1) * H].rearrange("(k p) n -> p k n", p=128), wr=[r_w3[hh]])
        for hh in range(2):
            em.dma(pool, w2[:, hh * 11:(hh + 1) * 11, :], W2[l, hh * H:(hh + 1) * H, :].rearrange("(j p) n -> p j n", p=128), wr=[r_w2[hh]])
        return w1, w3, w2, r_w1, r_w3, r_w2

    def phase_ffn(l, which, final, pre=None):
        i0 = 0 if which == 1 else 6
        tl = tiles_all()
        if final:
            tl = tl[1:]
        with contextlib.ExitStack() as ph:
            w1, w3, w2, r_w1, r_w3, r_w2 = pre if pre is not None else ffn_weights(ph, l, which)
            xT = [sb(ph, f"xT{i}", [128, 8, 512]) for i in range(2)]
            gT = sb(ph, "gT", [128, NJ, 512], BF16)
            hT = sb(ph, "hT", [128, 8, 512], BF16)
            rstd = sb(ph, "rstd", [128, 512])
            tmp = [sb(ph, f"tmp{i}", [128, 512]) for i in range(2)]
            r_x = [Res(), Res()]
            r_g, r_h, r_rstd = Res(), Res(), Res()
            r_tmp = [Res(), Res()]
            if final:
                ot = sb(ph, "ot", [128, 512])
                r_ot = Res()

            def load(ti):
                t0, N, s = tl[ti]
                gi = ti if not final else ti + 1
                em.dma(sp, xT[ti % 2][:, :, :N], XS[:, :, t0:t0 + N], rd=[r_XS[gi]], wr=[r_x[ti % 2]])

            load(0)
            norm_mod(xT[0], r_x[0], hT, r_h, rstd, r_rstd, tmp, r_tmp, tl[0][1], l, i0, tl[0][2], 6)
            for ti, (t0, N, s) in enumerate(tl):
                b = ti % 2
                gi = ti if not final else ti + 1
                if ti + 1 < len(tl):
                    load(ti + 1)
                for j in range(NJ):
                    hh = j // 11
                    pu1, pu3 = (j % 2), 2 + (j % 2)
                    for k in range(8):
                        em.op(pe, lambda: T_.matmul(ps[:, pu1, :N], w1[:, k, j * 128:(j + 1) * 128], hT[:, k, :N], start=(k == 0), stop=(k == 7)),
                              rd=[r_w1[hh], r_h], wr=[psr[pu1]], inc=(k == 7))
                    for k in range(8):
                        em.op(pe, lambda: T_.matmul(ps[:, pu3, :N], w3[:, k, j * 128:(j + 1) * 128], hT[:, k, :N], start=(k == 0), stop=(k == 7)),
                              rd=[r_w3[hh], r_h], wr=[psr[pu3]], inc=(k == 7))
                    tb = tmp[j % 2]
                    em.op(act, lambda: S_.activation(tb[:, :N], ps[:, pu1, :N], AF.Silu), rd=[psr[pu1]], wr=[r_tmp[j % 2]])
                    em.op(dve, lambda: V_.tensor_tensor(gT[:, j, :N], tb[:, :N], ps[:, pu3, :N], ALU.mult),
                          rd=[r_tmp[j % 2], psr[pu3]], wr=[r_g])
                for m in range(8):
                    if m == 2 and ti + 1 < len(tl):
                        nb_ = (ti + 1) % 2
                        norm_mod(xT[nb_], r_x[nb_], hT, r_h, rstd, r_rstd, tmp, r_tmp, tl[ti + 1][1], l, i0, tl[ti + 1][2], 6)
                    py = 4 + (m % 2)
                    for j in range(NJ):
                        em.op(pe, lambda: T_.matmul(ps[:, py, :N], w2[:, j, m * 128:(m + 1) * 128], gT[:, j, :N], start=(j == 0), stop=(j == NJ - 1)),
                              rd=[r_w2[j // 11], r_g], wr=[psr[py]], inc=(j == NJ - 1))
                    em.op(dve, lambda: V_.scalar_tensor_tensor(xT[b][:, m, :N], ps[:, py, :N], mcol(l, i0 + 2, m, s), xT[b][:, m, :N], ALU.mult, ALU.add),
                          rd=[psr[py], r_mod], wr=[r_x[b]])
                if not final:
                    em.dma(sp, XS[:, :, t0:t0 + N], xT[b][:, :, :N], rd=[r_x[b]], wr=[r_XS[gi]])
                else:
                    for tb_ in range(N // 128):
                        for half in range(2):
                            for mm in range(4):
                                m = half * 4 + mm
                                em.op(pe, lambda: T_.transpose(ps[:, 7, mm * 128:(mm + 1) * 128], xT[b][:, m, tb_ * 128:(tb_ + 1) * 128], ident),
                                      rd=[r_x[b]], wr=[psr[7]], inc=(mm == 3))
                            em.op(act, lambda: S_.copy(ot[:], ps[:, 7, :]), rd=[psr[7]], wr=[r_ot])
                            r0 = t0 - LC + tb_ * 128
                            em.dma(sp, out_d[r0:r0 + 128, half * 512:(half + 1) * 512], ot[:], rd=[r_ot])
            em.barrier()

    def phase_P(l):
        last = (l == DEPTH - 1)
        with contextlib.ExitStack() as ph:
            wfm = sb(ph, "wfm", [128, 8, NFM], BF16)
            wtm = sb(ph, "wtm", [128, 8, NTM], BF16)
            xT = [sb(ph, f"pxT{i}", [128, 8, 512]) for i in range(2)]
            hTs = [sb(ph, f"phT{i}", [128, 8, 512], BF16) for i in range(2)]
            r_hs = [Res(), Res()]
            rstd = sb(ph, "prstd", [128, 512])
            tmp = [sb(ph, f"ptmp{i}", [128, 512]) for i in range(2)]
            cosT = sb(ph, "cosT", [128, SEQ])
            sinT = sb(ph, "sinT", [128, SEQ])
            gcol = sb(ph, "gcol", [128, 4])
            czsz = sb(ph, "czsz", [128, 1024], BF16)
            hxf = sb(ph, "hxf", [128, 2, 512], BF16)
            vz = sb(ph, "vz", [128, 2, 512], BF16)
            NB = 10
            stb = [sb(ph, f"stb{i}", [128, 512], BF16) for i in range(NB)]
            stf = [sb(ph, f"stf{i}", [128, 512]) for i in range(NB)]
            r_stb = [Res() for _ in range(NB)]
            r_stf = [Res() for _ in range(NB)]
            cnt = {"b": 0, "f": 0}

            def nb():
                i = cnt["b"] % NB
                cnt["b"] += 1
                return stb[i], r_stb[i]

            def nf():
                i = cnt["f"] % NB
                cnt["f"] += 1
                return stf[i], r_stf[i]

            r_wfm, r_wtm, r_tab, r_x = Res(), Res(), Res(), [Res(), Res()]
            r_rstd, r_tmp, r_hxf, r_vz = Res(), [Res(), Res()], Res(), Res()
            em.dma(pool, wfm[:], w_in_r[l, :, 0:NFM].rearrange("(k p) n -> p k n", p=128), wr=[r_wfm])
            em.dma(pool, wtm[:], w_in_r[l, :, NFM:NFM + NTM].rearrange("(k p) n -> p k n", p=128), wr=[r_wtm])
            em.dma(sp, cosT[:], cosT_in, wr=[r_tab])
            em.dma(sp, sinT[:], sinT_in, wa=[r_tab])
            em.dma(sp, gcol[:], gcol_in[l], wa=[r_tab])
            em.dma(pool, czsz[:], czsz_in, wa=[r_tab])
            tl = tiles_all()

            def load(ti):
                t0, N, s = tl[ti]
                em.dma(sp, xT[ti % 2][:, :, :N], XS[:, :, t0:t0 + N], rd=[r_XS[ti]], wr=[r_x[ti % 2]])

            load(0)
            goff = {}
            o = 0
            for gi, (kind, c) in enumerate(FM_GROUPS):
                goff[gi] = o
                o += 32 if kind == "GG" else 128
            if OPT_PHT:
                norm_mod(xT[0], r_x[0], hTs[0], r_hs[0], rstd, r_rstd, tmp, r_tmp, tl[0][1], l, 3, tl[0][2], 0)
            for ti, (t0, N, s) in enumerate(tl):
                b = ti % 2
                hT, r_h = hTs[b], r_hs[b]
                if ti + 1 < len(tl):
                    load(ti + 1)
                if not OPT_PHT:
                    norm_mod(xT[b], r_x[b], hT, r_h, rstd, r_rstd, tmp, r_tmp, N, l, 3, s, 0)
                isx = (s == 0)
                xt0 = t0 - LC
                ACC = [1, 2, 5, 6]
                pend = []

                def stage_A(gi, kind, c, pb):
                    st8 = {}
                    if kind == "F":
                        em.op(act, lambda: S_.copy(hxf[:, c, :N], ps[:, pb, :N]), rd=[psr[pb]], wr=[r_hxf])
                        return None
                    if kind in ("GQ", "GK"):
                        st, rs = nf()
                        em.op(act, lambda: S_.copy(st[:, :N], ps[:, pb, :N]), rd=[psr[pb]], wr=[rs])
                        dst, rr = (GQs, r_scr["GQ"]) if kind == "GQ" else (GKs, r_scr["GK"])
                        em.dma(sp, dst[:, t0:t0 + N], st[:, :N], rd=[rs], wa=[rr])
                        return None
                    if kind == "GG":
                        st, rs = nf()
                        em.op(act, lambda: S_.copy(st[0:32, :N], ps[0:32, pb, :N]), rd=[psr[pb]], wr=[rs])
                        em.dma(sp, GGs[:, t0:t0 + N], st[0:32, :N], rd=[rs], wa=[r_scr["GG"]])
                        return None
                    sq, rsq = nb()
                    em.op(act, lambda: S_.activation(sq[:, :N], ps[:, pb, :N], AF.Square), rd=[psr[pb]], wr=[rsq])
                    st8.update(kind=kind, c=c, pb=pb, sq=sq, rsq=rsq)
                    return st8

                def stage_B(st8):
                    kind, c, pb, sq, rsq = st8["kind"], st8["c"], st8["pb"], st8["sq"], st8["rsq"]
                    gidx = {"NQ": 0, "NK": 1, "AQ": 2, "AK": 3}[kind]
                    em.op(pe, lambda: T_.matmul(ps[:, 3, :N], blk64_b, sq[:, :N], start=True, stop=True), rd=[rsq], wr=[psr[3]])
                    r64, rr64 = nf()
                    em.op(act, lambda: S_.activation(r64[:, :N], ps[:, 3, :N], AF.Ln, bias=eps_c[:, 0:1], scale=1.0 / 64), rd=[psr[3]], wr=[rr64])
                    em.op(act, lambda: S_.activation(r64[:, :N], r64[:, :N], AF.Exp, scale=-0.5), wr=[rr64])
                    qn, rqn = nb()
                    em.op(dve, lambda: V_.scalar_tensor_tensor(qn[:, :N], ps[:, pb, :N], gcol[:, gidx:gidx + 1], r64[:, :N], ALU.mult, ALU.mult),
                          rd=[psr[pb], rr64, r_tab], wr=[rqn])
                    st8.update(qn=qn, rqn=rqn)

                def stage_C(st8):
                    kind, c, qn, rqn = st8["kind"], st8["c"], st8["qn"], st8["rqn"]
                    res, rres = qn, rqn
                    if kind in ("AQ", "AK") and isx:
                        em.op(pe, lambda: T_.matmul(ps[:, 7, :N], prot_b, qn[:, :N], start=True, stop=True), rd=[rqn], wr=[psr[7]])
                        t1, rt1 = nf()
                        t2, rt2 = nf()
                        em.op(pool, lambda: G_.tensor_tensor(t1[:, :N], qn[:, :N], cosT[:, xt0:xt0 + N], ALU.mult), rd=[rqn, r_tab], wr=[rt1])
                        em.op(dve, lambda: V_.tensor_tensor(t2[:, :N], ps[:, 7, :N], sinT[:, xt0:xt0 + N], ALU.mult), rd=[psr[7], r_tab], wr=[rt2])
                        res, rres = nb()
                        em.op(pool, lambda: G_.tensor_tensor(res[:, :N], t1[:, :N], t2[:, :N], ALU.add), rd=[rt1, rt2], wr=[rres])
                    dst, rr = {"NQ": (NQs, r_scr["NQ"]), "NK": (NKs, r_scr["NK"]), "AQ": (AQs, r_scr["AQ"]), "AK": (AKs, r_scr["AK"])}[kind]
                    em.dma(sp, dst[:, c, t0:t0 + N], res[:, :N], rd=[rres], wa=[rr])

                def drain(keep):
                    while pend and len(pend) > keep:
                        st8 = pend[0]
                        if st8["stage"] == 1:
                            stage_C(st8)
                            pend.pop(0)
                        else:
                            break
                    for st8 in pend:
                        if st8["stage"] == 0 and st8["age"] >= 1:
                            stage_B(st8)
                            st8["stage"] = 1

                for gi, (kind, c) in enumerate(FM_GROUPS):
                    M = 32 if kind == "GG" else 128
                    pb = ACC[gi % 4]
                    for k in range(8):
                        em.op(pe, lambda: T_.matmul(ps[0:M, pb, :N], wfm[:, k, goff[gi]:goff[gi] + M], hT[:, k, :N], start=(k == 0), stop=(k == 7)),
                              rd=[r_wfm, r_h], wr=[psr[pb]], inc=(k == 7))
                    for st8 in list(pend):
                        st8["age"] += 1
                    for st8 in list(pend):
                        if st8["stage"] == 1 and st8["age"] >= 2:
                            stage_C(st8)
                            pend.remove(st8)
                    for st8 in pend:
                        if st8["stage"] == 0 and st8["age"] >= 1:
                            stage_B(st8)
                            st8["stage"] = 1
                    st8 = stage_A(gi, kind, c, pb)
                    if st8 is not None:
                        st8["stage"], st8["age"] = 0, 0
                        pend.append(st8)
                    if OPT_PHT and gi == 5 and ti + 1 < len(tl):
                        nb_ = (ti + 1) % 2
                        norm_mod(xT[nb_], r_x[nb_], hTs[nb_], r_hs[nb_], rstd, r_rstd, tmp, r_tmp, tl[ti + 1][1], l, 3, tl[ti + 1][2], 0, part="sq")
                    if OPT_PHT and gi == 9 and ti + 1 < len(tl):
                        nb_ = (ti + 1) % 2
                        norm_mod(xT[nb_], r_x[nb_], hTs[nb_], r_hs[nb_], rstd, r_rstd, tmp, r_tmp, tl[ti + 1][1], l, 3, tl[ti + 1][2], 0, part="rest")
                for st8 in list(pend):
                    if st8["stage"] == 0:
                        stage_B(st8)
                        st8["stage"] = 1
                for st8 in list(pend):
                    stage_C(st8)
                pend.clear()
                for tb_ in range(N // 128):
                    for c in range(2):
                        em.op(pe, lambda: T_.matmul(ps[:, 4, :].rearrange("p (r c f) -> p r c f", r=2, c=2)[:, :, c, :],
                                                    hxf[:, c, tb_ * 128:(tb_ + 1) * 128], bd_b.rearrange("p (r f) -> p r f", r=2),
                                                    start=True, stop=True),
                              rd=[r_hxf], wr=[psr[4]], inc=(c == 1))
                    if isx:
                        st, rs = nb()
                        em.op(act, lambda: S_.copy(st[:], ps[:, 4, :]), rd=[psr[4]], wr=[rs])
                        r0 = xt0 + tb_ * 128
                        em.dma(sp, VSX[r0:r0 + 128, :], st[:], rd=[rs], wa=[r_scr["VSX"]])
                    elif not last:
                        em.op(act, lambda: S_.copy(vz[:, tb_, :], ps[:, 4, :]), rd=[psr[4]], wr=[r_vz])
                if (not isx) and (not last):
                    for fc in range(2):
                        n_ = 0
                        for tb_ in range(2):
                            for ri in range(2):
                                em.op(pe, lambda: T_.matmul(ps[:, 7, 0:256], vz[:, tb_, ri * 256 + fc * 128: ri * 256 + (fc + 1) * 128],
                                                            czsz[:, ri * 512 + tb_ * 256: ri * 512 + (tb_ + 1) * 256],
                                                            start=(n_ == 0), stop=(n_ == 3)),
                                      rd=[r_vz, r_tab], wr=[psr[7]], inc=(n_ == 3))
                                n_ += 1
                        st, rs = nb()
                        em.op(act, lambda: S_.copy(st[:, 0:256], ps[:, 7, 0:256]), rd=[psr[7]], wr=[rs])
                        em.dma(sp, MIXT[:, fc, 0:256], st[:, 0:256], rd=[rs], wa=[r_MIXT])
                for tb_ in range(N // 128):
                    r0 = t0 + tb_ * 128
                    for g in range(2):
                        for k in range(8):
                            em.op(pe, lambda: T_.matmul(ps[:, 5 + g, :], hT[:, k, tb_ * 128:(tb_ + 1) * 128], wtm[:, k, g * 512:(g + 1) * 512],
                                                        start=(k == 0), stop=(k == 7)),
                                  rd=[r_wtm, r_h], wr=[psr[5 + g]], inc=(k == 7))
                    st, rs = nb()
                    em.op(act, lambda: S_.copy(st[:, 0:256], ps[:, 5, 0:256]), rd=[psr[5]], wr=[rs])
                    em.op(act, lambda: S_.activation(st[:, 256:512], ps[:, 5, 256:512], AF.Silu), rd=[psr[5]], wr=[rs])
                    em.dma(sp, NVs[r0:r0 + 128, :], st[:, 0:256], rd=[rs], wa=[r_scr["NV"]])
                    em.dma(sp, GRs[r0:r0 + 128, :], st[:, 256:512], rd=[rs], wa=[r_scr["GR"]])
                    st2, rs2 = nb()
                    em.op(dve, lambda: V_.tensor_copy(st2[:], ps[:, 6, :]), rd=[psr[6]], wr=[rs2])
                    em.dma(sp, GKts[r0:r0 + 128, :], st2[:, 0:128], rd=[rs2], wa=[r_scr["GKt"]])
                    em.dma(sp, GVs[r0:r0 + 128, :], st2[:, 128:384], rd=[rs2], wa=[r_scr["GV"]])
                    em.dma(sp, AVs[r0:r0 + 128, :], st2[:, 384:512], rd=[rs2], wa=[r_scr["AV"]])
            em.barrier()

    def phase_O(l):
        last = (l == DEPTH - 1)
        tl = tiles_all()
        with contextlib.ExitStack() as ph:
            wo = sb(ph, "wo", [128, 8, D], BF16)
            xT = [sb(ph, f"oxT{i}", [128, 8, 512]) for i in range(2)]
            mT = [sb(ph, f"omT{i}", [128, 8, 512], BF16) for i in range(2)]
            r_wo, r_x, r_m = Res(), [Res(), Res()], [Res(), Res()]
            em.dma(pool, wo[:], w_out[l].rearrange("(k p) n -> p k n", p=128), wr=[r_wo])
            idx = list(range(1, 9)) if last else list(range(9))

            def load(n):
                ti = idx[n]
                t0, N, s = tl[ti]
                em.dma(sp, xT[n % 2][:, :, :N], XS[:, :, t0:t0 + N], rd=[r_XS[ti]], wr=[r_x[n % 2]])
                em.dma(sp, mT[n % 2][:, :, :N], MIXT[:, :, t0:t0 + N], rd=[r_MIXT], wr=[r_m[n % 2]])

            load(0)
            for n, ti in enumerate(idx):
                t0, N, s = tl[ti]
                b = n % 2
                if n + 1 < len(idx):
                    load(n + 1)
                for m in range(8):
                    pb = m % 2
                    for k in range(8):
                        em.op(pe, lambda: T_.matmul(ps[:, pb, :N], wo[:, k, m * 128:(m + 1) * 128], mT[b][:, k, :N], start=(k == 0), stop=(k == 7)),
                              rd=[r_wo, r_m[b]], wr=[psr[pb]], inc=(k == 7))
                    em.op(dve, lambda: V_.scalar_tensor_tensor(xT[b][:, m, :N], ps[:, pb, :N], mcol(l, 5, m, s), xT[b][:, m, :N], ALU.mult, ALU.add),
                          rd=[psr[pb], r_mod], wr=[r_x[b]])
                em.dma(sp, XS[:, :, t0:t0 + N], xT[b][:, :, :N], rd=[r_x[b]], wr=[r_XS[ti]])
            em.barrier()

    def mix_gqa(l):
        last = (l == DEPTH - 1)
        with contextlib.ExitStack() as ph:
            AK = sb(ph, "gAK", [128, 2, T], BF16)
            V1 = sb(ph, "gV1", [128, 34, 2, 65], BF16)
            Q = [sb(ph, f"gQ{i}", [128, 512], BF16) for i in range(2)]
            PT = [sb(ph, f"gPT{i}", [128, 1024], BF16) for i in range(4)]
            rden = [sb(ph, f"grden{i}", [128, 512]) for i in range(2)]
            On = [sb(ph, f"gOn{i}", [64, 512]) for i in range(2)]
            osb = [sb(ph, f"gosb{i}", [64, 512], BF16) for i in range(2)]
            r_AK, r_V1, r_Q, r_PT = Res(), Res(), [Res(), Res()], [Res(), Res(), Res(), Res()]
            r_rden, r_On, r_osb = [Res(), Res()], [Res(), Res()], [Res(), Res()]
            em.dma(sp, AK[:], AKs, rd=[r_scr["AK"]], wr=[r_AK])
            em.op(pool, lambda: G_.memset(V1[:, :, :, 64:65], 1.0), wr=[r_V1])
            for q4 in range(2):
                a, b_ = q4 * 17, q4 * 17 + 17
                for j_ in range(2):
                    em.dma(sp, V1[:, a:b_, j_, 0:64], AVs[a * 128:b_ * 128, j_ * 64:(j_ + 1) * 64].rearrange("(t p) d -> p t d", p=128),
                           rd=[r_scr["AV"]], wa=[r_V1])
            qtiles = [(256 + 512 * i, 512, list(range(34))) for i in range(8)]
            if not last:
                qtiles = [(0, 256, [0, 1])] + qtiles
            steps = []
            n = 0
            for j in range(2):
                for (t0, N, keys) in qtiles:
                    for ki, kt in enumerate(keys):
                        steps.append((n, j, t0, N, ki, kt, len(keys)))
                    n += 1
            cur = {}

            SB = [0, 2, 6]

            def qk_ex(si):
                n_, j, t0, N, ki, kt, nk = steps[si]
                Qt, rq = Q[n_ % 2], r_Q[n_ % 2]
                if ki == 0:
                    em.dma(sp, Qt[:, :N], AQs[:, j, t0:t0 + N], rd=[r_scr["AQ"]], wr=[rq])
                sbk = SB[si % 3]
                for g in range(2):
                    em.op(pe, lambda: T_.matmul(ps[:, sbk + g, :N], AK[64 * g:64 * g + 64, j, kt * 128:(kt + 1) * 128],
                                                Qt[64 * g:64 * g + 64, :N], start=True, stop=True, tile_position=(64 * g, 0)),
                          rd=[r_AK, rq], wr=[psr[sbk + g]])
                pt, rpt = PT[si % 4], r_PT[si % 4]
                em.op(act, lambda: S_.activation(pt[:, 0:2 * N].rearrange("p (g n) -> p g n", g=2), ps[:, sbk:sbk + 2, :N], AF.Exp, scale=0.125),
                      rd=[psr[sbk], psr[sbk + 1]], wr=[rpt])

            def pv(si):
                n_, j, t0, N, ki, kt, nk = steps[si]
                pt, rpt = PT[si % 4], r_PT[si % 4]
                for g in range(2):
                    em.op(pe, lambda: T_.matmul(ps[0:65, 4 + g, :N], V1[:, kt, j, :], pt[:, g * N:(g + 1) * N],
                                                start=(ki == 0), stop=(ki == nk - 1)),
                          rd=[r_V1, rpt], wr=[psr[4 + g]], inc=(ki == nk - 1))
                if ki == nk - 1:
                    for g in range(2):
                        ob, rob = osb[g], r_osb[g]
                        em.op(dve, lambda: V_.reciprocal(rden[g][64:65, :N], ps[64:65, 4 + g, :N]), rd=[psr[4 + g]], wr=[r_rden[g]])
                        em.op(act, lambda: S_.copy(On[g][:, :N], ps[0:64, 4 + g, :N]), rd=[psr[4 + g]], wr=[r_On[g]])
                    for g in range(2):
                        ob, rob = osb[g], r_osb[g]
                        em.op(pe, lambda: T_.matmul(ps[0:64, 4 + g, :N], C("ones")[64:65, 0:64], rden[g][64:65, :N], start=True, stop=True, tile_position=(64, 0)),
                              rd=[r_rden[g]], wr=[psr[4 + g]])
                        em.op(dve, lambda: V_.tensor_tensor(ob[:, :N], On[g][:, :N], ps[0:64, 4 + g, :N], ALU.mult), rd=[r_On[g], psr[4 + g]], wr=[rob])
                        em.dma(sp, MIXT[64 * g:64 * g + 64, 6 + j, t0:t0 + N], ob[:, :N], rd=[rob], wa=[r_MIXT])

            LA = 2
            for si in range(min(LA, len(steps))):
                qk_ex(si)
            for si in range(len(steps)):
                if si + LA < len(steps):
                    qk_ex(si + LA)
                pv(si)
            em.barrier()

    def mix_na(l):
        last = (l == DEPTH - 1)
        with contextlib.ExitStack() as ph:
            NK = sb(ph, "nNK", [128, 2, T], BF16)
            NQ = sb(ph, "nNQ", [128, 2, T], BF16)
            V1 = sb(ph, "nV1", [128, 34, 4, 65], BF16)
            E = sb(ph, "nE", [128, 4, 21 * 128], BF16)
            PT = [sb(ph, f"nPT{i}", [128, 896], BF16) for i in range(4)]
            rden = [sb(ph, f"nrden{i}", [128, 512]) for i in range(2)]
            On = [sb(ph, f"nOn{i}", [64, 512]) for i in range(2)]
            osb = [sb(ph, f"nosb{i}", [64, 512], BF16) for i in range(2)]
            r_NK, r_NQ, r_V1, r_E, r_PT = Res(), Res(), Res(), Res(), [Res(), Res(), Res(), Res()]
            r_rden, r_On, r_osb = [Res(), Res()], [Res(), Res()], [Res(), Res()]
            em.dma(sp, NK[:], NKs, rd=[r_scr["NK"]], wr=[r_NK])
            em.dma(sp, NQ[:], NQs, rd=[r_scr["NQ"]], wr=[r_NQ])
            em.op(pool, lambda: G_.memset(V1[:, :, :, 64:65], 1.0), wr=[r_V1])
            for q4 in range(2):
                a, b_ = q4 * 17, q4 * 17 + 17
                for h_ in range(4):
                    em.dma(sp, V1[:, a:b_, h_, 0:64], NVs[a * 128:b_ * 128, h_ * 64:(h_ + 1) * 64].rearrange("(t p) d -> p t d", p=128),
                           rd=[r_scr["NV"]], wa=[r_V1])
            with contextlib.ExitStack() as ph2:
                stg = [sb(ph2, f"nstg{i}", [128, 21 * 128]) for i in range(2)]
                r_stg = [Res(), Res()]
                for h in range(4):
                    em.dma(sp, stg[h % 2][:], braw_in[l, :, h * 2688:(h + 1) * 2688], wr=[r_stg[h % 2]])
                    em.op(act, lambda: S_.activation(E[:, h, :], stg[h % 2][:], AF.Exp), rd=[r_stg[h % 2]], wr=[r_E])
                em.barrier()
            qts = []
            if not last:
                qts += [(0, [], 0, [0, 1]), (1, [], 0, [0, 1])]
            for rq in range(32):
                if 2 <= rq <= 29:
                    win, p0 = list(range(rq - 2, rq + 3)), 0
                elif rq < 2:
                    win, p0 = [0, 1, 2, 3], 5 + 4 * rq
                else:
                    win, p0 = [28, 29, 30, 31], 13 + 4 * (rq - 30)
                qts.append((2 + rq, [2 + w for w in win], p0, [0, 1]))
            steps = [(qn, h) for qn in range(len(qts)) for h in range(4)]

            def qk_ex(si):
                qn, h = steps[si]
                qt, wkeys, p0, ckeys = qts[qn]
                keys = wkeys + ckeys
                nk, nw = len(keys), len(wkeys)
                q0 = qt * 128
                c, g = h // 2, h % 2
                sbk = [0, 2, 6][si % 3]
                flat = ps[:, sbk:sbk + 2, :].rearrange("p b n -> p (b n)")
                for i, kt in enumerate(keys):
                    em.op(pe, lambda: T_.matmul(flat[:, i * 128:(i + 1) * 128], NK[64 * g:64 * g + 64, c, kt * 128:(kt + 1) * 128],
                                                NQ[64 * g:64 * g + 64, c, q0:q0 + 128], start=True, stop=True, tile_position=(64 * g, 0)),
                          rd=[r_NK, r_NQ], wr=[psr[sbk], psr[sbk + 1]], inc=(i == nk - 1))
                pt, rpt = PT[si % 4], r_PT[si % 4]
                em.op(act, lambda: S_.activation(pt[:, 0:nk * 128], flat[:, 0:nk * 128], AF.Exp, scale=0.125),
                      rd=[psr[sbk], psr[sbk + 1]], wr=[rpt])
                if nw:
                    em.op(dve, lambda: V_.tensor_tensor(pt[:, 0:nw * 128], pt[:, 0:nw * 128], E[:, h, p0 * 128:(p0 + nw) * 128], ALU.mult),
                          rd=[r_E], wr=[rpt])

            def pv(si):
                qn, h = steps[si]
                qt, wkeys, p0, ckeys = qts[qn]
                keys = wkeys + ckeys
                nk = len(keys)
                q0 = qt * 128
                ob_ = 4 + (qn % 2)
                pt, rpt = PT[si % 4], r_PT[si % 4]
                for i, kt in enumerate(keys):
                    em.op(pe, lambda: T_.matmul(ps[0:65, ob_, h * 128:(h + 1) * 128], V1[:, kt, h, :], pt[:, i * 128:(i + 1) * 128],
                                                start=(i == 0), stop=(i == nk - 1)),
                          rd=[r_V1, rpt], wr=[psr[ob_]], inc=(i == nk - 1))
                if h == 3:
                    ob, rob = osb[qn % 2], r_osb[qn % 2]
                    rd_, rrd = rden[qn % 2], r_rden[qn % 2]
                    on_, ron = On[qn % 2], r_On[qn % 2]
                    em.op(dve, lambda: V_.reciprocal(rd_[64:65, :], ps[64:65, ob_, :]), rd=[psr[ob_]], wr=[rrd])
                    em.op(act, lambda: S_.copy(on_[:, :], ps[0:64, ob_, :]), rd=[psr[ob_]], wr=[ron])
                    em.op(pe, lambda: T_.matmul(ps[0:64, ob_, :], C("ones")[64:65, 0:64], rd_[64:65, :], start=True, stop=True, tile_position=(64, 0)),
                          rd=[rrd], wr=[psr[ob_]])
                    em.op(dve, lambda: V_.tensor_tensor(ob[:, :], on_[:, :], ps[0:64, ob_, :], ALU.mult), rd=[ron, psr[ob_]], wr=[rob])
                    for hh in range(4):
                        c, g = hh // 2, hh % 2
                        em.dma(sp, MIXT[64 * g:64 * g + 64, 2 + c, q0:q0 + 128], ob[:, hh * 128:(hh + 1) * 128], rd=[rob], wa=[r_MIXT])

            LA = 2
            for si in range(min(LA, len(steps))):
                qk_ex(si)
            for si in range(len(steps)):
                if si + LA < len(steps):
                    qk_ex(si + LA)
                pv(si)
            em.barrier()

    def mix_fourier(l):
        with contextlib.ExitStack() as ph:
            Vst = sb(ph, "fVst", [128, 64, 256], BF16)
            Bst = sb(ph, "fBst", [128, 64, 256], BF16)
            Asb = sb(ph, "fAsb", [128, 64, 256], BF16)
            gtb = sb(ph, "fgtb", [128, 64, 64], BF16)
            Yt = sb(ph, "fYt", [128, 2, SEQ], BF16)
            r_Vq = [Res() for _ in range(4)]
            r_Aq = [Res() for _ in range(4)]
            r_Bq = [Res() for _ in range(4)]
            r_g, r_Y = Res(), Res()
            em.dma(pool, gtb[:], gtab_in.rearrange("p (a b) -> p a b", a=64), wr=[r_g])
            vsx4 = VSX.rearrange("(a b) (r f) -> a b r f", b=64, r=2)
            for q in range(4):
                for ri in range(2):
                    em.dma(sp, Vst[ri * 64:(ri + 1) * 64, 16 * q:16 * q + 16, :], vsx4[:, 16 * q:16 * q + 16, ri, :], rd=[r_scr["VSX"]], wa=[r_Vq[q]])
            for q in range(4):
                for cg in range(8 * q, 8 * q + 8):
                    pb = cg % 2
                    em.op(pe, lambda: T_.matmul(ps[:, pb, :], m1_b, Vst[:, 2 * cg:2 * cg + 2, :].rearrange("p a f -> p (a f)"), start=True, stop=True),
                          rd=[r_Vq[q]], wr=[psr[pb]])
                    dst = Asb[:, 2 * cg:2 * cg + 2, :].rearrange("p a f -> p (a f)")
                    if cg % 2 == 0:
                        em.op(act, lambda: S_.copy(dst, ps[:, pb, :]), rd=[psr[pb]], wr=[r_Aq[q]])
                    else:
                        em.op(dve, lambda: V_.tensor_copy(dst, ps[:, pb, :]), rd=[psr[pb]], wr=[r_Aq[q]])
                em.dma(sp, AS_[:, 16 * q:16 * q + 16, :], Asb[:, 16 * q:16 * q + 16, :], rd=[r_Aq[q]], wa=[r_scr["AS"]])
            for kq in range(4):
                for ri in range(2):
                    em.dma(sp, Bst[ri * 64:(ri + 1) * 64, 16 * kq:16 * kq + 16, :],
                           AS_[ri * 64 + 16 * kq:ri * 64 + 16 * kq + 16, :, :].rearrange("k n f -> n k f"),
                           rd=[r_scr["AS"]], wa=[r_Bq[kq]])
            n_ev = 0
            for kq in range(4):
                for c in range(2):
                    ytv = Yt[:, c, :].rearrange("p (k2 k1) -> p k2 k1", k1=64)
                    for kg in (2 * kq, 2 * kq + 1):
                        pb = 2 + (n_ev % 2)
                        for kk in range(8):
                            k1 = kg * 8 + kk
                            em.op(pe, lambda: T_.matmul(ps[:, pb, kk * 64:(kk + 1) * 64], Bst[:, k1, c * 128:(c + 1) * 128], gtb[:, k1, :], start=True, stop=True),
                                  rd=[r_Bq[kq], r_g], wr=[psr[pb]], inc=(kk == 7))
                        src = ps[:, pb, :].rearrange("p (kk k2) -> p k2 kk", kk=8)
                        if n_ev % 2 == 0:
                            em.op(act, lambda: S_.copy(ytv[:, :, kg * 8:(kg + 1) * 8], src), rd=[psr[pb]], wr=[r_Y])
                        else:
                            em.op(dve, lambda: V_.tensor_copy(ytv[:, :, kg * 8:(kg + 1) * 8], src), rd=[psr[pb]], wr=[r_Y])
                        n_ev += 1
            em.dma(sp, MIXT[:, 0:2, LC:T], Yt[:], rd=[r_Y], wa=[r_MIXT])
            em.barrier()

    def mix_fourier_v1(l):
        with contextlib.ExitStack() as ph:
            Vst = sb(ph, "fVst", [128, 64, 256], BF16)
            Asb = sb(ph, "fAsb", [128, 64, 256], BF16)
            gtb = sb(ph, "fgtb", [128, 64, 64], BF16)
            Yt = sb(ph, "fYt", [128, 2, SEQ], BF16)
            r_V, r_A, r_g, r_Y = Res(), Res(), Res(), Res()
            em.dma(pool, gtb[:], gtab_in.rearrange("p (a b) -> p a b", a=64), wr=[r_g])
            vsx4 = VSX.rearrange("(a b) (r f) -> a b r f", b=64, r=2)
            for ri in range(2):
                em.dma(sp, Vst[ri * 64:(ri + 1) * 64, :, :], vsx4[:, :, ri, :], rd=[r_scr["VSX"]], wa=[r_V])
            for cg in range(32):
                pb = cg % 2
                em.op(pe, lambda: T_.matmul(ps[:, pb, :], m1_b, Vst[:, 2 * cg:2 * cg + 2, :].rearrange("p a f -> p (a f)"), start=True, stop=True),
                      rd=[r_V], wr=[psr[pb]])
                dst = Asb[:, 2 * cg:2 * cg + 2, :].rearrange("p a f -> p (a f)")
                if cg % 2 == 0:
                    em.op(act, lambda: S_.copy(dst, ps[:, pb, :]), rd=[psr[pb]], wr=[r_A])
                else:
                    em.op(dve, lambda: V_.tensor_copy(dst, ps[:, pb, :]), rd=[psr[pb]], wr=[r_A])
            em.dma(sp, AS_, Asb[:], rd=[r_A], wr=[r_scr["AS"]])
            for ri in range(2):
                em.dma(sp, Vst[ri * 64:(ri + 1) * 64, :, :], AS_[ri * 64:(ri + 1) * 64, :, :].rearrange("k n f -> n k f"),
                       rd=[r_scr["AS"]], wr=([r_V] if ri == 0 else []), wa=([r_V] if ri == 1 else []))
            for c in range(2):
                ytv = Yt[:, c, :].rearrange("p (k2 k1) -> p k2 k1", k1=64)
                for kg in range(8):
                    pb = 2 + (kg % 2)
                    for kk in range(8):
                        k1 = kg * 8 + kk
                        em.op(pe, lambda: T_.matmul(ps[:, pb, kk * 64:(kk + 1) * 64], Vst[:, k1, c * 128:(c + 1) * 128], gtb[:, k1, :], start=True, stop=True),
                              rd=[r_V, r_g], wr=[psr[pb]], inc=(kk == 7))
                    src = ps[:, pb, :].rearrange("p (kk k2) -> p k2 kk", kk=8)
                    if kg % 2 == 0:
                        em.op(act, lambda: S_.copy(ytv[:, :, kg * 8:(kg + 1) * 8], src), rd=[psr[pb]], wr=[r_Y])
                    else:
                        em.op(dve, lambda: V_.tensor_copy(ytv[:, :, kg * 8:(kg + 1) * 8], src), rd=[psr[pb]], wr=[r_Y])
            em.dma(sp, MIXT[:, 0:2, LC:T], Yt[:], rd=[r_Y], wa=[r_MIXT])
            em.barrier()

    def mix_gla(l):
        last = (l == DEPTH - 1)
        NCH = T // 64
        qscale = 32.0 ** -0.5
        with contextlib.ExitStack() as phA:
            gcst = sb(phA, "lcst", [128, 1024])
            r_gcst = Res()
            em.dma(sp, gcst[:], cst_in[:, 896:1920], wr=[r_gcst])
            bmask = gcst[:, 0:256]
            trimask = gcst[0:64, 256:768]
            tri4 = gcst[0:64, 768:1024].rearrange("p (a i) -> p a i", a=4)
            qf = sb(phA, "lqf", [128, T], BF16)
            kf = sb(phA, "lkf", [128, T], BF16)
            qb = sb(phA, "lqb", [128, T], BF16)
            kb = sb(phA, "lkb", [128, T], BF16)
            Vv = sb(phA, "lVv", [64, NCH, 256], BF16)
            Sall = [sb(phA, f"lS{i}", [128, NCH, 256], BF16) for i in range(2)]
            Dcol = sb(phA, "lD", [128, 2, NCH])
            r_qk, r_Vv, r_S, r_D = Res(), Res(), [Res(), Res()], Res()
            for q4 in range(4):
                a, b_ = q4 * 17, q4 * 17 + 17
                em.dma(sp, Vv[:, a:b_, :], GVs[a * 64:b_ * 64, :].rearrange("(c p) f -> p c f", p=64), rd=[r_scr["GV"]], wa=[r_Vv])
            with contextlib.ExitStack() as phB:
                Kh = [sb(phB, f"lKh{i}", [64, NCH, 128], BF16) for i in range(2)]
                r_Kh = Res()
                with contextlib.ExitStack() as phC:
                    Wg = sb(phC, "lWg", [33, 256])
                    GGt = [sb(phC, f"lGG{i}", [33, 256]) for i in range(2)]
                    gq = [sb(phC, f"lgq{i}", [128, 256]) for i in range(2)]
                    gk = [sb(phC, f"lgk{i}", [128, 256]) for i in range(2)]
                    Ktg = [sb(phC, f"lKt{i}", [64, 4, 128], BF16) for i in range(2)]
                    SP = sb(phC, "lSP", [64, 4, 256])
                    SPh = sb(phC, "lSPh", [64, 4, 256], BF16)
                    SPl = sb(phC, "lSPl", [64, 4, 256], BF16)
                    tri4b = sb(phC, "ltri4b", [64, 4, 64], BF16)
                    r_SPh, r_SPl = Res(), Res()
                    EQ = sb(phC, "lEQ", [128, 2, 256])
                    EK = sb(phC, "lEK", [128, 2, 256])
                    EE = sb(phC, "lEE", [64, 2, 512])
                    one_c = sb(phC, "lone", [128, 1])
                    r_Wg, r_GG, r_gq, r_gk, r_Kt = Res(), [Res(), Res()], [Res(), Res()], [Res(), Res()], [Res(), Res()]
                    r_e1, r_SP, r_EQ, r_EK, r_EE, r_one = Res(), Res(), Res(), Res(), Res(), Res()
                    em.dma(sp, Wg[:], wg_in[l], wr=[r_Wg])
                    em.op(dve, lambda: V_.tensor_copy(tri4b[:], tri4), rd=[r_gcst], wr=[r_gcst])
                    em.op(dve, lambda: V_.memset(one_c[:], 1.0), wr=[r_one])
                    for i in range(2):
                        em.op(pool, lambda: G_.memset(GGt[i][:], 1.0), wr=[r_GG[i]])
                    for gi in range(NCH // 4):
                        b = gi % 2
                        t0 = gi * 256
                        em.dma(sp, GGt[b][0:32, :], GGs[:, t0:t0 + 256], rd=[r_scr["GG"]], wr=[r_GG[b]])
                        em.dma(sp, gq[b][:], GQs[:, t0:t0 + 256], rd=[r_scr["GQ"]], wr=[r_gq[b]])
                        em.dma(sp, gk[b][:], GKs[:, t0:t0 + 256], rd=[r_scr["GK"]], wr=[r_gk[b]])
                        em.dma(sp, Ktg[b][:], GKts[t0:t0 + 256, :].rearrange("(c p) f -> p c f", p=64), rd=[r_scr["GKt"]], wr=[r_Kt[b]])
                        for ci in range(4):
                            em.op(pe, lambda: T_.matmul(ps[0:64, ci // 2, (ci % 2) * 256:(ci % 2 + 1) * 256], GGt[b][0:33, ci * 64:(ci + 1) * 64], Wg[0:33, :],
                                                        start=True, stop=True),
                                  rd=[r_GG[b], r_Wg], wr=[psr[ci // 2]], inc=(ci % 2 == 1))
                        em.op(act, lambda: S_.activation(SP[:].rearrange("p (b c) f -> p b (c f)", b=2), ps[0:64, 0:2, :], AF.Exp, scale=-1.0),
                              rd=[psr[0], psr[1]], wr=[r_SP])
                        em.op(act, lambda: S_.activation(SP[:].rearrange("p c f -> p (c f)"), SP[:].rearrange("p c f -> p (c f)"), AF.Ln, bias=one_c[0:64, 0:1], scale=1.0),
                              rd=[r_one], wr=[r_SP])
                        em.op(pool, lambda: G_.tensor_copy(SPh[:], SP[:]), rd=[r_SP], wr=[r_SPh])
                        em.op(dve, lambda: V_.tensor_tensor(SPl[:], SP[:], SPh[:], ALU.subtract), rd=[r_SP, r_SPh], wr=[r_SPl])
                        for ci in range(4):
                            for hl, (SPx, rSPx) in enumerate(((SPh, r_SPh), (SPl, r_SPl))):
                                em.op(pe, lambda: T_.matmul(ps[:, 2, ci * 64:(ci + 1) * 64], SPx[:, ci, 0:128], tri4b[:, 0, :], start=(hl == 0), stop=(hl == 1)),
                                      rd=[rSPx, r_gcst], wr=[psr[2]], inc=False)
                            for hl, (SPx, rSPx) in enumerate(((SPh, r_SPh), (SPl, r_SPl))):
                                em.op(pe, lambda: T_.matmul(ps[:, 2, 256 + ci * 64:256 + (ci + 1) * 64], SPx[:, ci, 128:256], tri4b[:, 1, :], start=(hl == 0), stop=(hl == 1)),
                                      rd=[rSPx, r_gcst], wr=[psr[2]], inc=(ci == 3 and hl == 1))
                        for ci in range(4):
                            for hl, (SPx, rSPx) in enumerate(((SPh, r_SPh), (SPl, r_SPl))):
                                em.op(pe, lambda: T_.matmul(ps[0:64, 3, ci * 128:(ci + 1) * 128], tri4b[:, 2, :], SPx[:, ci, 0:128], start=(hl == 0), stop=(hl == 1)),
                                      rd=[rSPx, r_gcst], wr=[psr[3]], inc=False)
                            for hl, (SPx, rSPx) in enumerate(((SPh, r_SPh), (SPl, r_SPl))):
                                em.op(pe, lambda: T_.matmul(ps[0:64, 4, ci * 128:(ci + 1) * 128], tri4b[:, 3, :], SPx[:, ci, 128:256], start=(hl == 0), stop=(hl == 1)),
                                      rd=[rSPx, r_gcst], wr=[psr[4]], inc=(ci == 3 and hl == 1))
                        flatEQ = EQ[:].rearrange("p d n -> p (d n)")
                        flatEK = EK[:].rearrange("p d n -> p (d n)")
                        em.op(act, lambda: S_.activation(flatEQ, ps[:, 2, :], AF.Exp), rd=[psr[2]], wr=[r_EQ])
                        em.op(act, lambda: S_.activation(flatEK, ps[:, 2, :], AF.Exp, scale=-1.0), rd=[psr[2]], wr=[r_EK])
                        em.op(act, lambda: S_.activation(EE[:, 0, :], ps[0:64, 3, :], AF.Exp), rd=[psr[3]], wr=[r_EE])
                        em.op(act, lambda: S_.activation(EE[:, 1, :], ps[0:64, 4, :], AF.Exp), rd=[psr[4]], wr=[r_EE])
                        ts_ = slice(t0, t0 + 256)
                        em.op(dve, lambda: V_.scalar_tensor_tensor(qf[:, ts_], gq[b][:], qscale, EQ[:, 0, :], ALU.mult, ALU.mult), rd=[r_gq[b], r_EQ], wr=[r_qk])
                        em.op(dve, lambda: V_.scalar_tensor_tensor(qb[:, ts_], gq[b][:], qscale, EQ[:, 1, :], ALU.mult, ALU.mult), rd=[r_gq[b], r_EQ], wr=[r_qk])
                        em.op(dve, lambda: V_.tensor_tensor(kf[:, ts_], gk[b][:], EK[:, 0, :], ALU.mult), rd=[r_gk[b], r_EK], wr=[r_qk])
                        em.op(dve, lambda: V_.tensor_tensor(kb[:, ts_], gk[b][:], EK[:, 1, :], ALU.mult), rd=[r_gk[b], r_EK], wr=[r_qk])
                        em.op(pool, lambda: G_.tensor_copy(Dcol[:, 0, gi * 4:(gi + 1) * 4], EQ[:, 0, :].rearrange("p (c i) -> p c i", i=64)[:, :, 63]),
                              rd=[r_EQ], wr=[r_D])
                        em.op(pool, lambda: G_.tensor_copy(Dcol[:, 1, gi * 4:(gi + 1) * 4], EQ[:, 1, :].rearrange("p (c i) -> p c i", i=64)[:, :, 0]),
                              rd=[r_EQ], wr=[r_D])
                        em.op(pool, lambda: G_.tensor_tensor(Kh[0][:, gi * 4:(gi + 1) * 4, :], Ktg[b][:], EE[:, 0, :].rearrange("p (c f) -> p c f", c=4), ALU.mult),
                              rd=[r_Kt[b], r_EE], wr=[r_Kh])
                        em.op(pool, lambda: G_.tensor_tensor(Kh[1][:, gi * 4:(gi + 1) * 4, :], Ktg[b][:], EE[:, 1, :].rearrange("p (c f) -> p c f", c=4), ALU.mult),
                              rd=[r_Kt[b], r_EE], wr=[r_Kh])
                    em.barrier()
                if GLA_STOP == "G":
                    return
                with contextlib.ExitStack() as phC:
                    Sc = [[sb(phC, f"lSc{d}{i}", [128, 256]) for i in range(2)] for d in range(2)]
                    tm = [sb(phC, f"ltm{d}", [128, 256]) for d in range(2)]
                    r_Sc = [[Res(), Res()], [Res(), Res()]]
                    r_tm = [Res(), Res()]
                    order = [list(range(NCH)), [3, 2, 1, 0] + list(range(NCH - 1, 3, -1))]
                    for d in range(2):
                        em.op(dve, lambda: V_.memset(Sc[d][0][:], 0.0), wr=[r_Sc[d][0]])
                    for step in range(NCH):
                        for d in range(2):
                            c = order[d][step]
                            pb = 2 * d + (step % 2)
                            cur, nxt = step % 2, (step + 1) % 2
                            em.op(pe, lambda: T_.matmul(ps[:, pb, 0:256], Kh[d][:, c, :], Vv[:, c, :], start=True, stop=True),
                                  rd=[r_Kh, r_Vv], wr=[psr[pb]])
                            em.op(act, lambda: S_.copy(Sall[d][:, c, :], Sc[d][cur][:]), rd=[r_Sc[d][cur]], wr=[r_S[d]])
                            em.op(dve, lambda: V_.tensor_tensor(tm[d][:], ps[:, pb, 0:256], bmask, ALU.mult), rd=[psr[pb]], wr=[r_tm[d]])
                            em.op(dve, lambda: V_.scalar_tensor_tensor(Sc[d][nxt][:], Sc[d][cur][:], Dcol[:, d, c:c + 1], tm[d][:], ALU.mult, ALU.add),
                                  rd=[r_Sc[d][cur], r_tm[d], r_D], wr=[r_Sc[d][nxt]])
                    em.barrier()
            if GLA_STOP == "B1":
                return
            with contextlib.ExitStack() as phC:
                STm = [sb(phC, f"lST{i}", [64, 512], BF16) for i in range(2)]
                sq2 = [sb(phC, f"lsq{i}", [64, 512]) for i in range(2)]
                ss2 = [sb(phC, f"lss{i}", [64, 8]) for i in range(2)]
                tt2 = [sb(phC, f"ltt{i}", [64, 512]) for i in range(2)]
                r_sq2, r_ss2, r_tt2 = [Res(), Res()], [Res(), Res()], [Res(), Res()]
                glan = sb(phC, "lgl", [64, 2, 256])
                Rg = [sb(phC, f"lRg{i}", [64, 2, 256], BF16) for i in range(2)]
                cx = [sb(phC, f"lcx{i}", [64, 2, 256], BF16) for i in range(2)]
                cxT = [sb(phC, f"lcxT{i}", [128, 2, 128], BF16) for i in range(2)]
                identb = CB("ident", 64)[:, 0:64]
                r_ST, r_sq, r_ss, r_tt, r_gl, r_Rg, r_cx, r_cxT = [Res(), Res()], Res(), Res(), Res(), Res(), [Res(), Res()], [Res(), Res()], [Res(), Res()]
                for a in range(2):
                    em.dma(sp, glan[:, a, :], glan_in[l].partition_broadcast(64), wa=[r_gl])
                c0 = 4 if last else 0
                qT = [qf, qb]
                kT = [kf, kb]
                chunks = list(range(c0, NCH))

                def st_mask(c):
                    ci = c % 2
                    cs = slice(c * 64, (c + 1) * 64)
                    for d in range(2):
                        for h in range(4):
                            em.op(pe, lambda: T_.matmul(ps[0:64, h, d * 64:(d + 1) * 64], kT[d][32 * h:32 * h + 32, cs],
                                                        qT[d][32 * h:32 * h + 32, cs], start=True, stop=True, tile_position=(32 * h, 0)),
                                  rd=[r_qk], wr=[psr[h]], inc=(d == 1 and h == 3))
                    em.op(dve, lambda: V_.tensor_tensor(STm[ci][:].rearrange("p (h d i) -> p h d i", h=4, d=2),
                                                        ps[0:64, 0:4, 0:128].rearrange("p h (d i) -> p h d i", d=2),
                                                        trimask.rearrange("p (d h i) -> p h d i", d=2, h=4), ALU.mult),
                          rd=[psr[0], psr[1], psr[2], psr[3]], wr=[r_ST[ci]])

                def o_mm(c):
                    ci = c % 2
                    p = c // 2
                    pi = p - c0 // 2
                    ob = 4 + (pi % 2)
                    cs = slice(c * 64, (c + 1) * 64)
                    if ci == 0:
                        em.dma(sp, Rg[pi % 2][:], GRs[p * 128:(p + 1) * 128, :].rearrange("(c q) f -> q c f", q=64), rd=[r_scr["GR"]], wr=[r_Rg[pi % 2]])
                    oc = slice(ci * 256, (ci + 1) * 256)
                    em.op(pe, lambda: T_.matmul(ps[0:64, ob, oc], qf[:, cs], Sall[0][:, c, :], start=True, stop=False),
                          rd=[r_qk, r_S[0]], wr=[psr[ob]], inc=False)
                    em.op(pe, lambda: T_.matmul(ps[0:64, ob, oc], qb[:, cs], Sall[1][:, c, :], start=False, stop=False),
                          rd=[r_qk, r_S[1]], wr=[psr[ob]], inc=False)
                    for d in range(2):
                        for h in range(4):
                            fin = (d == 1 and h == 3)
                            em.op(pe, lambda: T_.matmul(ps[0:64, ob, ci * 256 + h * 64:ci * 256 + (h + 1) * 64],
                                                        STm[ci][:, h * 128 + d * 64:h * 128 + (d + 1) * 64], Vv[:, c, h * 64:(h + 1) * 64],
                                                        start=False, stop=fin),
                                  rd=[r_ST[ci], r_Vv], wr=[psr[ob]], inc=fin)

                def post(p):
                    pi = p - c0 // 2
                    ob = 4 + (pi % 2)
                    sq, ss, tt = sq2[pi % 2], ss2[pi % 2], tt2[pi % 2]
                    r_sq, r_ss, r_tt = r_sq2[pi % 2], r_ss2[pi % 2], r_tt2[pi % 2]
                    em.op(act, lambda: S_.activation(sq[:], ps[0:64, ob, :], AF.Square), rd=[psr[ob]], wr=[r_sq])
                    em.op(dve, lambda: V_.tensor_reduce(ss[:], sq[:].rearrange("p (g e) -> p g e", e=64), AX.X, ALU.add), rd=[r_sq], wr=[r_ss])
                    em.op(act, lambda: S_.activation(ss[:], ss[:], AF.Ln, bias=eps_c[0:64, 0:1], scale=1.0 / 64), wr=[r_ss])
                    em.op(act, lambda: S_.activation(ss[:], ss[:], AF.Exp, scale=-0.5), wr=[r_ss])
                    em.op(dve, lambda: V_.tensor_tensor(tt[:].rearrange("p (g e) -> p g e", e=64), ps[0:64, ob, :].rearrange("p (g e) -> p g e", e=64),
                                                        ss[:].unsqueeze(2).to_broadcast([64, 8, 64]), ALU.mult),
                          rd=[psr[ob], r_ss], wr=[r_tt])
                    em.op(pool, lambda: G_.tensor_tensor(tt[:], tt[:], glan[:].rearrange("p a f -> p (a f)"), ALU.mult), rd=[r_gl], wr=[r_tt])
                    cxp, rcx = cx[pi % 2], r_cx[pi % 2]
                    em.op(pool, lambda: G_.tensor_tensor(cxp[:].rearrange("p a f -> p (a f)"), tt[:], Rg[pi % 2][:].rearrange("p a f -> p (a f)"), ALU.mult),
                          rd=[r_tt, r_Rg[pi % 2]], wr=[rcx])

                def transp(p):
                    pi = p - c0 // 2
                    cxp, rcx = cx[pi % 2], r_cx[pi % 2]
                    tbk = 6 + (pi % 2)
                    pst = ps[:, tbk, :].bitcast(BF16)
                    for ci in range(2):
                        for fc in range(2):
                            em.op(pe, lambda: T_.transpose(pst[:, fc * 128 + ci * 64:fc * 128 + (ci + 1) * 64], cxp[:, ci, fc * 128:(fc + 1) * 128], identb),
                                  rd=[rcx], wr=[psr[tbk]], inc=(ci == 1 and fc == 1))
                    xo = cxT[pi % 2]
                    em.op(act, lambda: S_.copy(xo[:].rearrange("p a t -> p (a t)"), pst[:, 0:256]), rd=[psr[tbk]], wr=[r_cxT[pi % 2]])
                    em.dma(sp, MIXT[:, 4:6, p * 128:(p + 1) * 128], xo[:], rd=[r_cxT[pi % 2]], wa=[r_MIXT])

                todo_tr = []
                st_mask(chunks[0])
                for i, c in enumerate(chunks):
                    if i + 1 < len(chunks):
                        st_mask(chunks[i + 1])
                    o_mm(c)
                    if todo_tr and c % 2 == 0:
                        transp(todo_tr.pop(0))
                    if c % 2 == 1:
                        post(c // 2)
                        todo_tr.append(c // 2)
                while todo_tr:
                    transp(todo_tr.pop(0))
                em.barrier()

    if stop == "INIT":
        em.barrier()
        return
    phase_mod()
    if stop == "MOD":
        return
    phase_T()
    if stop == "T":
        return
    for l in range(DEPTH):
        last = (l == DEPTH - 1)
        phase_ffn(l, 1, False)
        if stop == f"F1_{l}":
            return
        phase_P(l)
        if stop == f"P_{l}":
            return
        r_MIXT.w, r_MIXT.rd = [], []
        sel = stop.split(":")[1].split(",") if (stop and ":" in stop and stop.startswith(f"MIX_{l}")) else ["gqa", "na", "fourier", "gla"]
        if "gla" in sel:
            mix_gla(l)
        if "fourier" in sel:
            (mix_fourier if OPT_FQ else mix_fourier_v1)(l)
        if "na" in sel:
            mix_na(l)
        with contextlib.ExitStack() as wst:
            pre2 = ffn_weights(wst, l, 2) if (OPT_PREFETCH and not (stop and stop.startswith(f"MIX_{l}"))) else None
            if "gqa" in sel:
                mix_gqa(l)
            if stop and stop.startswith(f"MIX_{l}"):
                return
            phase_O(l)
            if stop == f"O_{l}":
                return
            phase_ffn(l, 2, last, pre=pre2)


_CACHE = {}


def _prep_shared(inputs):
    f32 = lambda a: np.ascontiguousarray(np.asarray(a, dtype=np.float32))
    sh = dict(make_consts())
    sh["w_mod"] = f32(inputs["w_mod"])
    sh["bmod_c"] = f32(np.asarray(inputs["b_mod"]).reshape(DEPTH, 72, 128).transpose(0, 2, 1))
    for f_, nm in ((1, "ffn1"), (2, "ffn2")):
        sh[f"f{f_}w1"] = f32(inputs[f"{nm}_w1"])
        sh[f"f{f_}w3"] = f32(inputs[f"{nm}_w3"])
        sh[f"f{f_}w2"] = f32(inputs[f"{nm}_w2"])
    sh["w_in_r"] = f32(np.asarray(inputs["w_in"])[:, :, _colperm()])
    sh["w_out"] = f32(inputs["w_out"])
    t2 = lambda a: np.tile(np.asarray(a), (1, 2))
    sh["gcol"] = f32(np.stack([t2(inputs["na_q_norm"]), t2(inputs["na_k_norm"]), t2(inputs["gqa_q_norm"]), t2(inputs["gqa_k_norm"])], axis=2))
    sh["glan"] = f32(np.tile(np.asarray(inputs["gla_norm"]), (1, 4)))
    wg = np.zeros((DEPTH, 33, 256), np.float32)
    wg[:, 0:16, 0:128] = np.asarray(inputs["gla_w_gate_f"])
    wg[:, 16:32, 128:256] = np.asarray(inputs["gla_w_gate_b"])
    wg[:, 32, 0:128] = np.asarray(inputs["gla_b_gate_f"])
    wg[:, 32, 128:256] = np.asarray(inputs["gla_b_gate_b"])
    sh["wg"] = wg
    sh["braw"] = make_na_bias(np.asarray(inputs["na_rpb"], dtype=np.float32)).reshape(DEPTH, 128, 4 * 21 * 128)
    return sh


def _in_maps(inputs, n_cores=8):
    sh = _prep_shared(inputs)
    x = np.asarray(inputs["x"], dtype=np.float32)
    ctx = np.asarray(inputs["ctx"], dtype=np.float32)
    c = np.asarray(inputs["c"], dtype=np.float32)
    c_ctx = np.asarray(inputs["c_ctx"], dtype=np.float32)
    maps = []
    for b in range(n_cores):
        m = dict(sh)
        m["x"] = np.ascontiguousarray(x[b])
        m["ctx"] = np.ascontiguousarray(ctx[b])
        cc = np.stack([c[b].reshape(8, 128).T, c_ctx.reshape(8, 128).T], axis=2)
        m["cc"] = np.ascontiguousarray(cc.astype(np.float32))
        maps.append(m)
    return maps


def kernel(**inputs):
    make_consts()
    if "nc" not in _CACHE:
        _CACHE["nc"] = build()
    nc = _CACHE["nc"]
    maps = _in_maps(inputs)
    res = run_bass_kernel_spmd(nc, maps, core_ids=list(range(8)))
    return np.stack([np.asarray(r["out"], dtype=np.float32) for r in res.results], axis=0)
```
